# Optimizing a Trainium2 kernel written in Bass

```python
import math
import jax, jax.numpy as jnp
from jax import lax
import numpy as np

D_MODEL = 1024
BATCH = 8
SEQ = 2048
DEPTH = 1

HEAD_DIM = 64
A_Q_HEADS = 8
A_KV_HEADS = 2
A_GROUPS = A_Q_HEADS // A_KV_HEADS
A_WIDTH = A_Q_HEADS * HEAD_DIM
B_HEADS = 8
B_WIDTH = B_HEADS * HEAD_DIM
IDX_HEADS = 8
IDX_DIM = 32
WINDOW = 128
BLOCK = 128
TOPK_MAX = 256
N_BUCKETS = 32
MAX_DISTANCE = 128
RMS_EPS = 1e-6
SPLIT_SIZES = (
    A_WIDTH,
    A_KV_HEADS * HEAD_DIM,
    A_KV_HEADS * HEAD_DIM,
    A_WIDTH,
    B_WIDTH,
    B_WIDTH,
    B_WIDTH,
    B_WIDTH,
    IDX_HEADS * IDX_DIM,
    IDX_DIM,
    IDX_HEADS,
    2 * D_MODEL,
)
IN_WIDTH = sum(SPLIT_SIZES)

kernel_name = "hybrid_swa_sink_dsa_gated_block"


def rms_norm(x, g):
    xf = x.astype(jnp.float32)
    xf = xf * lax.rsqrt(jnp.mean(xf * xf, axis=-1, keepdims=True) + RMS_EPS)
    return (xf * g.astype(jnp.float32)).astype(x.dtype)


def t5_bucket(n):
    n = jnp.maximum(n, 0)
    max_exact = N_BUCKETS // 2
    nf = jnp.maximum(n, 1).astype(jnp.float32)
    large = max_exact + (jnp.log(nf / max_exact) / math.log(MAX_DISTANCE / max_exact)
                         * (N_BUCKETS - max_exact)).astype(jnp.int32)
    large = jnp.minimum(large, N_BUCKETS - 1)
    return jnp.where(n < max_exact, n, large)


def swa_sink_attention(q, k, v, sinks, table_a):
    B, S = q.shape[0], q.shape[1]
    nb = S // BLOCK
    qb = q.reshape(B, nb, BLOCK, A_KV_HEADS, A_GROUPS, HEAD_DIM)

    def band(t):
        tp = jnp.pad(t, ((0, 0), (BLOCK, 0), (0, 0), (0, 0)))
        prev = tp[:, :S].reshape(B, nb, BLOCK, A_KV_HEADS, HEAD_DIM)
        cur = t.reshape(B, nb, BLOCK, A_KV_HEADS, HEAD_DIM)
        return jnp.concatenate([prev, cur], axis=2)

    kb, vb = band(k), band(v)
    scores = jnp.einsum('bnqhgd,bnkhd->bnhgqk', qb, kb,
                        preferred_element_type=jnp.float32) * (HEAD_DIM ** -0.5)
    t_loc = jnp.arange(BLOCK)[:, None]
    s_loc = jnp.arange(2 * BLOCK)[None, :]
    dist = t_loc + BLOCK - s_loc
    bias = table_a.astype(jnp.float32)[t5_bucket(dist)]
    bias = bias.transpose(2, 0, 1).reshape(A_KV_HEADS, A_GROUPS, BLOCK, 2 * BLOCK)
    blk = jnp.arange(nb)[:, None, None]
    valid = (dist >= 0) & (dist < WINDOW) & (blk * BLOCK - BLOCK + s_loc >= 0)
    scores = jnp.where(valid[None, :, None, None], scores + bias, -jnp.inf)
    sink = jnp.broadcast_to(
        sinks.astype(jnp.float32).reshape(A_KV_HEADS, A_GROUPS)[None, None, :, :, None, None],
        scores.shape[:-1] + (1,))
    probs = jax.nn.softmax(jnp.concatenate([scores, sink], axis=-1), axis=-1)[..., :-1]
    out = jnp.einsum('bnhgqk,bnkhd->bnqhgd', probs.astype(v.dtype), vb)
    return out.reshape(B, S, A_Q_HEADS * HEAD_DIM)


def dsa_attention(q, k, v, q_idx, k_idx, w_idx, table_b):
    B, S = q.shape[0], q.shape[1]
    nb = S // BLOCK
    top_k = min(TOPK_MAX, S // 4)
    k_flat = k.reshape(B, S, B_HEADS * HEAD_DIM)
    v_flat = v.reshape(B, S, B_HEADS * HEAD_DIM)
    key_pos = jnp.arange(S)
    gather = jax.vmap(lambda table, ix: table[ix])

    def one_block(args):
        i, qblk, qiblk, wblk = args
        t = i * BLOCK + jnp.arange(BLOCK)
        dots = jnp.einsum('bqhe,bse->bqhs', qiblk, k_idx,
                          preferred_element_type=jnp.float32) * (IDX_DIM ** -0.5)
        w = wblk.astype(jnp.float32) * (IDX_HEADS ** -0.5)
        score_idx = jnp.einsum('bqh,bqhs->bqs', w, jax.nn.relu(dots))
        causal = key_pos[None, :] <= t[:, None]
        score_idx = jnp.where(causal[None], score_idx, -jnp.inf)
        _, idx = lax.top_k(score_idx, top_k)
        valid = idx <= t[None, :, None]
        kg = gather(k_flat, idx).reshape(B, BLOCK, top_k, B_HEADS, HEAD_DIM)
        vg = gather(v_flat, idx).reshape(B, BLOCK, top_k, B_HEADS, HEAD_DIM)
        sc = jnp.einsum('bqhd,bqkhd->bhqk', qblk, kg,
                        preferred_element_type=jnp.float32) * (HEAD_DIM ** -0.5)
        bias = table_b.astype(jnp.float32)[t5_bucket(t[None, :, None] - idx)]
        sc = jnp.where(valid[:, None], sc + bias.transpose(0, 3, 1, 2), -jnp.inf)
        p = jax.nn.softmax(sc, axis=-1)
        return jnp.einsum('bhqk,bqkhd->bqhd', p.astype(v.dtype), vg)

    def to_blocks(t):
        return jnp.moveaxis(t.reshape((B, nb, BLOCK) + t.shape[2:]), 1, 0)

    outs = lax.map(one_block, (jnp.arange(nb), to_blocks(q), to_blocks(q_idx), to_blocks(w_idx)))
    return jnp.moveaxis(outs, 0, 1).reshape(B, S, B_HEADS * HEAD_DIM)


def setup_inputs(seed: int = 0) -> dict:
    key = jax.random.key(seed)
    ks = jax.random.split(key, 12)
    f32 = jnp.float32
    return {
        "x": jax.random.normal(ks[0], (BATCH, SEQ, D_MODEL), f32),
        "norm_g": 1.0 + 0.05 * jax.random.normal(ks[1], (DEPTH, D_MODEL), f32),
        "w_in": jax.random.normal(ks[2], (DEPTH, D_MODEL, IN_WIDTH), f32) * D_MODEL ** -0.5,
        "qnorm_a": 1.0 + 0.05 * jax.random.normal(ks[3], (DEPTH, HEAD_DIM), f32),
        "knorm_a": 1.0 + 0.05 * jax.random.normal(ks[4], (DEPTH, HEAD_DIM), f32),
        "sinks_a": 0.5 * jax.random.normal(ks[5], (DEPTH, A_Q_HEADS), f32),
        "qnorm_b": 1.0 + 0.05 * jax.random.normal(ks[6], (DEPTH, HEAD_DIM), f32),
        "knorm_b": 1.0 + 0.05 * jax.random.normal(ks[7], (DEPTH, HEAD_DIM), f32),
        "rel_bias": 0.5 * jax.random.normal(ks[8], (N_BUCKETS, A_Q_HEADS + B_HEADS), f32),
        "w_proj_a": jax.random.normal(ks[9], (DEPTH, A_WIDTH, D_MODEL), f32) * A_WIDTH ** -0.5,
        "w_proj_b": jax.random.normal(ks[10], (DEPTH, B_WIDTH, D_MODEL), f32) * B_WIDTH ** -0.5,
        "w_out": jax.random.normal(ks[11], (DEPTH, D_MODEL, D_MODEL), f32) * D_MODEL ** -0.5,
    }


def reference(x, norm_g, w_in, qnorm_a, knorm_a, sinks_a, qnorm_b, knorm_b, rel_bias,
              w_proj_a, w_proj_b, w_out):
    B, S = x.shape[0], x.shape[1]
    offsets = [int(o) for o in np.cumsum(SPLIT_SIZES)[:-1]]
    table_a = rel_bias[:, :A_Q_HEADS]
    table_b = rel_bias[:, A_Q_HEADS:]
    for l in range(DEPTH):
        h = rms_norm(x, norm_g[l])
        proj = jnp.einsum('bsd,de->bse', h, w_in[l])
        (qa, ka, va, za, qb, kb, vb, zb, qi, ki, wi, gates) = jnp.split(proj, offsets, axis=-1)
        qa = rms_norm(qa.reshape(B, S, A_Q_HEADS, HEAD_DIM), qnorm_a[l])
        ka = rms_norm(ka.reshape(B, S, A_KV_HEADS, HEAD_DIM), knorm_a[l])
        va = va.reshape(B, S, A_KV_HEADS, HEAD_DIM)
        ya = swa_sink_attention(qa, ka, va, sinks_a[l], table_a) * jax.nn.silu(za)
        qb = rms_norm(qb.reshape(B, S, B_HEADS, HEAD_DIM), qnorm_b[l])
        kb = rms_norm(kb.reshape(B, S, B_HEADS, HEAD_DIM), knorm_b[l])
        vb = vb.reshape(B, S, B_HEADS, HEAD_DIM)
        qi = qi.reshape(B, S, IDX_HEADS, IDX_DIM)
        yb = dsa_attention(qb, kb, vb, qi, ki, wi, table_b) * jax.nn.silu(zb)
        g = jax.nn.sigmoid(gates.astype(jnp.float32)).astype(x.dtype)
        merged = (g[..., :D_MODEL] * jnp.einsum('bse,ed->bsd', ya, w_proj_a[l])
                  + g[..., D_MODEL:] * jnp.einsum('bse,ed->bsd', yb, w_proj_b[l]))
        x = x + jnp.einsum('bsd,de->bse', merged, w_out[l])
    return x
```

```python
import math
from contextlib import ExitStack

import numpy as np
import concourse.bass as bass
import concourse.mybir as mybir
from concourse.bass_utils import run_bass_kernel_spmd

F32 = mybir.dt.float32
BF16 = mybir.dt.bfloat16
ALU = mybir.AluOpType
AF = mybir.ActivationFunctionType
AX = mybir.AxisListType

S_LEN = 2048
D = 1024
NT = 16
INW = 5672
NIT = 16
K0 = 10
BIG = 1.0e30
ENGS = ("pe", "dve", "act", "pool", "sp")
_ESZ = {F32: 4, BF16: 2}
PAGE = 2048


def _esize(dt):
    return _ESZ[dt]


def _box(ap):
    t = ap.tensor
    tn = type(t).__name__
    if tn.startswith("DRam"):
        return None
    dims = list(ap.ap)
    off = int(ap.offset)
    es = _esize(ap.dtype)
    pstride = int(dims[0][0])
    npart = int(dims[0][1])
    if pstride <= 0:
        pstride = 1
        for s in list(t.shape)[1:]:
            pstride *= int(s)
    p0 = off // pstride
    f0 = off % pstride
    f1 = f0
    for st, n in dims[1:]:
        f1 += abs(int(st)) * (int(n) - 1)
    if tn.startswith("PSum"):
        base = 1 << 24
        base += int(t.name[2:]) * 2048
    else:
        base = int(t.manual_sbuf_range[0])
    return (p0, p0 + npart, base + f0 * es, base + (f1 + 1) * es)


def _ovl(a, b):
    return a[0] < b[1] and b[0] < a[1] and a[2] < b[3] and b[2] < a[3]


def _contains(a, b):
    return a[0] <= b[0] and a[1] >= b[1] and a[2] <= b[2] and a[3] >= b[3]


class Sched:
    def __init__(self, nc, stack):
        self.nc = nc
        self.stack = stack
        self.q = {e: [] for e in ENGS}
        self.sems = {}
        self.cnt = {}
        self.unit = {}
        for e in ENGS:
            self._mksem(e, 1)
        self.seen = {e: {} for e in ENGS}
        self.pages = {}
        self.n_inst = 0

    def _mksem(self, key, unit):
        self.sems[key] = self.stack.enter_context(self.nc.semaphore("s_" + key))
        self.cnt[key] = 0
        self.unit[key] = unit

    def _pages(self, box):
        return range(box[2] // PAGE, (box[3] - 1) // PAGE + 1)

    def _scan(self, box, want_reads, deps, eng, raw):
        for pg in self._pages(box):
            for r in self.pages.get(pg, ()):
                if (want_reads or r[1] == "w") and _ovl(r[0], box):
                    k, i = r[2]
                    if k == eng and eng == "pe":
                        continue
                    if deps.get(k, 0) < i:
                        deps[k] = i

    def _record(self, prod, rboxes, wboxes):
        for box in wboxes:
            for pg in self._pages(box):
                lst = self.pages.setdefault(pg, [])
                lst[:] = [r for r in lst if not _contains(box, r[0])]
                lst.append((box, "w", prod))
        for box in rboxes:
            for pg in self._pages(box):
                lst = self.pages.setdefault(pg, [])
                lst[:] = [r for r in lst if not (r[1] == "r" and r[2][0] == prod[0] and _contains(box, r[0]))]
                lst.append((box, "r", prod))

    def _emit_waits(self, eng, deps):
        seen = self.seen[eng]
        for k, i in deps.items():
            if k == eng and eng == "pe":
                continue
            if seen.get(k, 0) >= i:
                continue
            seen[k] = i
            self.q[eng].append(("wait", k, i * self.unit[k]))

    def op(self, eng, fn, reads=(), writes=(), inc=True):
        rb = [b for b in (_box(a) for a in reads) if b is not None]
        wb = [b for b in (_box(a) for a in writes) if b is not None]
        deps = {}
        for b in rb:
            self._scan(b, False, deps, eng, True)
        for b in wb:
            self._scan(b, True, deps, eng, False)
        self._emit_waits(eng, deps)
        idx = self.cnt[eng] + 1
        if inc:
            self.cnt[eng] = idx
        self.q[eng].append(("inst", fn, inc, eng))
        self._record((eng, idx), rb, wb)
        self.n_inst += 1

    def dma(self, qeng, semkey, pairs, **kw):
        if semkey not in self.sems:
            self._mksem(semkey, 16)
        rb = [b for b in (_box(p[1]) for p in pairs) if b is not None]
        wb = [b for b in (_box(p[0]) for p in pairs) if b is not None]
        deps = {}
        for b in rb:
            self._scan(b, False, deps, "__dma__", False)
        for b in wb:
            self._scan(b, True, deps, "__dma__", False)
        self._emit_waits(qeng, deps)
        idx = self.cnt[semkey] + len(pairs)
        self.cnt[semkey] = idx
        for (o, i) in pairs:
            self.q[qeng].append(("dma", o, i, semkey, kw))
        self._record((semkey, idx), rb, wb)
        self.n_inst += len(pairs)

    def wait_all(self, eng, keys):
        for k in keys:
            if self.cnt[k] > self.seen[eng].get(k, 0):
                self.seen[eng][k] = self.cnt[k]
                self.q[eng].append(("wait", k, self.cnt[k] * self.unit[k]))

    def emit(self):
        nc = self.nc
        sems = self.sems
        q = self.q

        def run(engobj, items):
            for it in items:
                if it[0] == "wait":
                    engobj.wait_ge(sems[it[1]], it[2])
                elif it[0] == "inst":
                    ins = it[1](engobj)
                    if it[2]:
                        ins.then_inc(sems[it[3]], 1)
                else:
                    _, o, i, sk, kw = it
                    engobj.dma_start(out=o, in_=i, **kw).then_inc(sems[sk], 16)

        with nc.Block() as block:
            @block.tensor
            def _(e):
                run(e, q["pe"])

            @block.vector
            def _(e):
                run(e, q["dve"])

            @block.scalar
            def _(e):
                run(e, q["act"])

            @block.gpsimd
            def _(e):
                run(e, q["pool"])

            @block.sync
            def _(e):
                run(e, q["sp"])


class Arena:
    def __init__(self, nc, base, size, tag):
        self.nc, self.base, self.size, self.tag, self.off = nc, base, size, tag, 0

    def alloc(self, name, shape, dt):
        n = 1
        for s in shape[1:]:
            n *= s
        nb = n * _esize(dt)
        nb = (nb + 31) // 32 * 32
        assert self.off + nb <= self.size, (self.tag, name, self.off, nb, self.size)
        t = self.nc.alloc_sbuf_tensor_at(name, list(shape), dt, offset=self.base + self.off)
        self.off += nb
        return t


def build_program(stop=99, dbg=None):
    nc = bass.Bass("TRN2", target_bir_lowering=False)
    dr = lambda name, shape, kind="ExternalInput": nc.dram_tensor(name, shape, F32, kind=kind).ap()
    x = dr("x", [S_LEN, D])
    w_in = dr("w_in", [D, INW])
    w_pa = dr("w_pa", [512, D])
    w_pb = dr("w_pb", [512, D])
    w_out = dr("w_out", [D, D])
    c_gT = dr("c_gT", [128, 8])
    c_gq = dr("c_gq", [128, 4])
    c_sink = dr("c_sink", [128, 8])
    c_cfar = dr("c_cfar", [128, 8])
    c_biasA = dr("c_biasA", [128, 2 * 8 * 128])
    c_biasB = dr("c_biasB", [128, 2 * 8 * 128])
    c_mA = dr("c_mA", [128, 2 * 128])
    c_negm = dr("c_negm", [128, 128])
    c_tril = dr("c_tril", [128, 128])
    c_ident = dr("c_ident", [128, 128])
    c_bd = dr("c_bd", [128, 128])
    c_pow2 = dr("c_pow2", [128, NIT])
    c_npow2 = dr("c_npow2", [128, NIT])
    c_cthr = dr("c_cthr", [128, NT])
    out = dr("out", [S_LEN, D], kind="ExternalOutput")

    with ExitStack() as st:
        S = Sched(nc, st)

        def E(eng, meth, inc=True, **kw):
            reads, writes = [], []
            for k, v in kw.items():
                if hasattr(v, "tensor") and hasattr(v, "ap"):
                    (writes if k in ("out", "accum_out", "ap") else reads).append(v)
            S.op(eng, lambda e: getattr(e, meth)(**kw), reads=reads, writes=writes, inc=inc)

        def MM(outp, lhsT, rhs, start, stop, inc=None, **kw):
            if inc is None:
                inc = stop
            S.op("pe", lambda e: e.matmul(outp, lhsT=lhsT, rhs=rhs, start=start, stop=stop, **kw),
                 reads=[lhsT, rhs], writes=[outp], inc=inc)

        def TR(outp, in_, ident, inc=True):
            S.op("pe", lambda e: e.transpose(out=outp, in_=in_, identity=ident),
                 reads=[in_, ident], writes=[outp], inc=inc)

        dkeys = []

        def dump(name, sb_ap, shape, dt):
            if dbg is None:
                return
            d = nc.dram_tensor(name, list(shape), dt, kind="ExternalOutput").ap()
            S.dma("sp", "d_dbg", [(d, sb_ap)])
            dbg.append(name)
            if "d_dbg" not in dkeys:
                dkeys.append("d_dbg")

        def finish():
            S.wait_all("sp", dkeys)
            S.emit()
            return nc

        base0 = (int(nc.sbuf_base) + 63) // 64 * 64
        top = int(nc.sbuf_top)
        cur = [base0]

        def region(size, tag):
            a = Arena(nc, cur[0], size, tag)
            cur[0] += size
            assert cur[0] <= top, (tag, cur[0], top)
            return a

        R_CONST = region(12288, "const")
        R_HT = region(32768, "hT")
        R_W = region(24576, "w")
        R_OT = region(32768, "oT")
        R_QKV = region(90432, "qkv")
        R_SP = region((top - cur[0]) // 64 * 64, "spare")

        pb0 = nc.alloc_psum_tensor("pb0", [128, 512], F32)
        pb1 = nc.alloc_psum_tensor("pb1", [128, 512], F32)
        pb2 = nc.alloc_psum_tensor("pb2", [128, 1024], F32)
        pb4 = nc.alloc_psum_tensor("pb4", [128, 1024], F32)
        pb6 = nc.alloc_psum_tensor("pb6", [128, 512], F32)
        pb7 = nc.alloc_psum_tensor("pb7", [128, 512], F32)
        pbf = [pb0.ap(), pb1.ap(), pb2.ap()[:, 0:512], pb2.ap()[:, 512:1024], pb4.ap()[:, 0:512],
               pb4.ap()[:, 512:1024], pb6.ap(), pb7.ap()]
        _h2, _h4 = pb2.bitcast(BF16).ap(), pb4.bitcast(BF16).ap()
        pbh = [pb0.bitcast(BF16).ap(), pb1.bitcast(BF16).ap(), _h2[:, 0:1024], _h2[:, 1024:2048], _h4[:, 0:1024],
               _h4[:, 1024:2048], pb6.bitcast(BF16).ap(), pb7.bitcast(BF16).ap()]
        sc2w = [pb2.ap().rearrange("p (f a b) -> p f a b", f=2, a=4), pb4.ap().rearrange("p (f a b) -> p f a b", f=2, a=4)]

        identb = R_CONST.alloc("identb", [128, 128], BF16)
        bdb = R_CONST.alloc("bdb", [128, 128], BF16)
        EbA = R_CONST.alloc("EbA", [128, 2, 8, 128], BF16)
        EbB = R_CONST.alloc("EbB", [128, 2, 8, 128], BF16)
        negm = R_CONST.alloc("negm", [128, 128], F32)
        trilb = R_CONST.alloc("trilb", [128, 128], BF16)
        gT = R_CONST.alloc("gT", [128, 8], F32)
        gq = R_CONST.alloc("gq", [128, 4], F32)
        esink = R_CONST.alloc("esink", [128, 8], F32)
        cfar = R_CONST.alloc("cfar", [128, 8], F32)
        epsT = R_CONST.alloc("epsT", [128, 1], F32)
        pow2 = R_CONST.alloc("pow2", [128, NIT], F32)
        npow2 = R_CONST.alloc("npow2", [128, NIT], F32)
        cthr = R_CONST.alloc("cthr", [128, NT], F32)
        sgn = R_CONST.alloc("sgn", [128, NT, 8], F32)
        wab = R_CONST.alloc("wab", [128, NT, 8], F32)
        ss = R_CONST.alloc("ss", [128, NT], F32)
        sd = R_CONST.alloc("sd", [128, NT], F32)
        rstd = R_CONST.alloc("rstd", [128, NT], F32)
        smalls = R_CONST.alloc("smalls", [128, 96], F32)
        Rrs = [smalls[:, 0:1], smalls[:, 7:8]]
        mid = smalls[:, 1:2]
        cntv = smalls[:, 2:3]
        dirv = smalls[:, 3:4]
        den = smalls[:, 8:16]
        Rk = smalls[:, 16:16 + NIT]
        negRks = [smalls[:, 32:32 + NIT], smalls[:, 48:48 + NIT]]
        csum = smalls[:, 4:5]
        dirS = smalls[:, 5:6]
        nmids = [smalls[:, 6:7], smalls[:, 64:65]]
        mids = [smalls[:, 1:2], smalls[:, 65:66]]
        cntvs = [smalls[:, 2:3], smalls[:, 66:67]]
        dirvs = [smalls[:, 3:4], smalls[:, 67:68]]
        csums = [smalls[:, 4:5], smalls[:, 68:69]]
        dirSs = [smalls[:, 5:6], smalls[:, 69:70]]
        Rks = [smalls[:, 16:16 + NIT], smalls[:, 70:70 + NIT]]

        hT = R_HT.alloc("hT", [128, 8, S_LEN], BF16)
        wsl = [R_W.alloc("wsl%d" % i, [128, 8, 512], BF16) for i in range(3)]
        oT = R_OT.alloc("oT", [128, 8, S_LEN], BF16)

        qaT = R_QKV.alloc("qaT", [128, 4, S_LEN], BF16)
        kaT2 = R_QKV.alloc("kaT2", [128, 2, S_LEN], BF16)
        vA = R_QKV.alloc("vA", [128, NT, 2, 65], BF16)
        qbT = R_QKV.alloc("qbT", [128, 4, S_LEN], BF16)
        kbT = R_QKV.alloc("kbT", [128, 4, S_LEN], BF16)
        vB = R_QKV.alloc("vB", [128, NT, 8, 65], BF16)
        q2T = R_QKV.alloc("q2T", [128, 2, S_LEN], BF16)
        kiT = R_QKV.alloc("kiT", [128, S_LEN], BF16)

        A1 = Arena(nc, R_OT.base, R_OT.size, "p01")
        xts = [A1.alloc("xt%d" % i, [128, D], F32) for i in range(2)]
        hns = [A1.alloc("hn%d" % i, [128, D], BF16) for i in range(2)]
        sqb = [A1.alloc("sqb%d" % i, [128, 512], BF16) for i in range(2)]
        sdb = [A1.alloc("sdb%d" % i, [128, 512], F32) for i in range(2)]
        q2b = [A1.alloc("q2b%d" % i, [128, 256], BF16) for i in range(2)]
        stgA = A1.alloc("stgA", [128, 2, 8, 128], F32)
        _sb = int(stgA.manual_sbuf_range[0])
        xts += [nc.alloc_sbuf_tensor_at("xt%d" % (2 + k), [128, D], F32, offset=_sb + 4096 * k) for k in range(2)]
        stgm = A1.alloc("stgm", [128, 2, 128], F32)
        stgi = A1.alloc("stgi", [128, 128], F32)
        stgd = A1.alloc("stgd", [128, 128], F32)
        stgt = A1.alloc("stgt", [128, 128], F32)

        xpre = set()
        for _i in range(2):
            S.dma("sp", "d_x%d" % _i, [(xts[_i][:], x[_i * 128:(_i + 1) * 128, :])])
            xpre.add(_i)
        S.dma("sp", "d_c", [
            (gT[:], c_gT[:, :]), (gq[:], c_gq[:, :]), (esink[:], c_sink[:, :]), (cfar[:], c_cfar[:, :]),
            (negm[:], c_negm[:, :]), (pow2[:], c_pow2[:, :]), (npow2[:], c_npow2[:, :]), (cthr[:], c_cthr[:, :]),
            (stgm[:].rearrange("p a b -> p (a b)"), c_mA[:, :]),
            (stgi[:], c_ident[:, :]), (stgd[:], c_bd[:, :]), (stgt[:], c_tril[:, :]),
        ])
        E("dve", "memset", ap=epsT[:], constant=1e-6)
        E("dve", "tensor_copy", out=identb[:], in_=stgi[:])
        E("dve", "tensor_copy", out=bdb[:], in_=stgd[:])
        E("dve", "tensor_copy", out=trilb[:], in_=stgt[:])
        E("pool", "memset", ap=vA[:, :, :, 64:65], constant=1.0)
        E("pool", "memset", ap=vB[:, :, :, 64:65], constant=1.0)
        E("dve", "tensor_scalar", out=gq[:, 0:1], in0=gq[:, 0:1], scalar1=0.125, scalar2=None, op0=ALU.mult)
        E("dve", "tensor_scalar", out=gq[:, 2:3], in0=gq[:, 2:3], scalar1=0.125, scalar2=None, op0=ALU.mult)
        E("act", "activation", out=esink[:], in_=esink[:], func=AF.Exp)
        S.dma("sp", "d_c2", [(stgA[:].rearrange("p a h t -> p (a h t)"), c_biasA[:, :])])
        E("act", "activation", out=stgA[:], in_=stgA[:], func=AF.Exp)
        for jt in range(2):
            E("dve", "tensor_tensor", out=EbA[:, jt, :, :], in0=stgA[:, jt, :, :],
              in1=stgm[:, jt, :].unsqueeze(1).broadcast_to([128, 8, 128]), op=ALU.mult)
        stgB = nc.alloc_sbuf_tensor_at("stgB", [128, 2, 8, 128], F32, offset=R_SP.base)
        S.dma("sp", "d_c3", [(stgB[:].rearrange("p a h t -> p (a h t)"), c_biasB[:, :])])
        for jt in range(2):
            E("dve", "tensor_tensor", out=stgB[:, jt, :, :], in0=stgB[:, jt, :, :],
              in1=cfar[:, :].unsqueeze(2).broadcast_to([128, 8, 128]), op=ALU.subtract)
        for jt in range(2):
            E("act", "activation", out=EbB[:, 1 - jt, :, :], in_=stgB[:, jt, :, :], func=AF.Exp)

        w_in_r = w_in.rearrange("(kc p) c -> p kc c", p=128)
        wstate = {"n": 0}

        def load_w(pieces):
            s = wstate["n"] % 3
            wstate["n"] += 1
            t = wsl[s]
            S.dma("pool", "d_w%d" % s,
                  [(t[:, :, d0:d0 + n], w_in_r[:, :, s0:s0 + n]) for (d0, n, s0) in pieces])
            return t

        def p0A(i):
            xt = xts[i % 4]
            hn = hns[i % 2]
            if i not in xpre:
                S.dma("sp", "d_x%d" % (i % 4), [(xt[:], x[i * 128:(i + 1) * 128, :])])
            E("dve", "scalar_tensor_tensor", out=hn[:], in0=xt[:], scalar=1.0, in1=xt[:],
              op0=ALU.mult, op1=ALU.mult, accum_out=ss[:, i:i + 1])
            E("act", "activation", out=sd[:, i:i + 1], in_=ss[:, i:i + 1], func=AF.Ln,
              scale=1.0 / D, bias=epsT[:])
            E("act", "activation", out=rstd[:, i:i + 1], in_=sd[:, i:i + 1], func=AF.Exp, scale=-0.5)
            E("act", "activation", out=hn[:], in_=xt[:], func=AF.Copy, scale=rstd[:, i:i + 1])
            ptr = pbh[i % 2][:, 0:1024].rearrange("p (a b) -> p a b", a=8)
            for kc in range(8):
                TR(ptr[:, kc, :], hn[:, kc * 128:(kc + 1) * 128], identb[:], inc=(kc == 7))

        def p0B(i):
            ptr = pbh[i % 2][:, 0:1024].rearrange("p (a b) -> p a b", a=8)
            E("dve", "tensor_tensor", out=hT[:, :, i * 128:(i + 1) * 128], in0=ptr,
              in1=gT[:, :].unsqueeze(2).broadcast_to([128, 8, 128]), op=ALU.mult)

        p0s = {"a": 0, "b": 0}

        def p0_adv():
            if p0s["a"] < NT:
                p0A(p0s["a"])
                p0s["a"] += 1
            if p0s["b"] < p0s["a"] - 1 or (p0s["a"] == NT and p0s["b"] < NT):
                p0B(p0s["b"])
                p0s["b"] += 1

        def p0_need(ntiles):
            while p0s["b"] < ntiles:
                if p0s["a"] < NT and p0s["a"] <= p0s["b"] + 1:
                    p0A(p0s["a"])
                    p0s["a"] += 1
                else:
                    p0B(p0s["b"])
                    p0s["b"] += 1

        def tsl(tg):
            return slice(tg * 512, (tg + 1) * 512)

        fmn = [0]
        pend = [None]

        def fm_mm(ws, ccol, tg):
            n = fmn[0]
            fmn[0] += 1
            acc = pbf[(2, 3, 6, 7)[n % 4]]
            for kc in range(8):
                MM(acc, ws[:, kc, ccol:ccol + 128], hT[:, kc, tsl(tg)], kc == 0, kc == 7)
            return n

        def fm_post(n, dst, gain, norm):
            acc = pbf[(2, 3, 6, 7)[n % 4]]
            if norm:
                sq = sqb[n % 2]
                E("act", "activation", out=sq[:], in_=acc, func=AF.Square)
                ssb = pbf[4 + n % 2]
                MM(ssb, bdb[:], sq[:], True, True)
                sdt = sdb[n % 2]
                E("act", "activation", out=sdt[:], in_=ssb, func=AF.Ln, scale=1.0 / 64, bias=epsT[:])
                E("act", "activation", out=sdt[:], in_=sdt[:], func=AF.Exp, scale=-0.5)
                E("dve", "scalar_tensor_tensor", out=dst, in0=acc, scalar=gain, in1=sdt[:],
                  op0=ALU.mult, op1=ALU.mult)
            else:
                E("act", "activation", out=dst, in_=acc, func=AF.Copy)

        pendq = []

        def fm(ws, ccol, dst, gain, norm, tg):
            n = fm_mm(ws, ccol, tg)
            if len(pendq) == 2:
                fm_post(*pendq.pop(0))
            pendq.append((n, dst, gain, norm))

        def fm_flush():
            while pendq:
                fm_post(*pendq.pop(0))

        ws = load_w([(0, 512, 0)])
        p0_need(NT)
        ws_qb = load_w([(0, 512, 1280)])
        ws_kb = load_w([(0, 512, 1792)])
        for tg in range(4):
            for c in range(4):
                fm(ws, c * 128, qaT[:, c, tsl(tg)], gq[:, 0:1], True, tg)
        ws_k = load_w([(0, 64, 512), (64, 64, 512), (128, 64, 576), (192, 64, 576),
                       (256, 32, 3584), (288, 32, 3584), (320, 32, 3584), (352, 32, 3584)])
        for c in range(4):
            for tg in range(4):
                fm(ws_qb, c * 128, qbT[:, c, tsl(tg)], gq[:, 2:3], True, tg)
        ws_vb = load_w([(0, 512, 2304)])
        for c in range(4):
            for tg in range(4):
                fm(ws_kb, c * 128, kbT[:, c, tsl(tg)], gq[:, 3:4], True, tg)
        ws_g5 = load_w([(0, 128, 640), (128, 256, 3328), (384, 8, 3616)])
        for c in range(2):
            for tg in range(4):
                fm(ws_k, c * 128, kaT2[:, c, tsl(tg)], gq[:, 1:2], True, tg)
        for tg in range(4):
            fm(ws_k, 256, kiT[:, tsl(tg)], None, False, tg)
        fm_flush()

        def tm_mm(ti):
            accv = pbf[2 + 2 * (ti % 2)]
            accq = pbf[3 + 2 * (ti % 2)]
            tok = slice(ti * 128, (ti + 1) * 128)
            for kc in range(8):
                MM(accv, hT[:, kc, tok], ws_vb[:, kc, 0:512], kc == 0, kc == 7)
            for kc in range(8):
                MM(accq[:, 0:392], hT[:, kc, tok], ws_g5[:, kc, 0:392], kc == 0, kc == 7)

        def tm_post(ti):
            accv = pbf[2 + 2 * (ti % 2)]
            accq = pbf[3 + 2 * (ti % 2)]
            tok = slice(ti * 128, (ti + 1) * 128)
            E("act", "activation", out=vB[:, ti, :, 0:64], in_=accv.rearrange("p (h d) -> p h d", h=8), func=AF.Copy)
            E("act", "activation", out=vA[:, ti, :, 0:64],
              in_=accq[:, 0:128].rearrange("p (h d) -> p h d", h=2), func=AF.Copy)
            E("act", "activation", out=sgn[:, ti, :], in_=accq[:, 384:392], func=AF.Sign)
            E("dve", "scalar_tensor_tensor", out=wab[:, ti, :], in0=accq[:, 384:392], scalar=0.0625,
              in1=sgn[:, ti, :], op0=ALU.mult, op1=ALU.mult)
            qb2 = q2b[ti % 2]
            E("dve", "tensor_tensor", out=qb2[:].rearrange("p (h e) -> p h e", h=8),
              in0=accq[:, 128:384].rearrange("p (h e) -> p h e", h=8),
              in1=wab[:, ti, :].unsqueeze(2).broadcast_to([128, 8, 32]), op=ALU.mult)
            ptr = pbh[ti % 2][:, 0:256].rearrange("p (a b) -> p a b", a=2)
            for g in range(2):
                TR(ptr[:, g, :], qb2[:, g * 128:(g + 1) * 128], identb[:], inc=(g == 1))
            E("act", "activation", out=q2T[:, :, tok], in_=ptr, func=AF.Copy)

        tm_mm(0)
        for ti in range(NT):
            if ti + 1 < NT:
                tm_mm(ti + 1)
            tm_post(ti)

        dump("d_hT", hT[:], [128, 8, S_LEN], BF16)
        if stop == 0:
            return finish()
        for nm, t in (("d_qaT", qaT), ("d_kaT2", kaT2), ("d_vA", vA), ("d_qbT", qbT), ("d_kbT", kbT), ("d_vB", vB),
                      ("d_q2T", q2T), ("d_kiT", kiT), ("d_sgn", sgn)):
            dump(nm, t[:], [int(v) for v in t.shape], t.dtype)
        if stop == 1:
            return finish()
        import os
        _nt2 = int(os.environ.get("DBG_NT", NT))
        A2 = Arena(nc, R_W.base, R_W.size, "p2a")
        A2b = Arena(nc, R_SP.base, R_SP.size, "p2b")
        scoresb = [A2.alloc("scores%d" % k, [128, S_LEN], F32) for k in range(2)]
        masktb = [A2.alloc("maskt%d" % k, [128, S_LEN], BF16) for k in range(2)]
        maskTs = [A2b.alloc("maskT%d" % k, [128, NT, 128], BF16) for k in range(2)]
        exBp = [A2b.alloc("exBp%d" % i, [128, 2, 4, 128], BF16) for i in range(2)]
        exB = [exBp[i // 2][:, i % 2, :, :] for i in range(4)]
        PTBp = [A2b.alloc("PTBp%d" % i, [128, 2, 4, 128], BF16) for i in range(2)]
        PTB = [PTBp[i // 2][:, i % 2, :, :] for i in range(4)]
        ob = A2b.alloc("ob", [128, D], BF16)

        oaccv = [pbf[6][:, 0:260].rearrange("p (h d) -> p h d", h=4) for k in range(2)]
        sacc = pbf[7]
        scb2 = [[pbf[2 + 2 * s_ + k].rearrange("p (a b) -> p a b", a=4) for k in range(2)] for s_ in range(2)]
        scb = scb2[1]
        mtr = pbh[1][:, 0:1024].rearrange("p (a b) -> p a b", a=8)
        otr = pbh[2][:, 0:1024].rearrange("p (a b) -> p a b", a=8)
        ctr = {"g": 0}

        def gen_S1(i):
            qs = slice(i * 128, (i + 1) * 128)
            nk = (i + 1) * 128
            par = i % 2
            scores = scoresb[par]
            maskt = masktb[par]
            junk = maskt
            maskT = maskTs[par]
            if i >= 2:
                Rb = [maskt[:, 0:512], maskt[:, 512:1024]]
                Dh = maskt[:, 1024:2048].rearrange("p (h c) -> p h c", h=8)
                E("dve", "tensor_tensor", out=Dh, in0=identb[:].unsqueeze(1).broadcast_to([128, 8, 128]),
                  in1=sgn[:, i, :].unsqueeze(2).broadcast_to([128, 8, 128]), op=ALU.mult)
                work = [(ch, h) for ch in range((nk + 511) // 512) for h in range(8)]

                def idx_front(ch, h):
                    cw = min(512, nk - ch * 512)
                    csl = slice(ch * 512, ch * 512 + cw)
                    g, r = h // 4, h % 4
                    ip = pbf[h % 2]
                    MM(ip[:, 0:cw], q2T[32 * r:32 * r + 32, g, qs], kiT[32 * r:32 * r + 32, csl], True, True,
                       tile_position=(32 * r, 0))
                    E("act", "activation", out=Rb[h % 2][:, 0:cw], in_=ip[:, 0:cw], func=AF.Relu)

                def idx_back(ch, h):
                    cw = min(512, nk - ch * 512)
                    csl = slice(ch * 512, ch * 512 + cw)
                    MM(sacc[:, 0:cw], Dh[:, h, :], Rb[h % 2][:, 0:cw], h == 0, h == 7)
                    if h == 7:
                        E("dve", "tensor_copy", out=scores[:, csl], in_=sacc[:, 0:cw])

                idx_front(*work[0])
                for n_, wk in enumerate(work):
                    if n_ + 1 < len(work):
                        idx_front(*work[n_ + 1])
                    idx_back(*wk)
                    yield
                idx_done[i] = True
                Rr = Rrs[par]
                mid, cntv, dirv, csum, dirS, Rk = mids[par], cntvs[par], dirvs[par], csums[par], dirSs[par], Rks[par]
                E("dve", "tensor_reduce", out=Rr, in_=scores[:, 0:nk], axis=AX.X, op=ALU.max, apply_absolute_value=True)
                E("dve", "tensor_tensor", out=scores[:, i * 128:nk], in0=scores[:, i * 128:nk], in1=negm[:], op=ALU.add)
                E("dve", "tensor_scalar", out=Rk, in0=pow2[:], scalar1=Rr, scalar2=None, op0=ALU.mult)
                E("dve", "memset", ap=mid, constant=0.0)
                E("act", "activation", out=negRks[par], in_=npow2[:], func=AF.Copy, scale=Rr)
                yield
                for k in range(K0):
                    E("dve", "tensor_scalar", out=junk[:, 0:nk], in0=scores[:, 0:nk], scalar1=mid, scalar2=None,
                      op0=ALU.is_ge, op1=ALU.add, accum_out=cntv)
                    yield
                    E("dve", "tensor_scalar", out=dirv, in0=cntv, scalar1=255.5, scalar2=0.5,
                      op0=ALU.is_ge, op1=ALU.subtract)
                    yield
                    E("dve", "scalar_tensor_tensor", out=mid, in0=dirv, scalar=Rk[:, k:k + 1], in1=mid,
                      op0=ALU.mult, op1=ALU.add)
                    yield
                nmid = nmids[par]
                E("act", "activation", out=nmid, in_=mid, func=AF.Copy, scale=-1.0)
                for k in range(K0, NIT):
                    E("act", "activation", out=junk[:, 0:nk], in_=scores[:, 0:nk], func=AF.Sign, bias=nmid,
                      accum_out=csum)
                    yield
                    E("act", "activation", out=dirS, in_=csum, func=AF.Sign, bias=cthr[:, i:i + 1])
                    yield
                    E("act", "activation", out=nmid, in_=dirS, func=AF.Identity, scale=negRks[par][:, k:k + 1],
                      bias=nmid)
                    yield
                E("dve", "tensor_scalar", out=maskt[:, 0:nk], in0=scores[:, 0:nk], scalar1=nmid, scalar2=0.0,
                  op0=ALU.add, op1=ALU.is_ge)
            elif i == 0:
                E("dve", "tensor_copy", out=maskt[:, 0:128], in_=trilb[:])
            else:
                E("dve", "memset", ap=maskt[:, 0:128], constant=1.0)
                E("dve", "tensor_copy", out=maskt[:, 128:256], in_=trilb[:])
            yield
            for j0 in range(0, i + 1, 8):
                njs = min(i + 1, j0 + 8) - j0
                for jj in range(njs):
                    j = j0 + jj
                    TR(mtr[:, jj, :], maskt[:, j * 128:(j + 1) * 128], identb[:], inc=(jj == njs - 1))
                E("act", "activation", out=maskT[:, j0:j0 + njs, :], in_=mtr[:, 0:njs, :], func=AF.Copy)
                yield

        def n_S1(i):
            nk = (i + 1) * 128
            n = 1 + (i // 8 + 1)
            if i >= 2:
                n += 8 * ((nk + 511) // 512) + 1 + 3 * NIT
            return n

        def gen_S2(i):
            qs = slice(i * 128, (i + 1) * 128)
            maskT = maskTs[i % 2]
            oa = oaccv[0]
            jts = [(0, i)] + ([(1, i - 1)] if i >= 1 else [])
            njt = len(jts)
            nfar = max(0, i - 1)
            groups = [("far", list(range(j0, min(nfar, j0 + 4)))) for j0 in range(0, nfar, 4)]
            groups.append(("near", [j for j in (i - 1, i) if j >= 0]))
            items = [("A", hk, None) for hk in range(2)] + [(c, kind, js) for c in range(4) for (kind, js) in groups]

            def front(item, gi):
                c, kind, js = item
                if c == "A":
                    hk = kind
                    for jn, (jt, j) in enumerate(jts):
                        for hh in range(4):
                            hq = 4 * hk + hh
                            cq, hf = hq // 2, hq % 2
                            MM(scb2[gi][hf][:, jn * 2 + hh // 2, :],
                               kaT2[64 * hf:64 * hf + 64, hk, j * 128:(j + 1) * 128],
                               qaT[64 * hf:64 * hf + 64, cq, qs], True, True,
                               inc=(jn == njt - 1 and hh == 3))
                else:
                    for jj, j in enumerate(js):
                        for hf in range(2):
                            MM(scb2[gi][hf][:, jj, :], kbT[64 * hf:64 * hf + 64, c, j * 128:(j + 1) * 128],
                               qbT[64 * hf:64 * hf + 64, c, qs], True, True,
                               inc=(jj == len(js) - 1 and hf == 1))

            def mid(item, gi, hf):
                c, kind, js = item
                ex = exB[2 * gi + hf]
                pt = PTB[2 * gi + hf]
                if c == "A":
                    hk = kind
                    if hf == 0:
                        E("act", "activation", out=exBp[gi][:, :, 0:2 * njt, :], in_=sc2w[gi][:, :, 0:2 * njt, :],
                          func=AF.Exp)
                    if hf == 0:
                        for jn, (jt, j) in enumerate(jts):
                            E("dve", "tensor_tensor", out=PTBp[gi][:, :, 2 * jn:2 * jn + 2, :],
                              in0=exBp[gi][:, :, 2 * jn:2 * jn + 2, :],
                              in1=EbA[:, jt, 4 * hk:4 * hk + 4, :].rearrange("p (hh f) c -> p f hh c", f=2),
                              op=ALU.mult)
                else:
                    n = len(js)
                    h = 2 * c + hf
                    if hf == 0:
                        E("act", "activation", out=exBp[gi][:, :, 0:n, :], in_=sc2w[gi][:, :, 0:n, :], func=AF.Exp)
                        E("dve", "tensor_tensor", out=PTBp[gi][:, :, 0:n, :], in0=exBp[gi][:, :, 0:n, :],
                          in1=maskT[:, js[0]:js[0] + n, :].unsqueeze(1).broadcast_to([128, 2, n, 128]), op=ALU.mult)
                        if kind == "near":
                            E("dve", "tensor_tensor", out=PTBp[gi][:, :, 0:n, :], in0=PTBp[gi][:, :, 0:n, :],
                              in1=EbB[:, 2 - n:2, 2 * c:2 * c + 2, :].rearrange("p a b c -> p b a c"), op=ALU.mult)

            def back(item, gi):
                c, kind, js = item
                if c == "A":
                    hk = kind
                    for hh in range(4):
                        pt = PTB[2 * gi + hh % 2]
                        for jn, (jt, j) in enumerate(jts):
                            MM(oa[:, hh, :], pt[:, 2 * jn + hh // 2, :], vA[:, j, hk, :], jn == 0, jn == njt - 1,
                               inc=(hh == 3 and jn == njt - 1))
                    E("dve", "tensor_tensor", out=den[:, 0:4], in0=oa[:, :, 64], in1=esink[:, 4 * hk:4 * hk + 4],
                      op=ALU.add)
                    E("dve", "reciprocal", out=den[:, 0:4], in_=den[:, 0:4])
                    E("dve", "tensor_tensor", out=ob[:, hk * 256:(hk + 1) * 256].rearrange("p (h d) -> p h d", h=4),
                      in0=oa[:, :, 0:64], in1=den[:, 0:4].unsqueeze(2).broadcast_to([128, 4, 64]), op=ALU.mult)
                else:
                    n = len(js)
                    k4 = c // 2
                    for hf in range(2):
                        h = 2 * c + hf
                        pt = PTB[2 * gi + hf]
                        for jj, j in enumerate(js):
                            first = (c % 2 == 0 and hf == 0 and j == 0)
                            last = (c % 2 == 1 and hf == 1 and j == i)
                            MM(oa[:, h % 4, :], pt[:, jj, :], vB[:, j, h, :], first, last,
                               inc=(hf == 1 and jj == n - 1))
                    if c % 2 == 1 and kind == "near":
                        E("dve", "reciprocal", out=den[:, 4:8], in_=oa[:, :, 64])
                        E("dve", "tensor_tensor",
                          out=ob[:, 512 + k4 * 256:512 + (k4 + 1) * 256].rearrange("p (h d) -> p h d", h=4),
                          in0=oa[:, :, 0:64], in1=den[:, 4:8].unsqueeze(2).broadcast_to([128, 4, 64]), op=ALU.mult)

            gi0 = ctr["g"] % 2
            ctr["g"] += len(items)
            front(items[0], gi0)
            yield
            for k, item in enumerate(items):
                gi = (gi0 + k) % 2
                for hf in range(2):
                    mid(item, gi, hf)
                    yield
                if k + 1 < len(items):
                    front(items[k + 1], 1 - gi)
                    yield
                back(item, gi)
                yield
            for c in range(8):
                TR(otr[:, c, :], ob[:, c * 128:(c + 1) * 128], identb[:], inc=(c == 7))
            E("act", "activation", out=oT[:, :, qs], in_=otr, func=AF.Copy)
            yield

        def n_S2(i):
            nfar = max(0, i - 1)
            return 2 + 4 * (2 + 4 * ((nfar + 3) // 4 + 1))

        w_pa_r = w_pa.rearrange("(ec p) d -> p ec d", p=128)
        w_pb_r = w_pb.rearrange("(ec p) d -> p ec d", p=128)
        w_out_r = w_out.rearrange("(dc p) e -> p dc e", p=128)
        WgB0 = nc.alloc_sbuf_tensor_at("WgB0", [128, 8, 512], BF16, offset=int(q2T.manual_sbuf_range[0]))
        WpA0 = nc.alloc_sbuf_tensor_at("WpA0", [128, 4, 512], BF16, offset=int(kiT.manual_sbuf_range[0]))
        pf = {"z0": False, "r": False}

        def pf_z0():
            if not pf["z0"]:
                pf["z0"] = True
                S.dma("pool", "d_w0", [(wsl[0][:], w_in_r[:, :, 768:1280])])

        def pf_rest():
            if not pf["r"]:
                pf["r"] = True
                S.dma("pool", "d_w1", [(wsl[1][:], w_in_r[:, :, 2816:3328])])
                S.dma("pool", "d_g0", [
                    (wsl[2][:], w_in_r[:, :, 3624:3624 + 512]),
                    (WgB0[:], w_in_r[:, :, 4648:4648 + 512]),
                    (WpA0[:], w_pa_r[:, :, 0:512]),
                ])

        live = {}
        idx_done = {0: True, 1: True}

        def start(j):
            if j < _nt2 and j not in live:
                live[j] = [gen_S1(j), n_S1(j), 0]

        def adv(j, frac):
            st = live.get(j)
            if st is None:
                return
            pv = live.get(j - 1)
            while pv is not None and pv[0] is not None and not idx_done.get(j - 1, False):
                try:
                    next(pv[0])
                    pv[2] += 1
                except StopIteration:
                    pv[0] = None
            while st[0] is not None and st[2] < frac * st[1]:
                try:
                    next(st[0])
                    st[2] += 1
                except StopIteration:
                    st[0] = None
            if frac >= 1.0:
                while st[0] is not None:
                    try:
                        next(st[0])
                    except StopIteration:
                        st[0] = None

        start(0)
        adv(0, 1.0)
        start(1)
        adv(1, 1.0)
        start(2)
        adv(2, 0.5)
        for i in range(_nt2):
            start(i + 1)
            start(i + 2)
            if i == NT - 2:
                pf_z0()
            if i == NT - 1:
                pf_rest()
            g2 = gen_S2(i)
            n2 = n_S2(i)
            s = 0
            for _ in g2:
                s += 1
                p = min(1.0, s / n2)
                adv(i + 1, min(1.0, 0.5 + 0.5 * p / 0.95))
                adv(i + 2, 0.5 * min(1.0, p / 0.8))
            adv(i + 1, 1.0)
            live.pop(i + 1, None)

        dump("d_oT", oT[:], [128, 8, S_LEN], BF16)
        if stop == 2:
            return finish()
        pf_z0()
        pf_rest()
        A3 = Arena(nc, R_QKV.base, R_QKV.size, "p3")
        A3b = Arena(nc, R_SP.base, R_SP.size, "p3b")
        mT = A3.alloc("mT", [128, 8, S_LEN], BF16)
        gws = [{"gA": wsl[2], "gB": WgB0, "pA": WpA0, "pB": None},
               {"gA": A3.alloc("WgA1", [128, 8, 512], BF16), "gB": A3.alloc("WgB1", [128, 8, 512], BF16),
                "pA": A3.alloc("WpA1", [128, 4, 512], BF16), "pB": A3.alloc("WpB1", [128, 4, 512], BF16)}]
        gws[0]["pB"] = A3.alloc("WpB0", [128, 4, 512], BF16)
        xts2 = [A3.alloc("xo%d" % i, [128, D], F32) for i in range(2)]
        tmpz = [A3b.alloc("tmpz%d" % i, [128, 512], BF16) for i in range(2)]
        sA = [A3b.alloc("sA%d" % i, [128, 512], F32) for i in range(2)]
        sB = [A3b.alloc("sB%d" % i, [128, 512], F32) for i in range(2)]
        tA = [A3b.alloc("tA%d" % i, [128, 512], F32) for i in range(2)]
        S.dma("pool", "d_g0b", [(gws[0]["pB"][:], w_pb_r[:, :, 0:512])])
        S.dma("pool", "d_g1", [
            (gws[1]["pA"][:], w_pa_r[:, :, 512:1024]), (gws[1]["pB"][:], w_pb_r[:, :, 512:1024]),
            (gws[1]["gA"][:], w_in_r[:, :, 3624 + 512:3624 + 1024]),
            (gws[1]["gB"][:], w_in_r[:, :, 4648 + 512:4648 + 1024]),
        ])

        zn = 0
        for zi, (col0, cbase) in enumerate(((768, 0), (2816, 4))):
            wz = wsl[zi]
            for cc in range(4):
                for tg in range(4):
                    pz = pbf[zn % 2]
                    tz = tmpz[zn % 2]
                    zn += 1
                    for kc in range(8):
                        MM(pz, wz[:, kc, cc * 128:(cc + 1) * 128], hT[:, kc, tsl(tg)], kc == 0, kc == 7)
                    E("act", "activation", out=tz[:], in_=pz, func=AF.Silu)
                    E("dve", "tensor_tensor", out=oT[:, cbase + cc, tsl(tg)], in0=oT[:, cbase + cc, tsl(tg)],
                      in1=tz[:], op=ALU.mult)
        wo = [wsl[0], wsl[1]]
        for hf in range(2):
            S.dma("pool", "d_w%d" % hf, [(wsl[hf][:], w_out_r[:, :, hf * 512:(hf + 1) * 512])])

        gn = 0
        for dg in range(2):
            gw = gws[dg]
            for dcl in range(4):
                dc = dg * 4 + dcl
                cs = slice(dcl * 128, (dcl + 1) * 128)
                for tg in range(4):
                    k = gn % 2
                    gn += 1
                    bPA, bPB, bgA, bgB = (2, 3, 4, 5) if k == 0 else (0, 1, 6, 7)
                    for kc in range(8):
                        MM(pbf[bgA], gw["gA"][:, kc, cs], hT[:, kc, tsl(tg)], kc == 0, kc == 7)
                    for kc in range(8):
                        MM(pbf[bgB], gw["gB"][:, kc, cs], hT[:, kc, tsl(tg)], kc == 0, kc == 7)
                    for ec in range(4):
                        MM(pbf[bPA], gw["pA"][:, ec, cs], oT[:, ec, tsl(tg)], ec == 0, ec == 3)
                    for ec in range(4):
                        MM(pbf[bPB], gw["pB"][:, ec, cs], oT[:, 4 + ec, tsl(tg)], ec == 0, ec == 3)
                    E("act", "activation", out=sA[k][:], in_=pbf[bgA], func=AF.Sigmoid)
                    E("act", "activation", out=sB[k][:], in_=pbf[bgB], func=AF.Sigmoid)
                    E("dve", "tensor_tensor", out=tA[k][:], in0=sA[k][:], in1=pbf[bPA], op=ALU.mult)
                    E("dve", "tensor_tensor", out=sB[k][:], in0=sB[k][:], in1=pbf[bPB], op=ALU.mult)
                    E("dve", "tensor_tensor", out=mT[:, dc, tsl(tg)], in0=tA[k][:], in1=sB[k][:], op=ALU.add)

        okeys = []
        for ti in range(NT):
            tok = slice(ti * 128, (ti + 1) * 128)
            xo = xts2[ti % 2]
            S.dma("sp", "d_xo%d" % (ti % 2), [(xo[:], x[tok, :])])
            for hf in range(2):
                po = pbf[2 * (ti % 2) + hf]
                for dc in range(8):
                    MM(po, mT[:, dc, tok], wo[hf][:, dc, :], dc == 0, dc == 7)
                E("dve", "tensor_tensor", out=xo[:, hf * 512:(hf + 1) * 512], in0=po,
                  in1=xo[:, hf * 512:(hf + 1) * 512], op=ALU.add)
            key = "d_o%d" % (ti % 2)
            if key not in okeys:
                okeys.append(key)
            S.dma("sp", key, [(out[tok, :], xo[:])])
        dkeys.extend(okeys)
        return finish()


def _t5_bucket_np(n):
    n = np.maximum(n, 0)
    nf = np.maximum(n, 1).astype(np.float32)
    large = 16 + (np.log(nf / np.float32(16)) / np.float32(math.log(128 / 16)) * np.float32(16)).astype(np.int32)
    large = np.minimum(large, 31)
    return np.where(n < 16, n, large)


def _host_consts(norm_g, qnorm_a, knorm_a, sinks_a, qnorm_b, knorm_b, rel_bias):
    f = np.float32
    s = np.arange(128)[:, None]
    t = np.arange(128)[None, :]
    d0 = t - s
    d1 = t + 128 - s
    b0 = _t5_bucket_np(d0)
    b1 = _t5_bucket_np(d1)
    ta = rel_bias[:, :8]
    tb = rel_bias[:, 8:]

    def gath(tab):
        a = np.stack([tab[b0], tab[b1]], axis=1)
        return np.ascontiguousarray(a.transpose(0, 1, 3, 2)).reshape(128, -1).astype(f)

    mA = np.stack([(s <= t), (s > t)], axis=1).astype(f).reshape(128, -1)
    tt = np.arange(128)[:, None]
    sx = np.arange(128)[None, :]
    tril = (sx <= tt).astype(f)
    bd = np.zeros((128, 128), f)
    bd[:64, :64] = 1
    bd[64:, 64:] = 1
    return {
        "c_gT": np.ascontiguousarray(norm_g.reshape(8, 128).T).astype(f),
        "c_gq": np.ascontiguousarray(np.stack([np.tile(qnorm_a, 2), np.tile(knorm_a, 2),
                                               np.tile(qnorm_b, 2), np.tile(knorm_b, 2)], axis=1)).astype(f),
        "c_sink": np.ascontiguousarray(np.broadcast_to(sinks_a[None, :], (128, 8))).astype(f),
        "c_cfar": np.ascontiguousarray(np.broadcast_to(tb[31][None, :], (128, 8))).astype(f),
        "c_biasA": gath(ta),
        "c_biasB": gath(tb),
        "c_mA": mA,
        "c_negm": np.where(sx <= tt, 0.0, -BIG).astype(f),
        "c_tril": tril,
        "c_ident": np.eye(128, dtype=f),
        "c_bd": bd,
        "c_pow2": np.ascontiguousarray(np.broadcast_to((0.5 ** np.arange(NIT))[None, :], (128, NIT))).astype(f),
        "c_npow2": np.ascontiguousarray(np.broadcast_to((-0.5 * 0.5 ** np.arange(NIT))[None, :], (128, NIT))).astype(f),
        "c_cthr": np.ascontiguousarray(np.broadcast_to(((np.arange(NT) + 1) * 128 - 511.5)[None, :], (128, NT))).astype(f),
    }


_CACHE = {}


def kernel(x, norm_g, w_in, qnorm_a, knorm_a, sinks_a, qnorm_b, knorm_b, rel_bias, w_proj_a, w_proj_b, w_out):
    a = lambda v: np.ascontiguousarray(np.asarray(v, dtype=np.float32))
    x = a(x)
    consts = _host_consts(a(norm_g)[0], a(qnorm_a)[0], a(knorm_a)[0], a(sinks_a)[0], a(qnorm_b)[0],
                          a(knorm_b)[0], a(rel_bias))
    shared = {"w_in": a(w_in)[0], "w_pa": a(w_proj_a)[0], "w_pb": a(w_proj_b)[0], "w_out": a(w_out)[0]}
    shared.update(consts)
    if "nc" not in _CACHE:
        _CACHE["nc"] = build_program()
    nc = _CACHE["nc"]
    in_maps = [dict(shared, x=x[b]) for b in range(8)]
    res = run_bass_kernel_spmd(nc, in_maps, core_ids=list(range(8)))
    return np.stack([r["out"] for r in res.results], axis=0).astype(np.float32)
```

```python
import math
from contextlib import ExitStack

import numpy as np
import concourse.bass as bass
import concourse.mybir as mybir
from concourse.bass_utils import run_bass_kernel_spmd

F32 = mybir.dt.float32
BF16 = mybir.dt.bfloat16
ALU = mybir.AluOpType
AF = mybir.ActivationFunctionType
AX = mybir.AxisListType

S_LEN = 2048
D = 1024
NT = 16
INW = 5672
NIT = 16
K0 = 10
BIG = 1.0e30
ENGS = ("pe", "dve", "act", "pool", "sp")
_ESZ = {F32: 4, BF16: 2}
PAGE = 2048


def _esize(dt):
    return _ESZ[dt]


def _box(ap):
    t = ap.tensor
    tn = type(t).__name__
    if tn.startswith("DRam"):
        return None
    dims = list(ap.ap)
    off = int(ap.offset)
    es = _esize(ap.dtype)
    pstride = int(dims[0][0])
    npart = int(dims[0][1])
    if pstride <= 0:
        pstride = 1
        for s in list(t.shape)[1:]:
            pstride *= int(s)
    p0 = off // pstride
    f0 = off % pstride
    f1 = f0
    for st, n in dims[1:]:
        f1 += abs(int(st)) * (int(n) - 1)
    if tn.startswith("PSum"):
        base = 1 << 24
        base += int(t.name[2:]) * 2048
    else:
        base = int(t.manual_sbuf_range[0])
    return (p0, p0 + npart, base + f0 * es, base + (f1 + 1) * es)


def _ovl(a, b):
    return a[0] < b[1] and b[0] < a[1] and a[2] < b[3] and b[2] < a[3]


def _contains(a, b):
    return a[0] <= b[0] and a[1] >= b[1] and a[2] <= b[2] and a[3] >= b[3]


class Sched:
    def __init__(self, nc, stack):
        self.nc = nc
        self.stack = stack
        self.q = {e: [] for e in ENGS}
        self.sems = {}
        self.cnt = {}
        self.unit = {}
        for e in ENGS:
            self._mksem(e, 1)
        self.seen = {e: {} for e in ENGS}
        self.pages = {}
        self.n_inst = 0

    def _mksem(self, key, unit):
        self.sems[key] = self.stack.enter_context(self.nc.semaphore("s_" + key))
        self.cnt[key] = 0
        self.unit[key] = unit

    def _pages(self, box):
        return range(box[2] // PAGE, (box[3] - 1) // PAGE + 1)

    def _scan(self, box, want_reads, deps, eng, raw):
        for pg in self._pages(box):
            for r in self.pages.get(pg, ()):
                if (want_reads or r[1] == "w") and _ovl(r[0], box):
                    k, i = r[2]
                    if k == eng and eng == "pe":
                        continue
                    if deps.get(k, 0) < i:
                        deps[k] = i

    def _record(self, prod, rboxes, wboxes):
        for box in wboxes:
            for pg in self._pages(box):
                lst = self.pages.setdefault(pg, [])
                lst[:] = [r for r in lst if not _contains(box, r[0])]
                lst.append((box, "w", prod))
        for box in rboxes:
            for pg in self._pages(box):
                lst = self.pages.setdefault(pg, [])
                lst[:] = [r for r in lst if not (r[1] == "r" and r[2][0] == prod[0] and _contains(box, r[0]))]
                lst.append((box, "r", prod))

    def _emit_waits(self, eng, deps):
        seen = self.seen[eng]
        for k, i in deps.items():
            if k == eng and eng == "pe":
                continue
            if seen.get(k, 0) >= i:
                continue
            seen[k] = i
            self.q[eng].append(("wait", k, i * self.unit[k]))

    def op(self, eng, fn, reads=(), writes=(), inc=True):
        rb = [b for b in (_box(a) for a in reads) if b is not None]
        wb = [b for b in (_box(a) for a in writes) if b is not None]
        deps = {}
        for b in rb:
            self._scan(b, False, deps, eng, True)
        for b in wb:
            self._scan(b, True, deps, eng, False)
        self._emit_waits(eng, deps)
        idx = self.cnt[eng] + 1
        if inc:
            self.cnt[eng] = idx
        self.q[eng].append(("inst", fn, inc, eng))
        self._record((eng, idx), rb, wb)
        self.n_inst += 1

    def dma(self, qeng, semkey, pairs, **kw):
        if semkey not in self.sems:
            self._mksem(semkey, 16)
        rb = [b for b in (_box(p[1]) for p in pairs) if b is not None]
        wb = [b for b in (_box(p[0]) for p in pairs) if b is not None]
        deps = {}
        for b in rb:
            self._scan(b, False, deps, "__dma__", False)
        for b in wb:
            self._scan(b, True, deps, "__dma__", False)
        self._emit_waits(qeng, deps)
        idx = self.cnt[semkey] + len(pairs)
        self.cnt[semkey] = idx
        for (o, i) in pairs:
            self.q[qeng].append(("dma", o, i, semkey, kw))
        self._record((semkey, idx), rb, wb)
        self.n_inst += len(pairs)

    def wait_all(self, eng, keys):
        for k in keys:
            if self.cnt[k] > self.seen[eng].get(k, 0):
                self.seen[eng][k] = self.cnt[k]
                self.q[eng].append(("wait", k, self.cnt[k] * self.unit[k]))

    def emit(self):
        nc = self.nc
        sems = self.sems
        q = self.q

        def run(engobj, items):
            for it in items:
                if it[0] == "wait":
                    engobj.wait_ge(sems[it[1]], it[2])
                elif it[0] == "inst":
                    ins = it[1](engobj)
                    if it[2]:
                        ins.then_inc(sems[it[3]], 1)
                else:
                    _, o, i, sk, kw = it
                    engobj.dma_start(out=o, in_=i, **kw).then_inc(sems[sk], 16)

        with nc.Block() as block:
            @block.tensor
            def _(e):
                run(e, q["pe"])

            @block.vector
            def _(e):
                run(e, q["dve"])

            @block.scalar
            def _(e):
                run(e, q["act"])

            @block.gpsimd
            def _(e):
                run(e, q["pool"])

            @block.sync
            def _(e):
                run(e, q["sp"])


class Arena:
    def __init__(self, nc, base, size, tag):
        self.nc, self.base, self.size, self.tag, self.off = nc, base, size, tag, 0

    def alloc(self, name, shape, dt):
        n = 1
        for s in shape[1:]:
            n *= s
        nb = n * _esize(dt)
        nb = (nb + 31) // 32 * 32
        assert self.off + nb <= self.size, (self.tag, name, self.off, nb, self.size)
        t = self.nc.alloc_sbuf_tensor_at(name, list(shape), dt, offset=self.base + self.off)
        self.off += nb
        return t


def build_program(stop=99, dbg=None):
    nc = bass.Bass("TRN2", target_bir_lowering=False)
    dr = lambda name, shape, kind="ExternalInput": nc.dram_tensor(name, shape, F32, kind=kind).ap()
    x = dr("x", [S_LEN, D])
    w_in = dr("w_in", [D, INW])
    w_pa = dr("w_pa", [512, D])
    w_pb = dr("w_pb", [512, D])
    w_out = dr("w_out", [D, D])
    c_gT = dr("c_gT", [128, 8])
    c_gq = dr("c_gq", [128, 4])
    c_sink = dr("c_sink", [128, 8])
    c_cfar = dr("c_cfar", [128, 8])
    c_biasA = dr("c_biasA", [128, 2 * 8 * 128])
    c_biasB = dr("c_biasB", [128, 2 * 8 * 128])
    c_mA = dr("c_mA", [128, 2 * 128])
    c_negm = dr("c_negm", [128, 128])
    c_tril = dr("c_tril", [128, 128])
    c_ident = dr("c_ident", [128, 128])
    c_bd = dr("c_bd", [128, 128])
    c_pow2 = dr("c_pow2", [128, NIT])
    c_npow2 = dr("c_npow2", [128, NIT])
    c_cthr = dr("c_cthr", [128, NT])
    out = dr("out", [S_LEN, D], kind="ExternalOutput")

    with ExitStack() as st:
        S = Sched(nc, st)

        def E(eng, meth, inc=True, **kw):
            reads, writes = [], []
            for k, v in kw.items():
                if hasattr(v, "tensor") and hasattr(v, "ap"):
                    (writes if k in ("out", "accum_out", "ap") else reads).append(v)
            S.op(eng, lambda e: getattr(e, meth)(**kw), reads=reads, writes=writes, inc=inc)

        def MM(outp, lhsT, rhs, start, stop, inc=None, **kw):
            if inc is None:
                inc = stop
            S.op("pe", lambda e: e.matmul(outp, lhsT=lhsT, rhs=rhs, start=start, stop=stop, **kw),
                 reads=[lhsT, rhs], writes=[outp], inc=inc)

        def TR(outp, in_, ident, inc=True):
            S.op("pe", lambda e: e.transpose(out=outp, in_=in_, identity=ident),
                 reads=[in_, ident], writes=[outp], inc=inc)

        dkeys = []

        def dump(name, sb_ap, shape, dt):
            if dbg is None:
                return
            d = nc.dram_tensor(name, list(shape), dt, kind="ExternalOutput").ap()
            S.dma("sp", "d_dbg", [(d, sb_ap)])
            dbg.append(name)
            if "d_dbg" not in dkeys:
                dkeys.append("d_dbg")

        def finish():
            S.wait_all("sp", dkeys)
            S.emit()
            return nc

        base0 = (int(nc.sbuf_base) + 63) // 64 * 64
        top = int(nc.sbuf_top)
        cur = [base0]

        def region(size, tag):
            a = Arena(nc, cur[0], size, tag)
            cur[0] += size
            assert cur[0] <= top, (tag, cur[0], top)
            return a

        R_CONST = region(12288, "const")
        R_HT = region(32768, "hT")
        R_W = region(24576, "w")
        R_OT = region(32768, "oT")
        R_QKV = region(90432, "qkv")
        R_SP = region((top - cur[0]) // 64 * 64, "spare")

        pb0 = nc.alloc_psum_tensor("pb0", [128, 512], F32)
        pb1 = nc.alloc_psum_tensor("pb1", [128, 512], F32)
        pb2 = nc.alloc_psum_tensor("pb2", [128, 1024], F32)
        pb4 = nc.alloc_psum_tensor("pb4", [128, 1024], F32)
        pb6 = nc.alloc_psum_tensor("pb6", [128, 512], F32)
        pb7 = nc.alloc_psum_tensor("pb7", [128, 512], F32)
        pbf = [pb0.ap(), pb1.ap(), pb2.ap()[:, 0:512], pb2.ap()[:, 512:1024], pb4.ap()[:, 0:512],
               pb4.ap()[:, 512:1024], pb6.ap(), pb7.ap()]
        _h2, _h4 = pb2.bitcast(BF16).ap(), pb4.bitcast(BF16).ap()
        pbh = [pb0.bitcast(BF16).ap(), pb1.bitcast(BF16).ap(), _h2[:, 0:1024], _h2[:, 1024:2048], _h4[:, 0:1024],
               _h4[:, 1024:2048], pb6.bitcast(BF16).ap(), pb7.bitcast(BF16).ap()]
        sc2w = [pb2.ap().rearrange("p (f a b) -> p f a b", f=2, a=4), pb4.ap().rearrange("p (f a b) -> p f a b", f=2, a=4)]

        identb = R_CONST.alloc("identb", [128, 128], BF16)
        bdb = R_CONST.alloc("bdb", [128, 128], BF16)
        EbA = R_CONST.alloc("EbA", [128, 2, 8, 128], BF16)
        EbB = R_CONST.alloc("EbB", [128, 2, 8, 128], BF16)
        negm = R_CONST.alloc("negm", [128, 128], F32)
        trilb = R_CONST.alloc("trilb", [128, 128], BF16)
        gT = R_CONST.alloc("gT", [128, 8], F32)
        gq = R_CONST.alloc("gq", [128, 4], F32)
        esink = R_CONST.alloc("esink", [128, 8], F32)
        cfar = R_CONST.alloc("cfar", [128, 8], F32)
        epsT = R_CONST.alloc("epsT", [128, 1], F32)
        pow2 = R_CONST.alloc("pow2", [128, NIT], F32)
        npow2 = R_CONST.alloc("npow2", [128, NIT], F32)
        cthr = R_CONST.alloc("cthr", [128, NT], F32)
        sgn = R_CONST.alloc("sgn", [128, NT, 8], F32)
        wab = R_CONST.alloc("wab", [128, NT, 8], F32)
        ss = R_CONST.alloc("ss", [128, NT], F32)
        sd = R_CONST.alloc("sd", [128, NT], F32)
        rstd = R_CONST.alloc("rstd", [128, NT], F32)
        smalls = R_CONST.alloc("smalls", [128, 96], F32)
        Rrs = [smalls[:, 0:1], smalls[:, 7:8]]
        mid = smalls[:, 1:2]
        cntv = smalls[:, 2:3]
        dirv = smalls[:, 3:4]
        den = smalls[:, 8:16]
        Rk = smalls[:, 16:16 + NIT]
        negRks = [smalls[:, 32:32 + NIT], smalls[:, 48:48 + NIT]]
        csum = smalls[:, 4:5]
        dirS = smalls[:, 5:6]
        nmids = [smalls[:, 6:7], smalls[:, 64:65]]
        mids = [smalls[:, 1:2], smalls[:, 65:66]]
        cntvs = [smalls[:, 2:3], smalls[:, 66:67]]
        dirvs = [smalls[:, 3:4], smalls[:, 67:68]]
        csums = [smalls[:, 4:5], smalls[:, 68:69]]
        dirSs = [smalls[:, 5:6], smalls[:, 69:70]]
        Rks = [smalls[:, 16:16 + NIT], smalls[:, 70:70 + NIT]]

        hT = R_HT.alloc("hT", [128, 8, S_LEN], BF16)
        wsl = [R_W.alloc("wsl%d" % i, [128, 8, 512], BF16) for i in range(3)]
        oT = R_OT.alloc("oT", [128, 8, S_LEN], BF16)

        qaT = R_QKV.alloc("qaT", [128, 4, S_LEN], BF16)
        kaT2 = R_QKV.alloc("kaT2", [128, 2, S_LEN], BF16)
        vA = R_QKV.alloc("vA", [128, NT, 2, 65], BF16)
        qbT = R_QKV.alloc("qbT", [128, 4, S_LEN], BF16)
        kbT = R_QKV.alloc("kbT", [128, 4, S_LEN], BF16)
        vB = R_QKV.alloc("vB", [128, NT, 8, 65], BF16)
        q2T = R_QKV.alloc("q2T", [128, 2, S_LEN], BF16)
        kiT = R_QKV.alloc("kiT", [128, S_LEN], BF16)

        A1 = Arena(nc, R_OT.base, R_OT.size, "p01")
        xts = [A1.alloc("xt%d" % i, [128, D], F32) for i in range(2)]
        hns = [A1.alloc("hn%d" % i, [128, D], BF16) for i in range(2)]
        sqb = [A1.alloc("sqb%d" % i, [128, 512], BF16) for i in range(2)]
        sdb = [A1.alloc("sdb%d" % i, [128, 512], F32) for i in range(2)]
        q2b = [A1.alloc("q2b%d" % i, [128, 256], BF16) for i in range(2)]
        stgA = A1.alloc("stgA", [128, 2, 8, 128], F32)
        _sb = int(stgA.manual_sbuf_range[0])
        xts += [nc.alloc_sbuf_tensor_at("xt%d" % (2 + k), [128, D], F32, offset=_sb + 4096 * k) for k in range(2)]
        stgm = A1.alloc("stgm", [128, 2, 128], F32)
        stgi = A1.alloc("stgi", [128, 128], F32)
        stgd = A1.alloc("stgd", [128, 128], F32)
        stgt = A1.alloc("stgt", [128, 128], F32)

        xpre = set()
        for _i in range(2):
            S.dma("sp", "d_x%d" % _i, [(xts[_i][:], x[_i * 128:(_i + 1) * 128, :])])
            xpre.add(_i)
        S.dma("sp", "d_c", [
            (gT[:], c_gT[:, :]), (gq[:], c_gq[:, :]), (esink[:], c_sink[:, :]), (cfar[:], c_cfar[:, :]),
            (negm[:], c_negm[:, :]), (pow2[:], c_pow2[:, :]), (npow2[:], c_npow2[:, :]), (cthr[:], c_cthr[:, :]),
            (stgm[:].rearrange("p a b -> p (a b)"), c_mA[:, :]),
            (stgi[:], c_ident[:, :]), (stgd[:], c_bd[:, :]), (stgt[:], c_tril[:, :]),
        ])
        E("dve", "memset", ap=epsT[:], constant=1e-6)
        E("dve", "tensor_copy", out=identb[:], in_=stgi[:])
        E("dve", "tensor_copy", out=bdb[:], in_=stgd[:])
        E("dve", "tensor_copy", out=trilb[:], in_=stgt[:])
        E("pool", "memset", ap=vA[:, :, :, 64:65], constant=1.0)
        E("pool", "memset", ap=vB[:, :, :, 64:65], constant=1.0)
        E("dve", "tensor_scalar", out=gq[:, 0:1], in0=gq[:, 0:1], scalar1=0.125, scalar2=None, op0=ALU.mult)
        E("dve", "tensor_scalar", out=gq[:, 2:3], in0=gq[:, 2:3], scalar1=0.125, scalar2=None, op0=ALU.mult)
        E("act", "activation", out=esink[:], in_=esink[:], func=AF.Exp)
        S.dma("sp", "d_c2", [(stgA[:].rearrange("p a h t -> p (a h t)"), c_biasA[:, :])])
        E("act", "activation", out=stgA[:], in_=stgA[:], func=AF.Exp)
        for jt in range(2):
            E("dve", "tensor_tensor", out=EbA[:, jt, :, :], in0=stgA[:, jt, :, :],
              in1=stgm[:, jt, :].unsqueeze(1).broadcast_to([128, 8, 128]), op=ALU.mult)
        stgB = nc.alloc_sbuf_tensor_at("stgB", [128, 2, 8, 128], F32, offset=R_SP.base)
        S.dma("sp", "d_c3", [(stgB[:].rearrange("p a h t -> p (a h t)"), c_biasB[:, :])])
        for jt in range(2):
            E("dve", "tensor_tensor", out=stgB[:, jt, :, :], in0=stgB[:, jt, :, :],
              in1=cfar[:, :].unsqueeze(2).broadcast_to([128, 8, 128]), op=ALU.subtract)
        for jt in range(2):
            E("act", "activation", out=EbB[:, 1 - jt, :, :], in_=stgB[:, jt, :, :], func=AF.Exp)

        w_in_r = w_in.rearrange("(kc p) c -> p kc c", p=128)
        wstate = {"n": 0}

        def load_w(pieces):
            s = wstate["n"] % 3
            wstate["n"] += 1
            t = wsl[s]
            S.dma("pool", "d_w%d" % s,
                  [(t[:, :, d0:d0 + n], w_in_r[:, :, s0:s0 + n]) for (d0, n, s0) in pieces])
            return t

        def p0A(i):
            xt = xts[i % 4]
            hn = hns[i % 2]
            if i not in xpre:
                S.dma("sp", "d_x%d" % (i % 4), [(xt[:], x[i * 128:(i + 1) * 128, :])])
            E("dve", "scalar_tensor_tensor", out=hn[:], in0=xt[:], scalar=1.0, in1=xt[:],
              op0=ALU.mult, op1=ALU.mult, accum_out=ss[:, i:i + 1])
            E("act", "activation", out=sd[:, i:i + 1], in_=ss[:, i:i + 1], func=AF.Ln,
              scale=1.0 / D, bias=epsT[:])
            E("act", "activation", out=rstd[:, i:i + 1], in_=sd[:, i:i + 1], func=AF.Exp, scale=-0.5)
            E("act", "activation", out=hn[:], in_=xt[:], func=AF.Copy, scale=rstd[:, i:i + 1])
            ptr = pbh[i % 2][:, 0:1024].rearrange("p (a b) -> p a b", a=8)
            for kc in range(8):
                TR(ptr[:, kc, :], hn[:, kc * 128:(kc + 1) * 128], identb[:], inc=(kc == 7))

        def p0B(i):
            ptr = pbh[i % 2][:, 0:1024].rearrange("p (a b) -> p a b", a=8)
            E("dve", "tensor_tensor", out=hT[:, :, i * 128:(i + 1) * 128], in0=ptr,
              in1=gT[:, :].unsqueeze(2).broadcast_to([128, 8, 128]), op=ALU.mult)

        p0s = {"a": 0, "b": 0}

        def p0_adv():
            if p0s["a"] < NT:
                p0A(p0s["a"])
                p0s["a"] += 1
            if p0s["b"] < p0s["a"] - 1 or (p0s["a"] == NT and p0s["b"] < NT):
                p0B(p0s["b"])
                p0s["b"] += 1

        def p0_need(ntiles):
            while p0s["b"] < ntiles:
                if p0s["a"] < NT and p0s["a"] <= p0s["b"] + 1:
                    p0A(p0s["a"])
                    p0s["a"] += 1
                else:
                    p0B(p0s["b"])
                    p0s["b"] += 1

        def tsl(tg):
            return slice(tg * 512, (tg + 1) * 512)

        fmn = [0]
        pend = [None]

        def fm_mm(ws, ccol, tg):
            n = fmn[0]
            fmn[0] += 1
            acc = pbf[(2, 3, 6, 7)[n % 4]]
            for kc in range(8):
                MM(acc, ws[:, kc, ccol:ccol + 128], hT[:, kc, tsl(tg)], kc == 0, kc == 7)
            return n

        def fm_post(n, dst, gain, norm):
            acc = pbf[(2, 3, 6, 7)[n % 4]]
            if norm:
                sq = sqb[n % 2]
                E("act", "activation", out=sq[:], in_=acc, func=AF.Square)
                ssb = pbf[4 + n % 2]
                MM(ssb, bdb[:], sq[:], True, True)
                sdt = sdb[n % 2]
                E("act", "activation", out=sdt[:], in_=ssb, func=AF.Ln, scale=1.0 / 64, bias=epsT[:])
                E("act", "activation", out=sdt[:], in_=sdt[:], func=AF.Exp, scale=-0.5)
                E("dve", "scalar_tensor_tensor", out=dst, in0=acc, scalar=gain, in1=sdt[:],
                  op0=ALU.mult, op1=ALU.mult)
            else:
                E("act", "activation", out=dst, in_=acc, func=AF.Copy)

        pendq = []

        def fm(ws, ccol, dst, gain, norm, tg):
            n = fm_mm(ws, ccol, tg)
            if len(pendq) == 2:
                fm_post(*pendq.pop(0))
            pendq.append((n, dst, gain, norm))

        def fm_flush():
            while pendq:
                fm_post(*pendq.pop(0))

        ws = load_w([(0, 512, 0)])
        p0_need(NT)
        ws_qb = load_w([(0, 512, 1280)])
        ws_kb = load_w([(0, 512, 1792)])
        for tg in range(4):
            for c in range(4):
                fm(ws, c * 128, qaT[:, c, tsl(tg)], gq[:, 0:1], True, tg)
        ws_k = load_w([(0, 64, 512), (64, 64, 512), (128, 64, 576), (192, 64, 576),
                       (256, 32, 3584), (288, 32, 3584), (320, 32, 3584), (352, 32, 3584)])
        for c in range(4):
            for tg in range(4):
                fm(ws_qb, c * 128, qbT[:, c, tsl(tg)], gq[:, 2:3], True, tg)
        ws_vb = load_w([(0, 512, 2304)])
        for c in range(4):
            for tg in range(4):
                fm(ws_kb, c * 128, kbT[:, c, tsl(tg)], gq[:, 3:4], True, tg)
        ws_g5 = load_w([(0, 128, 640), (128, 256, 3328), (384, 8, 3616)])
        for c in range(2):
            for tg in range(4):
                fm(ws_k, c * 128, kaT2[:, c, tsl(tg)], gq[:, 1:2], True, tg)
        for tg in range(4):
            fm(ws_k, 256, kiT[:, tsl(tg)], None, False, tg)
        fm_flush()

        def tm_mm(ti):
            accv = pbf[2 + 2 * (ti % 2)]
            accq = pbf[3 + 2 * (ti % 2)]
            tok = slice(ti * 128, (ti + 1) * 128)
            for kc in range(8):
                MM(accv, hT[:, kc, tok], ws_vb[:, kc, 0:512], kc == 0, kc == 7)
            for kc in range(8):
                MM(accq[:, 0:392], hT[:, kc, tok], ws_g5[:, kc, 0:392], kc == 0, kc == 7)

        def tm_post(ti):
            accv = pbf[2 + 2 * (ti % 2)]
            accq = pbf[3 + 2 * (ti % 2)]
            tok = slice(ti * 128, (ti + 1) * 128)
            E("act", "activation", out=vB[:, ti, :, 0:64], in_=accv.rearrange("p (h d) -> p h d", h=8), func=AF.Copy)
            E("act", "activation", out=vA[:, ti, :, 0:64],
              in_=accq[:, 0:128].rearrange("p (h d) -> p h d", h=2), func=AF.Copy)
            E("act", "activation", out=sgn[:, ti, :], in_=accq[:, 384:392], func=AF.Sign)
            E("dve", "scalar_tensor_tensor", out=wab[:, ti, :], in0=accq[:, 384:392], scalar=0.0625,
              in1=sgn[:, ti, :], op0=ALU.mult, op1=ALU.mult)
            qb2 = q2b[ti % 2]
            E("dve", "tensor_tensor", out=qb2[:].rearrange("p (h e) -> p h e", h=8),
              in0=accq[:, 128:384].rearrange("p (h e) -> p h e", h=8),
              in1=wab[:, ti, :].unsqueeze(2).broadcast_to([128, 8, 32]), op=ALU.mult)
            ptr = pbh[ti % 2][:, 0:256].rearrange("p (a b) -> p a b", a=2)
            for g in range(2):
                TR(ptr[:, g, :], qb2[:, g * 128:(g + 1) * 128], identb[:], inc=(g == 1))
            E("act", "activation", out=q2T[:, :, tok], in_=ptr, func=AF.Copy)

        tm_mm(0)
        for ti in range(NT):
            if ti + 1 < NT:
                tm_mm(ti + 1)
            tm_post(ti)

        dump("d_hT", hT[:], [128, 8, S_LEN], BF16)
        if stop == 0:
            return finish()
        for nm, t in (("d_qaT", qaT), ("d_kaT2", kaT2), ("d_vA", vA), ("d_qbT", qbT), ("d_kbT", kbT), ("d_vB", vB),
                      ("d_q2T", q2T), ("d_kiT", kiT), ("d_sgn", sgn)):
            dump(nm, t[:], [int(v) for v in t.shape], t.dtype)
        if stop == 1:
            return finish()
        import os
        _nt2 = int(os.environ.get("DBG_NT", NT))
        A2 = Arena(nc, R_W.base, R_W.size, "p2a")
        A2b = Arena(nc, R_SP.base, R_SP.size, "p2b")
        scoresb = [A2.alloc("scores%d" % k, [128, S_LEN], F32) for k in range(2)]
        masktb = [A2.alloc("maskt%d" % k, [128, S_LEN], BF16) for k in range(2)]
        maskTs = [A2b.alloc("maskT%d" % k, [128, NT, 128], BF16) for k in range(2)]
        exBp = [A2b.alloc("exBp%d" % i, [128, 2, 4, 128], BF16) for i in range(2)]
        exB = [exBp[i // 2][:, i % 2, :, :] for i in range(4)]
        PTBp = [A2b.alloc("PTBp%d" % i, [128, 2, 4, 128], BF16) for i in range(2)]
        PTB = [PTBp[i // 2][:, i % 2, :, :] for i in range(4)]
        ob = A2b.alloc("ob", [128, D], BF16)

        oaccv = [pbf[6][:, 0:260].rearrange("p (h d) -> p h d", h=4) for k in range(2)]
        sacc = pbf[7]
        scb2 = [[pbf[2 + 2 * s_ + k].rearrange("p (a b) -> p a b", a=4) for k in range(2)] for s_ in range(2)]
        scb = scb2[1]
        mtr = pbh[1][:, 0:1024].rearrange("p (a b) -> p a b", a=8)
        otr = pbh[2][:, 0:1024].rearrange("p (a b) -> p a b", a=8)
        ctr = {"g": 0}

        def gen_S1(i):
            qs = slice(i * 128, (i + 1) * 128)
            nk = (i + 1) * 128
            par = i % 2
            scores = scoresb[par]
            maskt = masktb[par]
            junk = maskt
            maskT = maskTs[par]
            if i >= 2:
                Rb = [maskt[:, 0:512], maskt[:, 512:1024]]
                Dh = maskt[:, 1024:2048].rearrange("p (h c) -> p h c", h=8)
                E("dve", "tensor_tensor", out=Dh, in0=identb[:].unsqueeze(1).broadcast_to([128, 8, 128]),
                  in1=sgn[:, i, :].unsqueeze(2).broadcast_to([128, 8, 128]), op=ALU.mult)
                work = [(ch, h) for ch in range((nk + 511) // 512) for h in range(8)]

                def idx_front(ch, h):
                    cw = min(512, nk - ch * 512)
                    csl = slice(ch * 512, ch * 512 + cw)
                    g, r = h // 4, h % 4
                    ip = pbf[h % 2]
                    MM(ip[:, 0:cw], q2T[32 * r:32 * r + 32, g, qs], kiT[32 * r:32 * r + 32, csl], True, True,
                       tile_position=(32 * r, 0))
                    E("act", "activation", out=Rb[h % 2][:, 0:cw], in_=ip[:, 0:cw], func=AF.Relu)

                def idx_back(ch, h):
                    cw = min(512, nk - ch * 512)
                    csl = slice(ch * 512, ch * 512 + cw)
                    MM(sacc[:, 0:cw], Dh[:, h, :], Rb[h % 2][:, 0:cw], h == 0, h == 7)
                    if h == 7:
                        E("dve", "tensor_copy", out=scores[:, csl], in_=sacc[:, 0:cw])

                idx_front(*work[0])
                for n_, wk in enumerate(work):
                    if n_ + 1 < len(work):
                        idx_front(*work[n_ + 1])
                    idx_back(*wk)
                    yield
                idx_done[i] = True
                Rr = Rrs[par]
                mid, cntv, dirv, csum, dirS, Rk = mids[par], cntvs[par], dirvs[par], csums[par], dirSs[par], Rks[par]
                E("dve", "tensor_reduce", out=Rr, in_=scores[:, 0:nk], axis=AX.X, op=ALU.max, apply_absolute_value=True)
                E("dve", "tensor_tensor", out=scores[:, i * 128:nk], in0=scores[:, i * 128:nk], in1=negm[:], op=ALU.add)
                E("dve", "tensor_scalar", out=Rk, in0=pow2[:], scalar1=Rr, scalar2=None, op0=ALU.mult)
                E("dve", "memset", ap=mid, constant=0.0)
                E("act", "activation", out=negRks[par], in_=npow2[:], func=AF.Copy, scale=Rr)
                yield
                for k in range(K0):
                    E("dve", "tensor_scalar", out=junk[:, 0:nk], in0=scores[:, 0:nk], scalar1=mid, scalar2=None,
                      op0=ALU.is_ge, op1=ALU.add, accum_out=cntv)
                    yield
                    E("dve", "tensor_scalar", out=dirv, in0=cntv, scalar1=255.5, scalar2=0.5,
                      op0=ALU.is_ge, op1=ALU.subtract)
                    yield
                    E("dve", "scalar_tensor_tensor", out=mid, in0=dirv, scalar=Rk[:, k:k + 1], in1=mid,
                      op0=ALU.mult, op1=ALU.add)
                    yield
                nmid = nmids[par]
                E("act", "activation", out=nmid, in_=mid, func=AF.Copy, scale=-1.0)
                for k in range(K0, NIT):
                    E("act", "activation", out=junk[:, 0:nk], in_=scores[:, 0:nk], func=AF.Sign, bias=nmid,
                      accum_out=csum)
                    yield
                    E("act", "activation", out=dirS, in_=csum, func=AF.Sign, bias=cthr[:, i:i + 1])
                    yield
                    E("act", "activation", out=nmid, in_=dirS, func=AF.Identity, scale=negRks[par][:, k:k + 1],
                      bias=nmid)
                    yield
                E("dve", "tensor_scalar", out=maskt[:, 0:nk], in0=scores[:, 0:nk], scalar1=nmid, scalar2=0.0,
                  op0=ALU.add, op1=ALU.is_ge)
            elif i == 0:
                E("dve", "tensor_copy", out=maskt[:, 0:128], in_=trilb[:])
            else:
                E("dve", "memset", ap=maskt[:, 0:128], constant=1.0)
                E("dve", "tensor_copy", out=maskt[:, 128:256], in_=trilb[:])
            yield
            for j0 in range(0, i + 1, 8):
                njs = min(i + 1, j0 + 8) - j0
                for jj in range(njs):
                    j = j0 + jj
                    TR(mtr[:, jj, :], maskt[:, j * 128:(j + 1) * 128], identb[:], inc=(jj == njs - 1))
                E("act", "activation", out=maskT[:, j0:j0 + njs, :], in_=mtr[:, 0:njs, :], func=AF.Copy)
                yield

        def n_S1(i):
            nk = (i + 1) * 128
            n = 1 + (i // 8 + 1)
            if i >= 2:
                n += 8 * ((nk + 511) // 512) + 1 + 3 * NIT
            return n

        def gen_S2(i):
            qs = slice(i * 128, (i + 1) * 128)
            maskT = maskTs[i % 2]
            oa = oaccv[0]
            jts = [(0, i)] + ([(1, i - 1)] if i >= 1 else [])
            njt = len(jts)
            nfar = max(0, i - 1)
            groups = [("far", list(range(j0, min(nfar, j0 + 4)))) for j0 in range(0, nfar, 4)]
            groups.append(("near", [j for j in (i - 1, i) if j >= 0]))
            items = [("A", hk, None) for hk in range(2)] + [(c, kind, js) for c in range(4) for (kind, js) in groups]

            def front(item, gi):
                c, kind, js = item
                if c == "A":
                    hk = kind
                    for jn, (jt, j) in enumerate(jts):
                        for hh in range(4):
                            hq = 4 * hk + hh
                            cq, hf = hq // 2, hq % 2
                            MM(scb2[gi][hf][:, jn * 2 + hh // 2, :],
                               kaT2[64 * hf:64 * hf + 64, hk, j * 128:(j + 1) * 128],
                               qaT[64 * hf:64 * hf + 64, cq, qs], True, True,
                               inc=(jn == njt - 1 and hh == 3))
                else:
                    for jj, j in enumerate(js):
                        for hf in range(2):
                            MM(scb2[gi][hf][:, jj, :], kbT[64 * hf:64 * hf + 64, c, j * 128:(j + 1) * 128],
                               qbT[64 * hf:64 * hf + 64, c, qs], True, True,
                               inc=(jj == len(js) - 1 and hf == 1))

            def mid(item, gi, hf):
                c, kind, js = item
                ex = exB[2 * gi + hf]
                pt = PTB[2 * gi + hf]
                if c == "A":
                    hk = kind
                    if hf == 0:
                        E("act", "activation", out=exBp[gi][:, :, 0:2 * njt, :], in_=sc2w[gi][:, :, 0:2 * njt, :],
                          func=AF.Exp)
                    if hf == 0:
                        for jn, (jt, j) in enumerate(jts):
                            E("dve", "tensor_tensor", out=PTBp[gi][:, :, 2 * jn:2 * jn + 2, :],
                              in0=exBp[gi][:, :, 2 * jn:2 * jn + 2, :],
                              in1=EbA[:, jt, 4 * hk:4 * hk + 4, :].rearrange("p (hh f) c -> p f hh c", f=2),
                              op=ALU.mult)
                else:
                    n = len(js)
                    h = 2 * c + hf
                    if hf == 0:
                        E("act", "activation", out=exBp[gi][:, :, 0:n, :], in_=sc2w[gi][:, :, 0:n, :], func=AF.Exp)
                        E("dve", "tensor_tensor", out=PTBp[gi][:, :, 0:n, :], in0=exBp[gi][:, :, 0:n, :],
                          in1=maskT[:, js[0]:js[0] + n, :].unsqueeze(1).broadcast_to([128, 2, n, 128]), op=ALU.mult)
                        if kind == "near":
                            E("dve", "tensor_tensor", out=PTBp[gi][:, :, 0:n, :], in0=PTBp[gi][:, :, 0:n, :],
                              in1=EbB[:, 2 - n:2, 2 * c:2 * c + 2, :].rearrange("p a b c -> p b a c"), op=ALU.mult)

            def back(item, gi):
                c, kind, js = item
                if c == "A":
                    hk = kind
                    for hh in range(4):
                        pt = PTB[2 * gi + hh % 2]
                        for jn, (jt, j) in enumerate(jts):
                            MM(oa[:, hh, :], pt[:, 2 * jn + hh // 2, :], vA[:, j, hk, :], jn == 0, jn == njt - 1,
                               inc=(hh == 3 and jn == njt - 1))
                    E("dve", "tensor_tensor", out=den[:, 0:4], in0=oa[:, :, 64], in1=esink[:, 4 * hk:4 * hk + 4],
                      op=ALU.add)
                    E("dve", "reciprocal", out=den[:, 0:4], in_=den[:, 0:4])
                    E("dve", "tensor_tensor", out=ob[:, hk * 256:(hk + 1) * 256].rearrange("p (h d) -> p h d", h=4),
                      in0=oa[:, :, 0:64], in1=den[:, 0:4].unsqueeze(2).broadcast_to([128, 4, 64]), op=ALU.mult)
                else:
                    n = len(js)
                    k4 = c // 2
                    for hf in range(2):
                        h = 2 * c + hf
                        pt = PTB[2 * gi + hf]
                        for jj, j in enumerate(js):
                            first = (c % 2 == 0 and hf == 0 and j == 0)
                            last = (c % 2 == 1 and hf == 1 and j == i)
                            MM(oa[:, h % 4, :], pt[:, jj, :], vB[:, j, h, :], first, last,
                               inc=(hf == 1 and jj == n - 1))
                    if c % 2 == 1 and kind == "near":
                        E("dve", "reciprocal", out=den[:, 4:8], in_=oa[:, :, 64])
                        E("dve", "tensor_tensor",
                          out=ob[:, 512 + k4 * 256:512 + (k4 + 1) * 256].rearrange("p (h d) -> p h d", h=4),
                          in0=oa[:, :, 0:64], in1=den[:, 4:8].unsqueeze(2).broadcast_to([128, 4, 64]), op=ALU.mult)

            gi0 = ctr["g"] % 2
            ctr["g"] += len(items)
            front(items[0], gi0)
            yield
            for k, item in enumerate(items):
                gi = (gi0 + k) % 2
                for hf in range(2):
                    mid(item, gi, hf)
                    yield
                if k + 1 < len(items):
                    front(items[k + 1], 1 - gi)
                    yield
                back(item, gi)
                yield
            for c in range(8):
                TR(otr[:, c, :], ob[:, c * 128:(c + 1) * 128], identb[:], inc=(c == 7))
            E("act", "activation", out=oT[:, :, qs], in_=otr, func=AF.Copy)
            yield

        def n_S2(i):
            nfar = max(0, i - 1)
            return 2 + 4 * (2 + 4 * ((nfar + 3) // 4 + 1))

        w_pa_r = w_pa.rearrange("(ec p) d -> p ec d", p=128)
        w_pb_r = w_pb.rearrange("(ec p) d -> p ec d", p=128)
        w_out_r = w_out.rearrange("(dc p) e -> p dc e", p=128)
        WgB0 = nc.alloc_sbuf_tensor_at("WgB0", [128, 8, 512], BF16, offset=int(q2T.manual_sbuf_range[0]))
        WpA0 = nc.alloc_sbuf_tensor_at("WpA0", [128, 4, 512], BF16, offset=int(kiT.manual_sbuf_range[0]))
        pf = {"z0": False, "r": False}

        def pf_z0():
            if not pf["z0"]:
                pf["z0"] = True
                S.dma("pool", "d_w0", [(wsl[0][:], w_in_r[:, :, 768:1280])])

        def pf_rest():
            if not pf["r"]:
                pf["r"] = True
                S.dma("pool", "d_w1", [(wsl[1][:], w_in_r[:, :, 2816:3328])])
                S.dma("pool", "d_g0", [
                    (wsl[2][:], w_in_r[:, :, 3624:3624 + 512]),
                    (WgB0[:], w_in_r[:, :, 4648:4648 + 512]),
                    (WpA0[:], w_pa_r[:, :, 0:512]),
                ])

        live = {}
        idx_done = {0: True, 1: True}

        def start(j):
            if j < _nt2 and j not in live:
                live[j] = [gen_S1(j), n_S1(j), 0]

        def adv(j, frac):
            st = live.get(j)
            if st is None:
                return
            pv = live.get(j - 1)
            while pv is not None and pv[0] is not None and not idx_done.get(j - 1, False):
                try:
                    next(pv[0])
                    pv[2] += 1
                except StopIteration:
                    pv[0] = None
            while st[0] is not None and st[2] < frac * st[1]:
                try:
                    next(st[0])
                    st[2] += 1
                except StopIteration:
                    st[0] = None
            if frac >= 1.0:
                while st[0] is not None:
                    try:
                        next(st[0])
                    except StopIteration:
                        st[0] = None

        start(0)
        adv(0, 1.0)
        for i in range(_nt2):
            start(i + 1)
            start(i + 2)
            if i == NT - 2:
                pf_z0()
            if i == NT - 1:
                pf_rest()
            g2 = gen_S2(i)
            n2 = n_S2(i)
            s = 0
            for _ in g2:
                s += 1
                p = min(1.0, s / n2)
                adv(i + 1, min(1.0, 0.5 + 0.5 * p / 0.95))
                adv(i + 2, 0.5 * p)
            adv(i + 1, 1.0)
            live.pop(i + 1, None)

        dump("d_oT", oT[:], [128, 8, S_LEN], BF16)
        if stop == 2:
            return finish()
        pf_z0()
        pf_rest()
        A3 = Arena(nc, R_QKV.base, R_QKV.size, "p3")
        A3b = Arena(nc, R_SP.base, R_SP.size, "p3b")
        mT = A3.alloc("mT", [128, 8, S_LEN], BF16)
        gws = [{"gA": wsl[2], "gB": WgB0, "pA": WpA0, "pB": None},
               {"gA": A3.alloc("WgA1", [128, 8, 512], BF16), "gB": A3.alloc("WgB1", [128, 8, 512], BF16),
                "pA": A3.alloc("WpA1", [128, 4, 512], BF16), "pB": A3.alloc("WpB1", [128, 4, 512], BF16)}]
        gws[0]["pB"] = A3.alloc("WpB0", [128, 4, 512], BF16)
        xts2 = [A3.alloc("xo%d" % i, [128, D], F32) for i in range(2)]
        tmpz = [A3b.alloc("tmpz%d" % i, [128, 512], BF16) for i in range(2)]
        sA = [A3b.alloc("sA%d" % i, [128, 512], F32) for i in range(2)]
        sB = [A3b.alloc("sB%d" % i, [128, 512], F32) for i in range(2)]
        tA = [A3b.alloc("tA%d" % i, [128, 512], F32) for i in range(2)]
        S.dma("pool", "d_g0b", [(gws[0]["pB"][:], w_pb_r[:, :, 0:512])])
        S.dma("pool", "d_g1", [
            (gws[1]["pA"][:], w_pa_r[:, :, 512:1024]), (gws[1]["pB"][:], w_pb_r[:, :, 512:1024]),
            (gws[1]["gA"][:], w_in_r[:, :, 3624 + 512:3624 + 1024]),
            (gws[1]["gB"][:], w_in_r[:, :, 4648 + 512:4648 + 1024]),
        ])

        zn = 0
        for zi, (col0, cbase) in enumerate(((768, 0), (2816, 4))):
            wz = wsl[zi]
            for cc in range(4):
                for tg in range(4):
                    pz = pbf[zn % 2]
                    tz = tmpz[zn % 2]
                    zn += 1
                    for kc in range(8):
                        MM(pz, wz[:, kc, cc * 128:(cc + 1) * 128], hT[:, kc, tsl(tg)], kc == 0, kc == 7)
                    E("act", "activation", out=tz[:], in_=pz, func=AF.Silu)
                    E("dve", "tensor_tensor", out=oT[:, cbase + cc, tsl(tg)], in0=oT[:, cbase + cc, tsl(tg)],
                      in1=tz[:], op=ALU.mult)
        wo = [wsl[0], wsl[1]]
        for hf in range(2):
            S.dma("pool", "d_w%d" % hf, [(wsl[hf][:], w_out_r[:, :, hf * 512:(hf + 1) * 512])])

        gn = 0
        for dg in range(2):
            gw = gws[dg]
            for dcl in range(4):
                dc = dg * 4 + dcl
                cs = slice(dcl * 128, (dcl + 1) * 128)
                for tg in range(4):
                    k = gn % 2
                    gn += 1
                    bPA, bPB, bgA, bgB = (2, 3, 4, 5) if k == 0 else (0, 1, 6, 7)
                    for kc in range(8):
                        MM(pbf[bgA], gw["gA"][:, kc, cs], hT[:, kc, tsl(tg)], kc == 0, kc == 7)
                    for kc in range(8):
                        MM(pbf[bgB], gw["gB"][:, kc, cs], hT[:, kc, tsl(tg)], kc == 0, kc == 7)
                    for ec in range(4):
                        MM(pbf[bPA], gw["pA"][:, ec, cs], oT[:, ec, tsl(tg)], ec == 0, ec == 3)
                    for ec in range(4):
                        MM(pbf[bPB], gw["pB"][:, ec, cs], oT[:, 4 + ec, tsl(tg)], ec == 0, ec == 3)
                    E("act", "activation", out=sA[k][:], in_=pbf[bgA], func=AF.Sigmoid)
                    E("act", "activation", out=sB[k][:], in_=pbf[bgB], func=AF.Sigmoid)
                    E("dve", "tensor_tensor", out=tA[k][:], in0=sA[k][:], in1=pbf[bPA], op=ALU.mult)
                    E("dve", "tensor_tensor", out=sB[k][:], in0=sB[k][:], in1=pbf[bPB], op=ALU.mult)
                    E("dve", "tensor_tensor", out=mT[:, dc, tsl(tg)], in0=tA[k][:], in1=sB[k][:], op=ALU.add)

        okeys = []
        for ti in range(NT):
            tok = slice(ti * 128, (ti + 1) * 128)
            xo = xts2[ti % 2]
            S.dma("sp", "d_xo%d" % (ti % 2), [(xo[:], x[tok, :])])
            for hf in range(2):
                po = pbf[2 * (ti % 2) + hf]
                for dc in range(8):
                    MM(po, mT[:, dc, tok], wo[hf][:, dc, :], dc == 0, dc == 7)
                E("dve", "tensor_tensor", out=xo[:, hf * 512:(hf + 1) * 512], in0=po,
                  in1=xo[:, hf * 512:(hf + 1) * 512], op=ALU.add)
            key = "d_o%d" % (ti % 2)
            if key not in okeys:
                okeys.append(key)
            S.dma("sp", key, [(out[tok, :], xo[:])])
        dkeys.extend(okeys)
        return finish()


def _t5_bucket_np(n):
    n = np.maximum(n, 0)
    nf = np.maximum(n, 1).astype(np.float32)
    large = 16 + (np.log(nf / np.float32(16)) / np.float32(math.log(128 / 16)) * np.float32(16)).astype(np.int32)
    large = np.minimum(large, 31)
    return np.where(n < 16, n, large)


def _host_consts(norm_g, qnorm_a, knorm_a, sinks_a, qnorm_b, knorm_b, rel_bias):
    f = np.float32
    s = np.arange(128)[:, None]
    t = np.arange(128)[None, :]
    d0 = t - s
    d1 = t + 128 - s
    b0 = _t5_bucket_np(d0)
    b1 = _t5_bucket_np(d1)
    ta = rel_bias[:, :8]
    tb = rel_bias[:, 8:]

    def gath(tab):
        a = np.stack([tab[b0], tab[b1]], axis=1)
        return np.ascontiguousarray(a.transpose(0, 1, 3, 2)).reshape(128, -1).astype(f)

    mA = np.stack([(s <= t), (s > t)], axis=1).astype(f).reshape(128, -1)
    tt = np.arange(128)[:, None]
    sx = np.arange(128)[None, :]
    tril = (sx <= tt).astype(f)
    bd = np.zeros((128, 128), f)
    bd[:64, :64] = 1
    bd[64:, 64:] = 1
    return {
        "c_gT": np.ascontiguousarray(norm_g.reshape(8, 128).T).astype(f),
        "c_gq": np.ascontiguousarray(np.stack([np.tile(qnorm_a, 2), np.tile(knorm_a, 2),
                                               np.tile(qnorm_b, 2), np.tile(knorm_b, 2)], axis=1)).astype(f),
        "c_sink": np.ascontiguousarray(np.broadcast_to(sinks_a[None, :], (128, 8))).astype(f),
        "c_cfar": np.ascontiguousarray(np.broadcast_to(tb[31][None, :], (128, 8))).astype(f),
        "c_biasA": gath(ta),
        "c_biasB": gath(tb),
        "c_mA": mA,
        "c_negm": np.where(sx <= tt, 0.0, -BIG).astype(f),
        "c_tril": tril,
        "c_ident": np.eye(128, dtype=f),
        "c_bd": bd,
        "c_pow2": np.ascontiguousarray(np.broadcast_to((0.5 ** np.arange(NIT))[None, :], (128, NIT))).astype(f),
        "c_npow2": np.ascontiguousarray(np.broadcast_to((-0.5 * 0.5 ** np.arange(NIT))[None, :], (128, NIT))).astype(f),
        "c_cthr": np.ascontiguousarray(np.broadcast_to(((np.arange(NT) + 1) * 128 - 511.5)[None, :], (128, NT))).astype(f),
    }


_CACHE = {}


def kernel(x, norm_g, w_in, qnorm_a, knorm_a, sinks_a, qnorm_b, knorm_b, rel_bias, w_proj_a, w_proj_b, w_out):
    a = lambda v: np.ascontiguousarray(np.asarray(v, dtype=np.float32))
    x = a(x)
    consts = _host_consts(a(norm_g)[0], a(qnorm_a)[0], a(knorm_a)[0], a(sinks_a)[0], a(qnorm_b)[0],
                          a(knorm_b)[0], a(rel_bias))
    shared = {"w_in": a(w_in)[0], "w_pa": a(w_proj_a)[0], "w_pb": a(w_proj_b)[0], "w_out": a(w_out)[0]}
    shared.update(consts)
    if "nc" not in _CACHE:
        _CACHE["nc"] = build_program()
    nc = _CACHE["nc"]
    in_maps = [dict(shared, x=x[b]) for b in range(8)]
    res = run_bass_kernel_spmd(nc, in_maps, core_ids=list(range(8)))
    return np.stack([r["out"] for r in res.results], axis=0).astype(np.float32)
```

```python
import math
from contextlib import ExitStack

import numpy as np
import concourse.bass as bass
import concourse.mybir as mybir
from concourse.bass_utils import run_bass_kernel_spmd

F32 = mybir.dt.float32
BF16 = mybir.dt.bfloat16
ALU = mybir.AluOpType
AF = mybir.ActivationFunctionType
AX = mybir.AxisListType

S_LEN = 2048
D = 1024
NT = 16
INW = 5672
NIT = 16
K0 = 10
BIG = 1.0e30
ENGS = ("pe", "dve", "act", "pool", "sp")
_ESZ = {F32: 4, BF16: 2}
PAGE = 2048


def _esize(dt):
    return _ESZ[dt]


def _box(ap):
    t = ap.tensor
    tn = type(t).__name__
    if tn.startswith("DRam"):
        return None
    dims = list(ap.ap)
    off = int(ap.offset)
    es = _esize(ap.dtype)
    pstride = int(dims[0][0])
    npart = int(dims[0][1])
    if pstride <= 0:
        pstride = 1
        for s in list(t.shape)[1:]:
            pstride *= int(s)
    p0 = off // pstride
    f0 = off % pstride
    f1 = f0
    for st, n in dims[1:]:
        f1 += abs(int(st)) * (int(n) - 1)
    if tn.startswith("PSum"):
        base = 1 << 24
        base += int(t.name[2:]) * 2048
    else:
        base = int(t.manual_sbuf_range[0])
    return (p0, p0 + npart, base + f0 * es, base + (f1 + 1) * es)


def _ovl(a, b):
    return a[0] < b[1] and b[0] < a[1] and a[2] < b[3] and b[2] < a[3]


def _contains(a, b):
    return a[0] <= b[0] and a[1] >= b[1] and a[2] <= b[2] and a[3] >= b[3]


class Sched:
    def __init__(self, nc, stack):
        self.nc = nc
        self.stack = stack
        self.q = {e: [] for e in ENGS}
        self.sems = {}
        self.cnt = {}
        self.unit = {}
        for e in ENGS:
            self._mksem(e, 1)
        self.seen = {e: {} for e in ENGS}
        self.pages = {}
        self.n_inst = 0

    def _mksem(self, key, unit):
        self.sems[key] = self.stack.enter_context(self.nc.semaphore("s_" + key))
        self.cnt[key] = 0
        self.unit[key] = unit

    def _pages(self, box):
        return range(box[2] // PAGE, (box[3] - 1) // PAGE + 1)

    def _scan(self, box, want_reads, deps, eng, raw):
        for pg in self._pages(box):
            for r in self.pages.get(pg, ()):
                if (want_reads or r[1] == "w") and _ovl(r[0], box):
                    k, i = r[2]
                    if k == eng and eng == "pe":
                        continue
                    if deps.get(k, 0) < i:
                        deps[k] = i

    def _record(self, prod, rboxes, wboxes):
        for box in wboxes:
            for pg in self._pages(box):
                lst = self.pages.setdefault(pg, [])
                lst[:] = [r for r in lst if not _contains(box, r[0])]
                lst.append((box, "w", prod))
        for box in rboxes:
            for pg in self._pages(box):
                lst = self.pages.setdefault(pg, [])
                lst[:] = [r for r in lst if not (r[1] == "r" and r[2][0] == prod[0] and _contains(box, r[0]))]
                lst.append((box, "r", prod))

    def _emit_waits(self, eng, deps):
        seen = self.seen[eng]
        for k, i in deps.items():
            if k == eng and eng == "pe":
                continue
            if seen.get(k, 0) >= i:
                continue
            seen[k] = i
            self.q[eng].append(("wait", k, i * self.unit[k]))

    def op(self, eng, fn, reads=(), writes=(), inc=True):
        rb = [b for b in (_box(a) for a in reads) if b is not None]
        wb = [b for b in (_box(a) for a in writes) if b is not None]
        deps = {}
        for b in rb:
            self._scan(b, False, deps, eng, True)
        for b in wb:
            self._scan(b, True, deps, eng, False)
        self._emit_waits(eng, deps)
        idx = self.cnt[eng] + 1
        if inc:
            self.cnt[eng] = idx
        self.q[eng].append(("inst", fn, inc, eng))
        self._record((eng, idx), rb, wb)
        self.n_inst += 1

    def dma(self, qeng, semkey, pairs, **kw):
        if semkey not in self.sems:
            self._mksem(semkey, 16)
        rb = [b for b in (_box(p[1]) for p in pairs) if b is not None]
        wb = [b for b in (_box(p[0]) for p in pairs) if b is not None]
        deps = {}
        for b in rb:
            self._scan(b, False, deps, "__dma__", False)
        for b in wb:
            self._scan(b, True, deps, "__dma__", False)
        self._emit_waits(qeng, deps)
        idx = self.cnt[semkey] + len(pairs)
        self.cnt[semkey] = idx
        for (o, i) in pairs:
            self.q[qeng].append(("dma", o, i, semkey, kw))
        self._record((semkey, idx), rb, wb)
        self.n_inst += len(pairs)

    def wait_all(self, eng, keys):
        for k in keys:
            if self.cnt[k] > self.seen[eng].get(k, 0):
                self.seen[eng][k] = self.cnt[k]
                self.q[eng].append(("wait", k, self.cnt[k] * self.unit[k]))

    def emit(self):
        nc = self.nc
        sems = self.sems
        q = self.q

        def run(engobj, items):
            for it in items:
                if it[0] == "wait":
                    engobj.wait_ge(sems[it[1]], it[2])
                elif it[0] == "inst":
                    ins = it[1](engobj)
                    if it[2]:
                        ins.then_inc(sems[it[3]], 1)
                else:
                    _, o, i, sk, kw = it
                    engobj.dma_start(out=o, in_=i, **kw).then_inc(sems[sk], 16)

        with nc.Block() as block:
            @block.tensor
            def _(e):
                run(e, q["pe"])

            @block.vector
            def _(e):
                run(e, q["dve"])

            @block.scalar
            def _(e):
                run(e, q["act"])

            @block.gpsimd
            def _(e):
                run(e, q["pool"])

            @block.sync
            def _(e):
                run(e, q["sp"])


class Arena:
    def __init__(self, nc, base, size, tag):
        self.nc, self.base, self.size, self.tag, self.off = nc, base, size, tag, 0

    def alloc(self, name, shape, dt):
        n = 1
        for s in shape[1:]:
            n *= s
        nb = n * _esize(dt)
        nb = (nb + 31) // 32 * 32
        assert self.off + nb <= self.size, (self.tag, name, self.off, nb, self.size)
        t = self.nc.alloc_sbuf_tensor_at(name, list(shape), dt, offset=self.base + self.off)
        self.off += nb
        return t


def build_program(stop=99, dbg=None):
    nc = bass.Bass("TRN2", target_bir_lowering=False)
    dr = lambda name, shape, kind="ExternalInput": nc.dram_tensor(name, shape, F32, kind=kind).ap()
    x = dr("x", [S_LEN, D])
    w_in = dr("w_in", [D, INW])
    w_pa = dr("w_pa", [512, D])
    w_pb = dr("w_pb", [512, D])
    w_out = dr("w_out", [D, D])
    c_gT = dr("c_gT", [128, 8])
    c_gq = dr("c_gq", [128, 4])
    c_sink = dr("c_sink", [128, 8])
    c_cfar = dr("c_cfar", [128, 8])
    c_biasA = dr("c_biasA", [128, 2 * 8 * 128])
    c_biasB = dr("c_biasB", [128, 2 * 8 * 128])
    c_mA = dr("c_mA", [128, 2 * 128])
    c_negm = dr("c_negm", [128, 128])
    c_tril = dr("c_tril", [128, 128])
    c_ident = dr("c_ident", [128, 128])
    c_bd = dr("c_bd", [128, 128])
    c_pow2 = dr("c_pow2", [128, NIT])
    c_npow2 = dr("c_npow2", [128, NIT])
    c_cthr = dr("c_cthr", [128, NT])
    out = dr("out", [S_LEN, D], kind="ExternalOutput")

    with ExitStack() as st:
        S = Sched(nc, st)

        def E(eng, meth, inc=True, **kw):
            reads, writes = [], []
            for k, v in kw.items():
                if hasattr(v, "tensor") and hasattr(v, "ap"):
                    (writes if k in ("out", "accum_out", "ap") else reads).append(v)
            S.op(eng, lambda e: getattr(e, meth)(**kw), reads=reads, writes=writes, inc=inc)

        def MM(outp, lhsT, rhs, start, stop, inc=None, **kw):
            if inc is None:
                inc = stop
            S.op("pe", lambda e: e.matmul(outp, lhsT=lhsT, rhs=rhs, start=start, stop=stop, **kw),
                 reads=[lhsT, rhs], writes=[outp], inc=inc)

        def TR(outp, in_, ident, inc=True):
            S.op("pe", lambda e: e.transpose(out=outp, in_=in_, identity=ident),
                 reads=[in_, ident], writes=[outp], inc=inc)

        dkeys = []

        def dump(name, sb_ap, shape, dt):
            if dbg is None:
                return
            d = nc.dram_tensor(name, list(shape), dt, kind="ExternalOutput").ap()
            S.dma("sp", "d_dbg", [(d, sb_ap)])
            dbg.append(name)
            if "d_dbg" not in dkeys:
                dkeys.append("d_dbg")

        def finish():
            S.wait_all("sp", dkeys)
            S.emit()
            return nc

        base0 = (int(nc.sbuf_base) + 63) // 64 * 64
        top = int(nc.sbuf_top)
        cur = [base0]

        def region(size, tag):
            a = Arena(nc, cur[0], size, tag)
            cur[0] += size
            assert cur[0] <= top, (tag, cur[0], top)
            return a

        R_CONST = region(12288, "const")
        R_HT = region(32768, "hT")
        R_W = region(24576, "w")
        R_OT = region(32768, "oT")
        R_QKV = region(90432, "qkv")
        R_SP = region((top - cur[0]) // 64 * 64, "spare")

        pb0 = nc.alloc_psum_tensor("pb0", [128, 512], F32)
        pb1 = nc.alloc_psum_tensor("pb1", [128, 512], F32)
        pb2 = nc.alloc_psum_tensor("pb2", [128, 1024], F32)
        pb4 = nc.alloc_psum_tensor("pb4", [128, 1024], F32)
        pb6 = nc.alloc_psum_tensor("pb6", [128, 512], F32)
        pb7 = nc.alloc_psum_tensor("pb7", [128, 512], F32)
        pbf = [pb0.ap(), pb1.ap(), pb2.ap()[:, 0:512], pb2.ap()[:, 512:1024], pb4.ap()[:, 0:512],
               pb4.ap()[:, 512:1024], pb6.ap(), pb7.ap()]
        _h2, _h4 = pb2.bitcast(BF16).ap(), pb4.bitcast(BF16).ap()
        pbh = [pb0.bitcast(BF16).ap(), pb1.bitcast(BF16).ap(), _h2[:, 0:1024], _h2[:, 1024:2048], _h4[:, 0:1024],
               _h4[:, 1024:2048], pb6.bitcast(BF16).ap(), pb7.bitcast(BF16).ap()]
        sc2w = [pb2.ap().rearrange("p (f a b) -> p f a b", f=2, a=4), pb4.ap().rearrange("p (f a b) -> p f a b", f=2, a=4)]

        identb = R_CONST.alloc("identb", [128, 128], BF16)
        bdb = R_CONST.alloc("bdb", [128, 128], BF16)
        EbA = R_CONST.alloc("EbA", [128, 2, 8, 128], BF16)
        EbB = R_CONST.alloc("EbB", [128, 2, 8, 128], BF16)
        negm = R_CONST.alloc("negm", [128, 128], F32)
        trilb = R_CONST.alloc("trilb", [128, 128], BF16)
        gT = R_CONST.alloc("gT", [128, 8], F32)
        gq = R_CONST.alloc("gq", [128, 4], F32)
        esink = R_CONST.alloc("esink", [128, 8], F32)
        cfar = R_CONST.alloc("cfar", [128, 8], F32)
        epsT = R_CONST.alloc("epsT", [128, 1], F32)
        pow2 = R_CONST.alloc("pow2", [128, NIT], F32)
        npow2 = R_CONST.alloc("npow2", [128, NIT], F32)
        cthr = R_CONST.alloc("cthr", [128, NT], F32)
        sgn = R_CONST.alloc("sgn", [128, NT, 8], F32)
        wab = R_CONST.alloc("wab", [128, NT, 8], F32)
        ss = R_CONST.alloc("ss", [128, NT], F32)
        sd = R_CONST.alloc("sd", [128, NT], F32)
        rstd = R_CONST.alloc("rstd", [128, NT], F32)
        smalls = R_CONST.alloc("smalls", [128, 96], F32)
        Rrs = [smalls[:, 0:1], smalls[:, 7:8]]
        mid = smalls[:, 1:2]
        cntv = smalls[:, 2:3]
        dirv = smalls[:, 3:4]
        den = smalls[:, 8:16]
        Rk = smalls[:, 16:16 + NIT]
        negRks = [smalls[:, 32:32 + NIT], smalls[:, 48:48 + NIT]]
        csum = smalls[:, 4:5]
        dirS = smalls[:, 5:6]
        nmids = [smalls[:, 6:7], smalls[:, 64:65]]
        mids = [smalls[:, 1:2], smalls[:, 65:66]]
        cntvs = [smalls[:, 2:3], smalls[:, 66:67]]
        dirvs = [smalls[:, 3:4], smalls[:, 67:68]]
        csums = [smalls[:, 4:5], smalls[:, 68:69]]
        dirSs = [smalls[:, 5:6], smalls[:, 69:70]]
        Rks = [smalls[:, 16:16 + NIT], smalls[:, 70:70 + NIT]]

        hT = R_HT.alloc("hT", [128, 8, S_LEN], BF16)
        wsl = [R_W.alloc("wsl%d" % i, [128, 8, 512], BF16) for i in range(3)]
        oT = R_OT.alloc("oT", [128, 8, S_LEN], BF16)

        qaT = R_QKV.alloc("qaT", [128, 4, S_LEN], BF16)
        kaT2 = R_QKV.alloc("kaT2", [128, 2, S_LEN], BF16)
        vA = R_QKV.alloc("vA", [128, NT, 2, 65], BF16)
        qbT = R_QKV.alloc("qbT", [128, 4, S_LEN], BF16)
        kbT = R_QKV.alloc("kbT", [128, 4, S_LEN], BF16)
        vB = R_QKV.alloc("vB", [128, NT, 8, 65], BF16)
        q2T = R_QKV.alloc("q2T", [128, 2, S_LEN], BF16)
        kiT = R_QKV.alloc("kiT", [128, S_LEN], BF16)

        A1 = Arena(nc, R_OT.base, R_OT.size, "p01")
        xts = [A1.alloc("xt%d" % i, [128, D], F32) for i in range(2)]
        hns = [A1.alloc("hn%d" % i, [128, D], BF16) for i in range(2)]
        sqb = [A1.alloc("sqb%d" % i, [128, 512], BF16) for i in range(2)]
        sdb = [A1.alloc("sdb%d" % i, [128, 512], F32) for i in range(2)]
        q2b = [A1.alloc("q2b%d" % i, [128, 256], BF16) for i in range(2)]
        stgA = A1.alloc("stgA", [128, 2, 8, 128], F32)
        _sb = int(stgA.manual_sbuf_range[0])
        xts += [nc.alloc_sbuf_tensor_at("xt%d" % (2 + k), [128, D], F32, offset=_sb + 4096 * k) for k in range(2)]
        stgm = A1.alloc("stgm", [128, 2, 128], F32)
        stgi = A1.alloc("stgi", [128, 128], F32)
        stgd = A1.alloc("stgd", [128, 128], F32)
        stgt = A1.alloc("stgt", [128, 128], F32)

        xpre = set()
        for _i in range(2):
            S.dma("sp", "d_x%d" % _i, [(xts[_i][:], x[_i * 128:(_i + 1) * 128, :])])
            xpre.add(_i)
        S.dma("sp", "d_c", [
            (gT[:], c_gT[:, :]), (gq[:], c_gq[:, :]), (esink[:], c_sink[:, :]), (cfar[:], c_cfar[:, :]),
            (negm[:], c_negm[:, :]), (pow2[:], c_pow2[:, :]), (npow2[:], c_npow2[:, :]), (cthr[:], c_cthr[:, :]),
            (stgm[:].rearrange("p a b -> p (a b)"), c_mA[:, :]),
            (stgi[:], c_ident[:, :]), (stgd[:], c_bd[:, :]), (stgt[:], c_tril[:, :]),
        ])
        E("dve", "memset", ap=epsT[:], constant=1e-6)
        E("dve", "tensor_copy", out=identb[:], in_=stgi[:])
        E("dve", "tensor_copy", out=bdb[:], in_=stgd[:])
        E("dve", "tensor_copy", out=trilb[:], in_=stgt[:])
        E("pool", "memset", ap=vA[:, :, :, 64:65], constant=1.0)
        E("pool", "memset", ap=vB[:, :, :, 64:65], constant=1.0)
        E("dve", "tensor_scalar", out=gq[:, 0:1], in0=gq[:, 0:1], scalar1=0.125, scalar2=None, op0=ALU.mult)
        E("dve", "tensor_scalar", out=gq[:, 2:3], in0=gq[:, 2:3], scalar1=0.125, scalar2=None, op0=ALU.mult)
        E("act", "activation", out=esink[:], in_=esink[:], func=AF.Exp)
        S.dma("sp", "d_c2", [(stgA[:].rearrange("p a h t -> p (a h t)"), c_biasA[:, :])])
        E("act", "activation", out=stgA[:], in_=stgA[:], func=AF.Exp)
        for jt in range(2):
            E("dve", "tensor_tensor", out=EbA[:, jt, :, :], in0=stgA[:, jt, :, :],
              in1=stgm[:, jt, :].unsqueeze(1).broadcast_to([128, 8, 128]), op=ALU.mult)
        stgB = nc.alloc_sbuf_tensor_at("stgB", [128, 2, 8, 128], F32, offset=R_SP.base)
        S.dma("sp", "d_c3", [(stgB[:].rearrange("p a h t -> p (a h t)"), c_biasB[:, :])])
        for jt in range(2):
            E("dve", "tensor_tensor", out=stgB[:, jt, :, :], in0=stgB[:, jt, :, :],
              in1=cfar[:, :].unsqueeze(2).broadcast_to([128, 8, 128]), op=ALU.subtract)
        for jt in range(2):
            E("act", "activation", out=EbB[:, 1 - jt, :, :], in_=stgB[:, jt, :, :], func=AF.Exp)

        w_in_r = w_in.rearrange("(kc p) c -> p kc c", p=128)
        wstate = {"n": 0}

        def load_w(pieces):
            s = wstate["n"] % 3
            wstate["n"] += 1
            t = wsl[s]
            S.dma("pool", "d_w%d" % s,
                  [(t[:, :, d0:d0 + n], w_in_r[:, :, s0:s0 + n]) for (d0, n, s0) in pieces])
            return t

        def p0A(i):
            xt = xts[i % 4]
            hn = hns[i % 2]
            if i not in xpre:
                S.dma("sp", "d_x%d" % (i % 4), [(xt[:], x[i * 128:(i + 1) * 128, :])])
            E("act", "activation", out=hn[:], in_=xt[:], func=AF.Square, accum_out=ss[:, i:i + 1])
            E("act", "activation", out=sd[:, i:i + 1], in_=ss[:, i:i + 1], func=AF.Ln,
              scale=1.0 / D, bias=epsT[:])
            E("act", "activation", out=rstd[:, i:i + 1], in_=sd[:, i:i + 1], func=AF.Exp, scale=-0.5)
            E("pool", "tensor_scalar", out=hn[:], in0=xt[:], scalar1=rstd[:, i:i + 1], scalar2=1.0,
              op0=ALU.mult, op1=ALU.mult)
            ptr = pbh[i % 2][:, 0:1024].rearrange("p (a b) -> p a b", a=8)
            for kc in range(8):
                TR(ptr[:, kc, :], hn[:, kc * 128:(kc + 1) * 128], identb[:], inc=(kc == 7))

        def p0B(i):
            ptr = pbh[i % 2][:, 0:1024].rearrange("p (a b) -> p a b", a=8)
            E("dve", "tensor_tensor", out=hT[:, :, i * 128:(i + 1) * 128], in0=ptr,
              in1=gT[:, :].unsqueeze(2).broadcast_to([128, 8, 128]), op=ALU.mult)

        p0s = {"a": 0, "b": 0}

        def p0_adv():
            if p0s["a"] < NT:
                p0A(p0s["a"])
                p0s["a"] += 1
            if p0s["b"] < p0s["a"] - 1 or (p0s["a"] == NT and p0s["b"] < NT):
                p0B(p0s["b"])
                p0s["b"] += 1

        def p0_need(ntiles):
            while p0s["b"] < ntiles:
                if p0s["a"] < NT and p0s["a"] <= p0s["b"] + 1:
                    p0A(p0s["a"])
                    p0s["a"] += 1
                else:
                    p0B(p0s["b"])
                    p0s["b"] += 1

        def tsl(tg):
            return slice(tg * 512, (tg + 1) * 512)

        fmn = [0]
        pend = [None]

        def fm_mm(ws, ccol, tg):
            n = fmn[0]
            fmn[0] += 1
            acc = pbf[(2, 3, 6, 7)[n % 4]]
            for kc in range(8):
                MM(acc, ws[:, kc, ccol:ccol + 128], hT[:, kc, tsl(tg)], kc == 0, kc == 7)
            return n

        def fm_post(n, dst, gain, norm):
            acc = pbf[(2, 3, 6, 7)[n % 4]]
            if norm:
                sq = sqb[n % 2]
                E("act", "activation", out=sq[:], in_=acc, func=AF.Square)
                ssb = pbf[4 + n % 2]
                MM(ssb, bdb[:], sq[:], True, True)
                sdt = sdb[n % 2]
                E("act", "activation", out=sdt[:], in_=ssb, func=AF.Ln, scale=1.0 / 64, bias=epsT[:])
                E("act", "activation", out=sdt[:], in_=sdt[:], func=AF.Exp, scale=-0.5)
                E("dve", "scalar_tensor_tensor", out=dst, in0=acc, scalar=gain, in1=sdt[:],
                  op0=ALU.mult, op1=ALU.mult)
            else:
                E("act", "activation", out=dst, in_=acc, func=AF.Copy)

        pendq = []

        def fm(ws, ccol, dst, gain, norm, tg):
            n = fm_mm(ws, ccol, tg)
            if len(pendq) == 2:
                fm_post(*pendq.pop(0))
            pendq.append((n, dst, gain, norm))

        def fm_flush():
            while pendq:
                fm_post(*pendq.pop(0))

        ws = load_w([(0, 512, 0)])
        p0_need(NT)
        ws_qb = load_w([(0, 512, 1280)])
        ws_kb = load_w([(0, 512, 1792)])
        for tg in range(4):
            for c in range(4):
                fm(ws, c * 128, qaT[:, c, tsl(tg)], gq[:, 0:1], True, tg)
        ws_k = load_w([(0, 64, 512), (64, 64, 512), (128, 64, 576), (192, 64, 576),
                       (256, 32, 3584), (288, 32, 3584), (320, 32, 3584), (352, 32, 3584)])
        for c in range(4):
            for tg in range(4):
                fm(ws_qb, c * 128, qbT[:, c, tsl(tg)], gq[:, 2:3], True, tg)
        ws_vb = load_w([(0, 512, 2304)])
        for c in range(4):
            for tg in range(4):
                fm(ws_kb, c * 128, kbT[:, c, tsl(tg)], gq[:, 3:4], True, tg)
        ws_g5 = load_w([(0, 128, 640), (128, 256, 3328), (384, 8, 3616)])
        for c in range(2):
            for tg in range(4):
                fm(ws_k, c * 128, kaT2[:, c, tsl(tg)], gq[:, 1:2], True, tg)
        for tg in range(4):
            fm(ws_k, 256, kiT[:, tsl(tg)], None, False, tg)
        fm_flush()

        def tm_mm(ti):
            accv = pbf[2 + 2 * (ti % 2)]
            accq = pbf[3 + 2 * (ti % 2)]
            tok = slice(ti * 128, (ti + 1) * 128)
            for kc in range(8):
                MM(accv, hT[:, kc, tok], ws_vb[:, kc, 0:512], kc == 0, kc == 7)
            for kc in range(8):
                MM(accq[:, 0:392], hT[:, kc, tok], ws_g5[:, kc, 0:392], kc == 0, kc == 7)

        def tm_post(ti):
            accv = pbf[2 + 2 * (ti % 2)]
            accq = pbf[3 + 2 * (ti % 2)]
            tok = slice(ti * 128, (ti + 1) * 128)
            E("act", "activation", out=vB[:, ti, :, 0:64], in_=accv.rearrange("p (h d) -> p h d", h=8), func=AF.Copy)
            E("act", "activation", out=vA[:, ti, :, 0:64],
              in_=accq[:, 0:128].rearrange("p (h d) -> p h d", h=2), func=AF.Copy)
            E("act", "activation", out=sgn[:, ti, :], in_=accq[:, 384:392], func=AF.Sign)
            E("dve", "scalar_tensor_tensor", out=wab[:, ti, :], in0=accq[:, 384:392], scalar=0.0625,
              in1=sgn[:, ti, :], op0=ALU.mult, op1=ALU.mult)
            qb2 = q2b[ti % 2]
            E("dve", "tensor_tensor", out=qb2[:].rearrange("p (h e) -> p h e", h=8),
              in0=accq[:, 128:384].rearrange("p (h e) -> p h e", h=8),
              in1=wab[:, ti, :].unsqueeze(2).broadcast_to([128, 8, 32]), op=ALU.mult)
            ptr = pbh[ti % 2][:, 0:256].rearrange("p (a b) -> p a b", a=2)
            for g in range(2):
                TR(ptr[:, g, :], qb2[:, g * 128:(g + 1) * 128], identb[:], inc=(g == 1))
            E("act", "activation", out=q2T[:, :, tok], in_=ptr, func=AF.Copy)

        tm_mm(0)
        for ti in range(NT):
            if ti + 1 < NT:
                tm_mm(ti + 1)
            tm_post(ti)

        dump("d_hT", hT[:], [128, 8, S_LEN], BF16)
        if stop == 0:
            return finish()
        for nm, t in (("d_qaT", qaT), ("d_kaT2", kaT2), ("d_vA", vA), ("d_qbT", qbT), ("d_kbT", kbT), ("d_vB", vB),
                      ("d_q2T", q2T), ("d_kiT", kiT), ("d_sgn", sgn)):
            dump(nm, t[:], [int(v) for v in t.shape], t.dtype)
        if stop == 1:
            return finish()
        import os
        _nt2 = int(os.environ.get("DBG_NT", NT))
        A2 = Arena(nc, R_W.base, R_W.size, "p2a")
        A2b = Arena(nc, R_SP.base, R_SP.size, "p2b")
        scoresb = [A2.alloc("scores%d" % k, [128, S_LEN], F32) for k in range(2)]
        masktb = [A2.alloc("maskt%d" % k, [128, S_LEN], BF16) for k in range(2)]
        maskTs = [A2b.alloc("maskT%d" % k, [128, NT, 128], BF16) for k in range(2)]
        exBp = [A2b.alloc("exBp%d" % i, [128, 2, 4, 128], BF16) for i in range(2)]
        exB = [exBp[i // 2][:, i % 2, :, :] for i in range(4)]
        PTBp = [A2b.alloc("PTBp%d" % i, [128, 2, 4, 128], BF16) for i in range(2)]
        PTB = [PTBp[i // 2][:, i % 2, :, :] for i in range(4)]
        ob = A2b.alloc("ob", [128, D], BF16)

        oaccv = [pbf[6][:, 0:260].rearrange("p (h d) -> p h d", h=4) for k in range(2)]
        sacc = pbf[7]
        scb2 = [[pbf[2 + 2 * s_ + k].rearrange("p (a b) -> p a b", a=4) for k in range(2)] for s_ in range(2)]
        scb = scb2[1]
        mtr = pbh[1][:, 0:1024].rearrange("p (a b) -> p a b", a=8)
        otr = pbh[2][:, 0:1024].rearrange("p (a b) -> p a b", a=8)
        ctr = {"g": 0}

        def gen_S1(i):
            qs = slice(i * 128, (i + 1) * 128)
            nk = (i + 1) * 128
            par = i % 2
            scores = scoresb[par]
            maskt = masktb[par]
            junk = maskt
            maskT = maskTs[par]
            if i >= 2:
                Rb = [maskt[:, 0:512], maskt[:, 512:1024]]
                Dh = maskt[:, 1024:2048].rearrange("p (h c) -> p h c", h=8)
                E("dve", "tensor_tensor", out=Dh, in0=identb[:].unsqueeze(1).broadcast_to([128, 8, 128]),
                  in1=sgn[:, i, :].unsqueeze(2).broadcast_to([128, 8, 128]), op=ALU.mult)
                work = [(ch, h) for ch in range((nk + 511) // 512) for h in range(8)]

                def idx_front(ch, h):
                    cw = min(512, nk - ch * 512)
                    csl = slice(ch * 512, ch * 512 + cw)
                    g, r = h // 4, h % 4
                    ip = pbf[h % 2]
                    MM(ip[:, 0:cw], q2T[32 * r:32 * r + 32, g, qs], kiT[32 * r:32 * r + 32, csl], True, True,
                       tile_position=(32 * r, 0))
                    E("act", "activation", out=Rb[h % 2][:, 0:cw], in_=ip[:, 0:cw], func=AF.Relu)

                def idx_back(ch, h):
                    cw = min(512, nk - ch * 512)
                    csl = slice(ch * 512, ch * 512 + cw)
                    MM(sacc[:, 0:cw], Dh[:, h, :], Rb[h % 2][:, 0:cw], h == 0, h == 7)
                    if h == 7:
                        E("dve", "tensor_copy", out=scores[:, csl], in_=sacc[:, 0:cw])

                idx_front(*work[0])
                for n_, wk in enumerate(work):
                    if n_ + 1 < len(work):
                        idx_front(*work[n_ + 1])
                    idx_back(*wk)
                    yield
                idx_done[i] = True
                Rr = Rrs[par]
                mid, cntv, dirv, csum, dirS, Rk = mids[par], cntvs[par], dirvs[par], csums[par], dirSs[par], Rks[par]
                E("dve", "tensor_reduce", out=Rr, in_=scores[:, 0:nk], axis=AX.X, op=ALU.max, apply_absolute_value=True)
                E("dve", "tensor_tensor", out=scores[:, i * 128:nk], in0=scores[:, i * 128:nk], in1=negm[:], op=ALU.add)
                E("dve", "tensor_scalar", out=Rk, in0=pow2[:], scalar1=Rr, scalar2=None, op0=ALU.mult)
                E("dve", "memset", ap=mid, constant=0.0)
                E("act", "activation", out=negRks[par], in_=npow2[:], func=AF.Copy, scale=Rr)
                yield
                for k in range(K0):
                    E("dve", "tensor_scalar", out=junk[:, 0:nk], in0=scores[:, 0:nk], scalar1=mid, scalar2=None,
                      op0=ALU.is_ge, op1=ALU.add, accum_out=cntv)
                    yield
                    E("dve", "tensor_scalar", out=dirv, in0=cntv, scalar1=255.5, scalar2=0.5,
                      op0=ALU.is_ge, op1=ALU.subtract)
                    yield
                    E("dve", "scalar_tensor_tensor", out=mid, in0=dirv, scalar=Rk[:, k:k + 1], in1=mid,
                      op0=ALU.mult, op1=ALU.add)
                    yield
                nmid = nmids[par]
                E("act", "activation", out=nmid, in_=mid, func=AF.Copy, scale=-1.0)
                for k in range(K0, NIT):
                    E("act", "activation", out=junk[:, 0:nk], in_=scores[:, 0:nk], func=AF.Sign, bias=nmid,
                      accum_out=csum)
                    yield
                    E("act", "activation", out=dirS, in_=csum, func=AF.Sign, bias=cthr[:, i:i + 1])
                    yield
                    E("act", "activation", out=nmid, in_=dirS, func=AF.Identity, scale=negRks[par][:, k:k + 1],
                      bias=nmid)
                    yield
                E("dve", "tensor_scalar", out=maskt[:, 0:nk], in0=scores[:, 0:nk], scalar1=nmid, scalar2=0.0,
                  op0=ALU.add, op1=ALU.is_ge)
            elif i == 0:
                E("dve", "tensor_copy", out=maskt[:, 0:128], in_=trilb[:])
            else:
                E("dve", "memset", ap=maskt[:, 0:128], constant=1.0)
                E("dve", "tensor_copy", out=maskt[:, 128:256], in_=trilb[:])
            yield
            for j0 in range(0, i + 1, 8):
                njs = min(i + 1, j0 + 8) - j0
                for jj in range(njs):
                    j = j0 + jj
                    TR(mtr[:, jj, :], maskt[:, j * 128:(j + 1) * 128], identb[:], inc=(jj == njs - 1))
                E("act", "activation", out=maskT[:, j0:j0 + njs, :], in_=mtr[:, 0:njs, :], func=AF.Copy)
                yield

        def n_S1(i):
            nk = (i + 1) * 128
            n = 1 + (i // 8 + 1)
            if i >= 2:
                n += 8 * ((nk + 511) // 512) + 1 + 3 * NIT
            return n

        def gen_S2(i):
            qs = slice(i * 128, (i + 1) * 128)
            maskT = maskTs[i % 2]
            oa = oaccv[0]
            jts = [(0, i)] + ([(1, i - 1)] if i >= 1 else [])
            njt = len(jts)
            nfar = max(0, i - 1)
            groups = [("far", list(range(j0, min(nfar, j0 + 4)))) for j0 in range(0, nfar, 4)]
            groups.append(("near", [j for j in (i - 1, i) if j >= 0]))
            items = [("A", hk, None) for hk in range(2)] + [(c, kind, js) for c in range(4) for (kind, js) in groups]

            def front(item, gi):
                c, kind, js = item
                if c == "A":
                    hk = kind
                    for jn, (jt, j) in enumerate(jts):
                        for hh in range(4):
                            hq = 4 * hk + hh
                            cq, hf = hq // 2, hq % 2
                            MM(scb2[gi][hf][:, jn * 2 + hh // 2, :],
                               kaT2[64 * hf:64 * hf + 64, hk, j * 128:(j + 1) * 128],
                               qaT[64 * hf:64 * hf + 64, cq, qs], True, True,
                               inc=(jn == njt - 1 and hh == 3))
                else:
                    for jj, j in enumerate(js):
                        for hf in range(2):
                            MM(scb2[gi][hf][:, jj, :], kbT[64 * hf:64 * hf + 64, c, j * 128:(j + 1) * 128],
                               qbT[64 * hf:64 * hf + 64, c, qs], True, True,
                               inc=(jj == len(js) - 1 and hf == 1))

            def mid(item, gi, hf):
                c, kind, js = item
                ex = exB[2 * gi + hf]
                pt = PTB[2 * gi + hf]
                if c == "A":
                    hk = kind
                    if hf == 0:
                        E("act", "activation", out=exBp[gi][:, :, 0:2 * njt, :], in_=sc2w[gi][:, :, 0:2 * njt, :],
                          func=AF.Exp)
                    if hf == 0:
                        for jn, (jt, j) in enumerate(jts):
                            E("dve", "tensor_tensor", out=PTBp[gi][:, :, 2 * jn:2 * jn + 2, :],
                              in0=exBp[gi][:, :, 2 * jn:2 * jn + 2, :],
                              in1=EbA[:, jt, 4 * hk:4 * hk + 4, :].rearrange("p (hh f) c -> p f hh c", f=2),
                              op=ALU.mult)
                else:
                    n = len(js)
                    h = 2 * c + hf
                    if hf == 0:
                        E("act", "activation", out=exBp[gi][:, :, 0:n, :], in_=sc2w[gi][:, :, 0:n, :], func=AF.Exp)
                        E("dve", "tensor_tensor", out=PTBp[gi][:, :, 0:n, :], in0=exBp[gi][:, :, 0:n, :],
                          in1=maskT[:, js[0]:js[0] + n, :].unsqueeze(1).broadcast_to([128, 2, n, 128]), op=ALU.mult)
                        if kind == "near":
                            E("dve", "tensor_tensor", out=PTBp[gi][:, :, 0:n, :], in0=PTBp[gi][:, :, 0:n, :],
                              in1=EbB[:, 2 - n:2, 2 * c:2 * c + 2, :].rearrange("p a b c -> p b a c"), op=ALU.mult)

            def back(item, gi):
                c, kind, js = item
                if c == "A":
                    hk = kind
                    for hh in range(4):
                        pt = PTB[2 * gi + hh % 2]
                        for jn, (jt, j) in enumerate(jts):
                            MM(oa[:, hh, :], pt[:, 2 * jn + hh // 2, :], vA[:, j, hk, :], jn == 0, jn == njt - 1,
                               inc=(hh == 3 and jn == njt - 1))
                    E("dve", "tensor_tensor", out=den[:, 0:4], in0=oa[:, :, 64], in1=esink[:, 4 * hk:4 * hk + 4],
                      op=ALU.add)
                    E("dve", "reciprocal", out=den[:, 0:4], in_=den[:, 0:4])
                    E("dve", "tensor_tensor", out=ob[:, hk * 256:(hk + 1) * 256].rearrange("p (h d) -> p h d", h=4),
                      in0=oa[:, :, 0:64], in1=den[:, 0:4].unsqueeze(2).broadcast_to([128, 4, 64]), op=ALU.mult)
                else:
                    n = len(js)
                    k4 = c // 2
                    for hf in range(2):
                        h = 2 * c + hf
                        pt = PTB[2 * gi + hf]
                        for jj, j in enumerate(js):
                            first = (c % 2 == 0 and hf == 0 and j == 0)
                            last = (c % 2 == 1 and hf == 1 and j == i)
                            MM(oa[:, h % 4, :], pt[:, jj, :], vB[:, j, h, :], first, last,
                               inc=(hf == 1 and jj == n - 1))
                    if c % 2 == 1 and kind == "near":
                        E("dve", "reciprocal", out=den[:, 4:8], in_=oa[:, :, 64])
                        E("dve", "tensor_tensor",
                          out=ob[:, 512 + k4 * 256:512 + (k4 + 1) * 256].rearrange("p (h d) -> p h d", h=4),
                          in0=oa[:, :, 0:64], in1=den[:, 4:8].unsqueeze(2).broadcast_to([128, 4, 64]), op=ALU.mult)

            gi0 = ctr["g"] % 2
            ctr["g"] += len(items)
            front(items[0], gi0)
            yield
            for k, item in enumerate(items):
                gi = (gi0 + k) % 2
                for hf in range(2):
                    mid(item, gi, hf)
                    yield
                if k + 1 < len(items):
                    front(items[k + 1], 1 - gi)
                    yield
                back(item, gi)
                yield
            for c in range(8):
                TR(otr[:, c, :], ob[:, c * 128:(c + 1) * 128], identb[:], inc=(c == 7))
            E("act", "activation", out=oT[:, :, qs], in_=otr, func=AF.Copy)
            yield

        def n_S2(i):
            nfar = max(0, i - 1)
            return 2 + 4 * (2 + 4 * ((nfar + 3) // 4 + 1))

        w_pa_r = w_pa.rearrange("(ec p) d -> p ec d", p=128)
        w_pb_r = w_pb.rearrange("(ec p) d -> p ec d", p=128)
        w_out_r = w_out.rearrange("(dc p) e -> p dc e", p=128)
        WgB0 = nc.alloc_sbuf_tensor_at("WgB0", [128, 8, 512], BF16, offset=int(q2T.manual_sbuf_range[0]))
        WpA0 = nc.alloc_sbuf_tensor_at("WpA0", [128, 4, 512], BF16, offset=int(kiT.manual_sbuf_range[0]))
        pf = {"z0": False, "r": False}

        def pf_z0():
            if not pf["z0"]:
                pf["z0"] = True
                S.dma("pool", "d_w0", [(wsl[0][:], w_in_r[:, :, 768:1280])])

        def pf_rest():
            if not pf["r"]:
                pf["r"] = True
                S.dma("pool", "d_w1", [(wsl[1][:], w_in_r[:, :, 2816:3328])])
                S.dma("pool", "d_g0", [
                    (wsl[2][:], w_in_r[:, :, 3624:3624 + 512]),
                    (WgB0[:], w_in_r[:, :, 4648:4648 + 512]),
                    (WpA0[:], w_pa_r[:, :, 0:512]),
                ])

        live = {}
        idx_done = {0: True, 1: True}

        def start(j):
            if j < _nt2 and j not in live:
                live[j] = [gen_S1(j), n_S1(j), 0]

        def adv(j, frac):
            st = live.get(j)
            if st is None:
                return
            pv = live.get(j - 1)
            while pv is not None and pv[0] is not None and not idx_done.get(j - 1, False):
                try:
                    next(pv[0])
                    pv[2] += 1
                except StopIteration:
                    pv[0] = None
            while st[0] is not None and st[2] < frac * st[1]:
                try:
                    next(st[0])
                    st[2] += 1
                except StopIteration:
                    st[0] = None
            if frac >= 1.0:
                while st[0] is not None:
                    try:
                        next(st[0])
                    except StopIteration:
                        st[0] = None

        start(0)
        adv(0, 1.0)
        for i in range(_nt2):
            start(i + 1)
            start(i + 2)
            if i == NT - 2:
                pf_z0()
            if i == NT - 1:
                pf_rest()
            g2 = gen_S2(i)
            n2 = n_S2(i)
            s = 0
            for _ in g2:
                s += 1
                p = min(1.0, s / n2)
                adv(i + 1, min(1.0, 0.5 + 0.5 * p / 0.95))
                adv(i + 2, 0.5 * p)
            adv(i + 1, 1.0)
            live.pop(i + 1, None)

        dump("d_oT", oT[:], [128, 8, S_LEN], BF16)
        if stop == 2:
            return finish()
        pf_z0()
        pf_rest()
        A3 = Arena(nc, R_QKV.base, R_QKV.size, "p3")
        A3b = Arena(nc, R_SP.base, R_SP.size, "p3b")
        mT = A3.alloc("mT", [128, 8, S_LEN], BF16)
        gws = [{"gA": wsl[2], "gB": WgB0, "pA": WpA0, "pB": None},
               {"gA": A3.alloc("WgA1", [128, 8, 512], BF16), "gB": A3.alloc("WgB1", [128, 8, 512], BF16),
                "pA": A3.alloc("WpA1", [128, 4, 512], BF16), "pB": A3.alloc("WpB1", [128, 4, 512], BF16)}]
        gws[0]["pB"] = A3.alloc("WpB0", [128, 4, 512], BF16)
        xts2 = [A3.alloc("xo%d" % i, [128, D], F32) for i in range(2)]
        tmpz = [A3b.alloc("tmpz%d" % i, [128, 512], BF16) for i in range(2)]
        sA = [A3b.alloc("sA%d" % i, [128, 512], F32) for i in range(2)]
        sB = [A3b.alloc("sB%d" % i, [128, 512], F32) for i in range(2)]
        tA = [A3b.alloc("tA%d" % i, [128, 512], F32) for i in range(2)]
        S.dma("pool", "d_g0b", [(gws[0]["pB"][:], w_pb_r[:, :, 0:512])])
        S.dma("pool", "d_g1", [
            (gws[1]["pA"][:], w_pa_r[:, :, 512:1024]), (gws[1]["pB"][:], w_pb_r[:, :, 512:1024]),
            (gws[1]["gA"][:], w_in_r[:, :, 3624 + 512:3624 + 1024]),
            (gws[1]["gB"][:], w_in_r[:, :, 4648 + 512:4648 + 1024]),
        ])

        zn = 0
        for zi, (col0, cbase) in enumerate(((768, 0), (2816, 4))):
            wz = wsl[zi]
            for cc in range(4):
                for tg in range(4):
                    pz = pbf[zn % 2]
                    tz = tmpz[zn % 2]
                    zn += 1
                    for kc in range(8):
                        MM(pz, wz[:, kc, cc * 128:(cc + 1) * 128], hT[:, kc, tsl(tg)], kc == 0, kc == 7)
                    E("act", "activation", out=tz[:], in_=pz, func=AF.Silu)
                    E("dve", "tensor_tensor", out=oT[:, cbase + cc, tsl(tg)], in0=oT[:, cbase + cc, tsl(tg)],
                      in1=tz[:], op=ALU.mult)
        wo = [wsl[0], wsl[1]]
        for hf in range(2):
            S.dma("pool", "d_w%d" % hf, [(wsl[hf][:], w_out_r[:, :, hf * 512:(hf + 1) * 512])])

        gn = 0
        for dg in range(2):
            gw = gws[dg]
            for dcl in range(4):
                dc = dg * 4 + dcl
                cs = slice(dcl * 128, (dcl + 1) * 128)
                for tg in range(4):
                    k = gn % 2
                    gn += 1
                    bPA, bPB, bgA, bgB = (2, 3, 4, 5) if k == 0 else (0, 1, 6, 7)
                    for kc in range(8):
                        MM(pbf[bgA], gw["gA"][:, kc, cs], hT[:, kc, tsl(tg)], kc == 0, kc == 7)
                    for kc in range(8):
                        MM(pbf[bgB], gw["gB"][:, kc, cs], hT[:, kc, tsl(tg)], kc == 0, kc == 7)
                    for ec in range(4):
                        MM(pbf[bPA], gw["pA"][:, ec, cs], oT[:, ec, tsl(tg)], ec == 0, ec == 3)
                    for ec in range(4):
                        MM(pbf[bPB], gw["pB"][:, ec, cs], oT[:, 4 + ec, tsl(tg)], ec == 0, ec == 3)
                    E("act", "activation", out=sA[k][:], in_=pbf[bgA], func=AF.Sigmoid)
                    E("act", "activation", out=sB[k][:], in_=pbf[bgB], func=AF.Sigmoid)
                    E("dve", "tensor_tensor", out=tA[k][:], in0=sA[k][:], in1=pbf[bPA], op=ALU.mult)
                    E("dve", "tensor_tensor", out=sB[k][:], in0=sB[k][:], in1=pbf[bPB], op=ALU.mult)
                    E("dve", "tensor_tensor", out=mT[:, dc, tsl(tg)], in0=tA[k][:], in1=sB[k][:], op=ALU.add)

        okeys = []
        for ti in range(NT):
            tok = slice(ti * 128, (ti + 1) * 128)
            xo = xts2[ti % 2]
            S.dma("sp", "d_xo%d" % (ti % 2), [(xo[:], x[tok, :])])
            for hf in range(2):
                po = pbf[2 * (ti % 2) + hf]
                for dc in range(8):
                    MM(po, mT[:, dc, tok], wo[hf][:, dc, :], dc == 0, dc == 7)
                E("dve", "tensor_tensor", out=xo[:, hf * 512:(hf + 1) * 512], in0=po,
                  in1=xo[:, hf * 512:(hf + 1) * 512], op=ALU.add)
            key = "d_o%d" % (ti % 2)
            if key not in okeys:
                okeys.append(key)
            S.dma("sp", key, [(out[tok, :], xo[:])])
        dkeys.extend(okeys)
        return finish()


def _t5_bucket_np(n):
    n = np.maximum(n, 0)
    nf = np.maximum(n, 1).astype(np.float32)
    large = 16 + (np.log(nf / np.float32(16)) / np.float32(math.log(128 / 16)) * np.float32(16)).astype(np.int32)
    large = np.minimum(large, 31)
    return np.where(n < 16, n, large)


def _host_consts(norm_g, qnorm_a, knorm_a, sinks_a, qnorm_b, knorm_b, rel_bias):
    f = np.float32
    s = np.arange(128)[:, None]
    t = np.arange(128)[None, :]
    d0 = t - s
    d1 = t + 128 - s
    b0 = _t5_bucket_np(d0)
    b1 = _t5_bucket_np(d1)
    ta = rel_bias[:, :8]
    tb = rel_bias[:, 8:]

    def gath(tab):
        a = np.stack([tab[b0], tab[b1]], axis=1)
        return np.ascontiguousarray(a.transpose(0, 1, 3, 2)).reshape(128, -1).astype(f)

    mA = np.stack([(s <= t), (s > t)], axis=1).astype(f).reshape(128, -1)
    tt = np.arange(128)[:, None]
    sx = np.arange(128)[None, :]
    tril = (sx <= tt).astype(f)
    bd = np.zeros((128, 128), f)
    bd[:64, :64] = 1
    bd[64:, 64:] = 1
    return {
        "c_gT": np.ascontiguousarray(norm_g.reshape(8, 128).T).astype(f),
        "c_gq": np.ascontiguousarray(np.stack([np.tile(qnorm_a, 2), np.tile(knorm_a, 2),
                                               np.tile(qnorm_b, 2), np.tile(knorm_b, 2)], axis=1)).astype(f),
        "c_sink": np.ascontiguousarray(np.broadcast_to(sinks_a[None, :], (128, 8))).astype(f),
        "c_cfar": np.ascontiguousarray(np.broadcast_to(tb[31][None, :], (128, 8))).astype(f),
        "c_biasA": gath(ta),
        "c_biasB": gath(tb),
        "c_mA": mA,
        "c_negm": np.where(sx <= tt, 0.0, -BIG).astype(f),
        "c_tril": tril,
        "c_ident": np.eye(128, dtype=f),
        "c_bd": bd,
        "c_pow2": np.ascontiguousarray(np.broadcast_to((0.5 ** np.arange(NIT))[None, :], (128, NIT))).astype(f),
        "c_npow2": np.ascontiguousarray(np.broadcast_to((-0.5 * 0.5 ** np.arange(NIT))[None, :], (128, NIT))).astype(f),
        "c_cthr": np.ascontiguousarray(np.broadcast_to(((np.arange(NT) + 1) * 128 - 511.5)[None, :], (128, NT))).astype(f),
    }


_CACHE = {}


def kernel(x, norm_g, w_in, qnorm_a, knorm_a, sinks_a, qnorm_b, knorm_b, rel_bias, w_proj_a, w_proj_b, w_out):
    a = lambda v: np.ascontiguousarray(np.asarray(v, dtype=np.float32))
    x = a(x)
    consts = _host_consts(a(norm_g)[0], a(qnorm_a)[0], a(knorm_a)[0], a(sinks_a)[0], a(qnorm_b)[0],
                          a(knorm_b)[0], a(rel_bias))
    shared = {"w_in": a(w_in)[0], "w_pa": a(w_proj_a)[0], "w_pb": a(w_proj_b)[0], "w_out": a(w_out)[0]}
    shared.update(consts)
    if "nc" not in _CACHE:
        _CACHE["nc"] = build_program()
    nc = _CACHE["nc"]
    in_maps = [dict(shared, x=x[b]) for b in range(8)]
    res = run_bass_kernel_spmd(nc, in_maps, core_ids=list(range(8)))
    return np.stack([r["out"] for r in res.results], axis=0).astype(np.float32)
```

```python
import math
from contextlib import ExitStack

import numpy as np
import concourse.bass as bass
import concourse.mybir as mybir
from concourse.bass_utils import run_bass_kernel_spmd

F32 = mybir.dt.float32
BF16 = mybir.dt.bfloat16
ALU = mybir.AluOpType
AF = mybir.ActivationFunctionType
AX = mybir.AxisListType

S_LEN = 2048
D = 1024
NT = 16
INW = 5672
NIT = 16
K0 = 10
BIG = 1.0e30
ENGS = ("pe", "dve", "act", "pool", "sp")
_ESZ = {F32: 4, BF16: 2}
PAGE = 2048


def _esize(dt):
    return _ESZ[dt]


def _box(ap):
    t = ap.tensor
    tn = type(t).__name__
    if tn.startswith("DRam"):
        return None
    dims = list(ap.ap)
    off = int(ap.offset)
    es = _esize(ap.dtype)
    pstride = int(dims[0][0])
    npart = int(dims[0][1])
    if pstride <= 0:
        pstride = 1
        for s in list(t.shape)[1:]:
            pstride *= int(s)
    p0 = off // pstride
    f0 = off % pstride
    f1 = f0
    for st, n in dims[1:]:
        f1 += abs(int(st)) * (int(n) - 1)
    if tn.startswith("PSum"):
        base = 1 << 24
        base += int(t.name[2:]) * 2048
    else:
        base = int(t.manual_sbuf_range[0])
    return (p0, p0 + npart, base + f0 * es, base + (f1 + 1) * es)


def _ovl(a, b):
    return a[0] < b[1] and b[0] < a[1] and a[2] < b[3] and b[2] < a[3]


def _contains(a, b):
    return a[0] <= b[0] and a[1] >= b[1] and a[2] <= b[2] and a[3] >= b[3]


class Sched:
    def __init__(self, nc, stack):
        self.nc = nc
        self.stack = stack
        self.q = {e: [] for e in ENGS}
        self.sems = {}
        self.cnt = {}
        self.unit = {}
        for e in ENGS:
            self._mksem(e, 1)
        self.seen = {e: {} for e in ENGS}
        self.pages = {}
        self.n_inst = 0

    def _mksem(self, key, unit):
        self.sems[key] = self.stack.enter_context(self.nc.semaphore("s_" + key))
        self.cnt[key] = 0
        self.unit[key] = unit

    def _pages(self, box):
        return range(box[2] // PAGE, (box[3] - 1) // PAGE + 1)

    def _scan(self, box, want_reads, deps, eng, raw):
        for pg in self._pages(box):
            for r in self.pages.get(pg, ()):
                if (want_reads or r[1] == "w") and _ovl(r[0], box):
                    k, i = r[2]
                    if k == eng and eng == "pe":
                        continue
                    if deps.get(k, 0) < i:
                        deps[k] = i

    def _record(self, prod, rboxes, wboxes):
        for box in wboxes:
            for pg in self._pages(box):
                lst = self.pages.setdefault(pg, [])
                lst[:] = [r for r in lst if not _contains(box, r[0])]
                lst.append((box, "w", prod))
        for box in rboxes:
            for pg in self._pages(box):
                lst = self.pages.setdefault(pg, [])
                lst[:] = [r for r in lst if not (r[1] == "r" and r[2][0] == prod[0] and _contains(box, r[0]))]
                lst.append((box, "r", prod))

    def _emit_waits(self, eng, deps):
        seen = self.seen[eng]
        for k, i in deps.items():
            if k == eng and eng == "pe":
                continue
            if seen.get(k, 0) >= i:
                continue
            seen[k] = i
            self.q[eng].append(("wait", k, i * self.unit[k]))

    def op(self, eng, fn, reads=(), writes=(), inc=True):
        rb = [b for b in (_box(a) for a in reads) if b is not None]
        wb = [b for b in (_box(a) for a in writes) if b is not None]
        deps = {}
        for b in rb:
            self._scan(b, False, deps, eng, True)
        for b in wb:
            self._scan(b, True, deps, eng, False)
        self._emit_waits(eng, deps)
        idx = self.cnt[eng] + 1
        if inc:
            self.cnt[eng] = idx
        self.q[eng].append(("inst", fn, inc, eng))
        self._record((eng, idx), rb, wb)
        self.n_inst += 1

    def dma(self, qeng, semkey, pairs, **kw):
        if semkey not in self.sems:
            self._mksem(semkey, 16)
        rb = [b for b in (_box(p[1]) for p in pairs) if b is not None]
        wb = [b for b in (_box(p[0]) for p in pairs) if b is not None]
        deps = {}
        for b in rb:
            self._scan(b, False, deps, "__dma__", False)
        for b in wb:
            self._scan(b, True, deps, "__dma__", False)
        self._emit_waits(qeng, deps)
        idx = self.cnt[semkey] + len(pairs)
        self.cnt[semkey] = idx
        for (o, i) in pairs:
            self.q[qeng].append(("dma", o, i, semkey, kw))
        self._record((semkey, idx), rb, wb)
        self.n_inst += len(pairs)

    def wait_all(self, eng, keys):
        for k in keys:
            if self.cnt[k] > self.seen[eng].get(k, 0):
                self.seen[eng][k] = self.cnt[k]
                self.q[eng].append(("wait", k, self.cnt[k] * self.unit[k]))

    def emit(self):
        nc = self.nc
        sems = self.sems
        q = self.q

        def run(engobj, items):
            for it in items:
                if it[0] == "wait":
                    engobj.wait_ge(sems[it[1]], it[2])
                elif it[0] == "inst":
                    ins = it[1](engobj)
                    if it[2]:
                        ins.then_inc(sems[it[3]], 1)
                else:
                    _, o, i, sk, kw = it
                    engobj.dma_start(out=o, in_=i, **kw).then_inc(sems[sk], 16)

        with nc.Block() as block:
            @block.tensor
            def _(e):
                run(e, q["pe"])

            @block.vector
            def _(e):
                run(e, q["dve"])

            @block.scalar
            def _(e):
                run(e, q["act"])

            @block.gpsimd
            def _(e):
                run(e, q["pool"])

            @block.sync
            def _(e):
                run(e, q["sp"])


class Arena:
    def __init__(self, nc, base, size, tag):
        self.nc, self.base, self.size, self.tag, self.off = nc, base, size, tag, 0

    def alloc(self, name, shape, dt):
        n = 1
        for s in shape[1:]:
            n *= s
        nb = n * _esize(dt)
        nb = (nb + 31) // 32 * 32
        assert self.off + nb <= self.size, (self.tag, name, self.off, nb, self.size)
        t = self.nc.alloc_sbuf_tensor_at(name, list(shape), dt, offset=self.base + self.off)
        self.off += nb
        return t


def build_program(stop=99, dbg=None):
    nc = bass.Bass("TRN2", target_bir_lowering=False)
    dr = lambda name, shape, kind="ExternalInput": nc.dram_tensor(name, shape, F32, kind=kind).ap()
    x = dr("x", [S_LEN, D])
    w_in = dr("w_in", [D, INW])
    w_pa = dr("w_pa", [512, D])
    w_pb = dr("w_pb", [512, D])
    w_out = dr("w_out", [D, D])
    c_gT = dr("c_gT", [128, 8])
    c_gq = dr("c_gq", [128, 4])
    c_sink = dr("c_sink", [128, 8])
    c_cfar = dr("c_cfar", [128, 8])
    c_biasA = dr("c_biasA", [128, 2 * 8 * 128])
    c_biasB = dr("c_biasB", [128, 2 * 8 * 128])
    c_mA = dr("c_mA", [128, 2 * 128])
    c_negm = dr("c_negm", [128, 128])
    c_tril = dr("c_tril", [128, 128])
    c_ident = dr("c_ident", [128, 128])
    c_bd = dr("c_bd", [128, 128])
    c_pow2 = dr("c_pow2", [128, NIT])
    c_npow2 = dr("c_npow2", [128, NIT])
    c_cthr = dr("c_cthr", [128, NT])
    out = dr("out", [S_LEN, D], kind="ExternalOutput")

    with ExitStack() as st:
        S = Sched(nc, st)

        def E(eng, meth, inc=True, **kw):
            reads, writes = [], []
            for k, v in kw.items():
                if hasattr(v, "tensor") and hasattr(v, "ap"):
                    (writes if k in ("out", "accum_out", "ap") else reads).append(v)
            S.op(eng, lambda e: getattr(e, meth)(**kw), reads=reads, writes=writes, inc=inc)

        def MM(outp, lhsT, rhs, start, stop, inc=None, **kw):
            if inc is None:
                inc = stop
            S.op("pe", lambda e: e.matmul(outp, lhsT=lhsT, rhs=rhs, start=start, stop=stop, **kw),
                 reads=[lhsT, rhs], writes=[outp], inc=inc)

        def TR(outp, in_, ident, inc=True):
            S.op("pe", lambda e: e.transpose(out=outp, in_=in_, identity=ident),
                 reads=[in_, ident], writes=[outp], inc=inc)

        dkeys = []

        def dump(name, sb_ap, shape, dt):
            if dbg is None:
                return
            d = nc.dram_tensor(name, list(shape), dt, kind="ExternalOutput").ap()
            S.dma("sp", "d_dbg", [(d, sb_ap)])
            dbg.append(name)
            if "d_dbg" not in dkeys:
                dkeys.append("d_dbg")

        def finish():
            S.wait_all("sp", dkeys)
            S.emit()
            return nc

        base0 = (int(nc.sbuf_base) + 63) // 64 * 64
        top = int(nc.sbuf_top)
        cur = [base0]

        def region(size, tag):
            a = Arena(nc, cur[0], size, tag)
            cur[0] += size
            assert cur[0] <= top, (tag, cur[0], top)
            return a

        R_CONST = region(12288, "const")
        R_HT = region(32768, "hT")
        R_W = region(24576, "w")
        R_OT = region(32768, "oT")
        R_QKV = region(90432, "qkv")
        R_SP = region((top - cur[0]) // 64 * 64, "spare")

        pb0 = nc.alloc_psum_tensor("pb0", [128, 512], F32)
        pb1 = nc.alloc_psum_tensor("pb1", [128, 512], F32)
        pb2 = nc.alloc_psum_tensor("pb2", [128, 1024], F32)
        pb4 = nc.alloc_psum_tensor("pb4", [128, 1024], F32)
        pb6 = nc.alloc_psum_tensor("pb6", [128, 512], F32)
        pb7 = nc.alloc_psum_tensor("pb7", [128, 512], F32)
        pbf = [pb0.ap(), pb1.ap(), pb2.ap()[:, 0:512], pb2.ap()[:, 512:1024], pb4.ap()[:, 0:512],
               pb4.ap()[:, 512:1024], pb6.ap(), pb7.ap()]
        _h2, _h4 = pb2.bitcast(BF16).ap(), pb4.bitcast(BF16).ap()
        pbh = [pb0.bitcast(BF16).ap(), pb1.bitcast(BF16).ap(), _h2[:, 0:1024], _h2[:, 1024:2048], _h4[:, 0:1024],
               _h4[:, 1024:2048], pb6.bitcast(BF16).ap(), pb7.bitcast(BF16).ap()]
        sc2w = [pb2.ap().rearrange("p (f a b) -> p f a b", f=2, a=4), pb4.ap().rearrange("p (f a b) -> p f a b", f=2, a=4)]

        identb = R_CONST.alloc("identb", [128, 128], BF16)
        bdb = R_CONST.alloc("bdb", [128, 128], BF16)
        EbA = R_CONST.alloc("EbA", [128, 2, 8, 128], BF16)
        EbB = R_CONST.alloc("EbB", [128, 2, 8, 128], BF16)
        negm = R_CONST.alloc("negm", [128, 128], F32)
        trilb = R_CONST.alloc("trilb", [128, 128], BF16)
        gT = R_CONST.alloc("gT", [128, 8], F32)
        gq = R_CONST.alloc("gq", [128, 4], F32)
        esink = R_CONST.alloc("esink", [128, 8], F32)
        cfar = R_CONST.alloc("cfar", [128, 8], F32)
        epsT = R_CONST.alloc("epsT", [128, 1], F32)
        pow2 = R_CONST.alloc("pow2", [128, NIT], F32)
        npow2 = R_CONST.alloc("npow2", [128, NIT], F32)
        cthr = R_CONST.alloc("cthr", [128, NT], F32)
        sgn = R_CONST.alloc("sgn", [128, NT, 8], F32)
        wab = R_CONST.alloc("wab", [128, NT, 8], F32)
        ss = R_CONST.alloc("ss", [128, NT], F32)
        sd = R_CONST.alloc("sd", [128, NT], F32)
        rstd = R_CONST.alloc("rstd", [128, NT], F32)
        smalls = R_CONST.alloc("smalls", [128, 96], F32)
        Rrs = [smalls[:, 0:1], smalls[:, 7:8]]
        mid = smalls[:, 1:2]
        cntv = smalls[:, 2:3]
        dirv = smalls[:, 3:4]
        den = smalls[:, 8:16]
        Rk = smalls[:, 16:16 + NIT]
        negRks = [smalls[:, 32:32 + NIT], smalls[:, 48:48 + NIT]]
        csum = smalls[:, 4:5]
        dirS = smalls[:, 5:6]
        nmids = [smalls[:, 6:7], smalls[:, 64:65]]
        mids = [smalls[:, 1:2], smalls[:, 65:66]]
        cntvs = [smalls[:, 2:3], smalls[:, 66:67]]
        dirvs = [smalls[:, 3:4], smalls[:, 67:68]]
        csums = [smalls[:, 4:5], smalls[:, 68:69]]
        dirSs = [smalls[:, 5:6], smalls[:, 69:70]]
        Rks = [smalls[:, 16:16 + NIT], smalls[:, 70:70 + NIT]]

        hT = R_HT.alloc("hT", [128, 8, S_LEN], BF16)
        wsl = [R_W.alloc("wsl%d" % i, [128, 8, 512], BF16) for i in range(3)]
        oT = R_OT.alloc("oT", [128, 8, S_LEN], BF16)

        qaT = R_QKV.alloc("qaT", [128, 4, S_LEN], BF16)
        kaT2 = R_QKV.alloc("kaT2", [128, 2, S_LEN], BF16)
        vA = R_QKV.alloc("vA", [128, NT, 2, 65], BF16)
        qbT = R_QKV.alloc("qbT", [128, 4, S_LEN], BF16)
        kbT = R_QKV.alloc("kbT", [128, 4, S_LEN], BF16)
        vB = R_QKV.alloc("vB", [128, NT, 8, 65], BF16)
        q2T = R_QKV.alloc("q2T", [128, 2, S_LEN], BF16)
        kiT = R_QKV.alloc("kiT", [128, S_LEN], BF16)

        A1 = Arena(nc, R_OT.base, R_OT.size, "p01")
        xts = [A1.alloc("xt%d" % i, [128, D], F32) for i in range(2)]
        hns = [A1.alloc("hn%d" % i, [128, D], BF16) for i in range(2)]
        sqb = [A1.alloc("sqb%d" % i, [128, 512], BF16) for i in range(2)]
        sdb = [A1.alloc("sdb%d" % i, [128, 512], F32) for i in range(2)]
        q2b = [A1.alloc("q2b%d" % i, [128, 256], BF16) for i in range(2)]
        stgA = A1.alloc("stgA", [128, 2, 8, 128], F32)
        _sb = int(stgA.manual_sbuf_range[0])
        xts += [nc.alloc_sbuf_tensor_at("xt%d" % (2 + k), [128, D], F32, offset=_sb + 4096 * k) for k in range(2)]
        stgm = A1.alloc("stgm", [128, 2, 128], F32)
        stgi = A1.alloc("stgi", [128, 128], F32)
        stgd = A1.alloc("stgd", [128, 128], F32)
        stgt = A1.alloc("stgt", [128, 128], F32)

        xpre = set()
        for _i in range(2):
            S.dma("sp", "d_x%d" % _i, [(xts[_i][:], x[_i * 128:(_i + 1) * 128, :])])
            xpre.add(_i)
        S.dma("sp", "d_c", [
            (gT[:], c_gT[:, :]), (gq[:], c_gq[:, :]), (esink[:], c_sink[:, :]), (cfar[:], c_cfar[:, :]),
            (negm[:], c_negm[:, :]), (pow2[:], c_pow2[:, :]), (npow2[:], c_npow2[:, :]), (cthr[:], c_cthr[:, :]),
            (stgm[:].rearrange("p a b -> p (a b)"), c_mA[:, :]),
            (stgi[:], c_ident[:, :]), (stgd[:], c_bd[:, :]), (stgt[:], c_tril[:, :]),
        ])
        E("dve", "memset", ap=epsT[:], constant=1e-6)
        E("dve", "tensor_copy", out=identb[:], in_=stgi[:])
        E("dve", "tensor_copy", out=bdb[:], in_=stgd[:])
        E("dve", "tensor_copy", out=trilb[:], in_=stgt[:])
        E("pool", "memset", ap=vA[:, :, :, 64:65], constant=1.0)
        E("pool", "memset", ap=vB[:, :, :, 64:65], constant=1.0)
        E("dve", "tensor_scalar", out=gq[:, 0:1], in0=gq[:, 0:1], scalar1=0.125, scalar2=None, op0=ALU.mult)
        E("dve", "tensor_scalar", out=gq[:, 2:3], in0=gq[:, 2:3], scalar1=0.125, scalar2=None, op0=ALU.mult)
        E("act", "activation", out=esink[:], in_=esink[:], func=AF.Exp)
        w_in_r = w_in.rearrange("(kc p) c -> p kc c", p=128)
        wstate = {"n": 0}

        def load_w(pieces):
            s = wstate["n"] % 3
            wstate["n"] += 1
            t = wsl[s]
            S.dma("pool", "d_w%d" % s,
                  [(t[:, :, d0:d0 + n], w_in_r[:, :, s0:s0 + n]) for (d0, n, s0) in pieces])
            return t

        def p0A(i):
            xt = xts[i % 4]
            hn = hns[i % 2]
            if i not in xpre:
                S.dma("sp", "d_x%d" % (i % 4), [(xt[:], x[i * 128:(i + 1) * 128, :])])
            E("act", "activation", out=hn[:], in_=xt[:], func=AF.Square, accum_out=ss[:, i:i + 1])
            E("act", "activation", out=sd[:, i:i + 1], in_=ss[:, i:i + 1], func=AF.Ln,
              scale=1.0 / D, bias=epsT[:])
            E("act", "activation", out=rstd[:, i:i + 1], in_=sd[:, i:i + 1], func=AF.Exp, scale=-0.5)
            E("pool", "tensor_scalar", out=hn[:], in0=xt[:], scalar1=rstd[:, i:i + 1], scalar2=1.0,
              op0=ALU.mult, op1=ALU.mult)
            ptr = pbh[i % 2][:, 0:1024].rearrange("p (a b) -> p a b", a=8)
            for kc in range(8):
                TR(ptr[:, kc, :], hn[:, kc * 128:(kc + 1) * 128], identb[:], inc=(kc == 7))

        def p0B(i):
            ptr = pbh[i % 2][:, 0:1024].rearrange("p (a b) -> p a b", a=8)
            E("dve", "tensor_tensor", out=hT[:, :, i * 128:(i + 1) * 128], in0=ptr,
              in1=gT[:, :].unsqueeze(2).broadcast_to([128, 8, 128]), op=ALU.mult)

        p0s = {"a": 0, "b": 0}

        def p0_adv():
            if p0s["a"] < NT:
                p0A(p0s["a"])
                p0s["a"] += 1
            if p0s["b"] < p0s["a"] - 1 or (p0s["a"] == NT and p0s["b"] < NT):
                p0B(p0s["b"])
                p0s["b"] += 1

        def p0_need(ntiles):
            while p0s["b"] < ntiles:
                if p0s["a"] < NT and p0s["a"] <= p0s["b"] + 1:
                    p0A(p0s["a"])
                    p0s["a"] += 1
                else:
                    p0B(p0s["b"])
                    p0s["b"] += 1

        def tsl(tg):
            return slice(tg * 512, (tg + 1) * 512)

        fmn = [0]
        pend = [None]

        def fm_mm(ws, ccol, tg):
            n = fmn[0]
            fmn[0] += 1
            acc = pbf[(2, 3, 6, 7)[n % 4]]
            for kc in range(8):
                MM(acc, ws[:, kc, ccol:ccol + 128], hT[:, kc, tsl(tg)], kc == 0, kc == 7)
            return n

        def fm_post(n, dst, gain, norm):
            acc = pbf[(2, 3, 6, 7)[n % 4]]
            if norm:
                sq = sqb[n % 2]
                E("act", "activation", out=sq[:], in_=acc, func=AF.Square)
                ssb = pbf[4 + n % 2]
                MM(ssb, bdb[:], sq[:], True, True)
                sdt = sdb[n % 2]
                E("act", "activation", out=sdt[:], in_=ssb, func=AF.Ln, scale=1.0 / 64, bias=epsT[:])
                E("act", "activation", out=sdt[:], in_=sdt[:], func=AF.Exp, scale=-0.5)
                E("dve", "scalar_tensor_tensor", out=dst, in0=acc, scalar=gain, in1=sdt[:],
                  op0=ALU.mult, op1=ALU.mult)
            else:
                E("act", "activation", out=dst, in_=acc, func=AF.Copy)

        pendq = []

        def fm(ws, ccol, dst, gain, norm, tg):
            n = fm_mm(ws, ccol, tg)
            if len(pendq) == 2:
                fm_post(*pendq.pop(0))
            pendq.append((n, dst, gain, norm))

        def fm_flush():
            while pendq:
                fm_post(*pendq.pop(0))

        ws = load_w([(0, 512, 0)])
        p0_need(NT)
        S.dma("sp", "d_c2", [(stgA[:].rearrange("p a h t -> p (a h t)"), c_biasA[:, :])])
        E("act", "activation", out=stgA[:], in_=stgA[:], func=AF.Exp)
        for jt in range(2):
            E("dve", "tensor_tensor", out=EbA[:, jt, :, :], in0=stgA[:, jt, :, :],
              in1=stgm[:, jt, :].unsqueeze(1).broadcast_to([128, 8, 128]), op=ALU.mult)
        stgB = nc.alloc_sbuf_tensor_at("stgB", [128, 2, 8, 128], F32, offset=R_SP.base)
        S.dma("sp", "d_c3", [(stgB[:].rearrange("p a h t -> p (a h t)"), c_biasB[:, :])])
        for jt in range(2):
            E("dve", "tensor_tensor", out=stgB[:, jt, :, :], in0=stgB[:, jt, :, :],
              in1=cfar[:, :].unsqueeze(2).broadcast_to([128, 8, 128]), op=ALU.subtract)
        for jt in range(2):
            E("act", "activation", out=EbB[:, 1 - jt, :, :], in_=stgB[:, jt, :, :], func=AF.Exp)

        ws_qb = load_w([(0, 512, 1280)])
        ws_kb = load_w([(0, 512, 1792)])
        for tg in range(4):
            for c in range(4):
                fm(ws, c * 128, qaT[:, c, tsl(tg)], gq[:, 0:1], True, tg)
        ws_k = load_w([(0, 64, 512), (64, 64, 512), (128, 64, 576), (192, 64, 576),
                       (256, 32, 3584), (288, 32, 3584), (320, 32, 3584), (352, 32, 3584)])
        for c in range(4):
            for tg in range(4):
                fm(ws_qb, c * 128, qbT[:, c, tsl(tg)], gq[:, 2:3], True, tg)
        ws_vb = load_w([(0, 512, 2304)])
        for c in range(4):
            for tg in range(4):
                fm(ws_kb, c * 128, kbT[:, c, tsl(tg)], gq[:, 3:4], True, tg)
        ws_g5 = load_w([(0, 128, 640), (128, 256, 3328), (384, 8, 3616)])
        for c in range(2):
            for tg in range(4):
                fm(ws_k, c * 128, kaT2[:, c, tsl(tg)], gq[:, 1:2], True, tg)
        for tg in range(4):
            fm(ws_k, 256, kiT[:, tsl(tg)], None, False, tg)
        fm_flush()

        def tm_mm(ti):
            accv = pbf[2 + 2 * (ti % 2)]
            accq = pbf[3 + 2 * (ti % 2)]
            tok = slice(ti * 128, (ti + 1) * 128)
            for kc in range(8):
                MM(accv, hT[:, kc, tok], ws_vb[:, kc, 0:512], kc == 0, kc == 7)
            for kc in range(8):
                MM(accq[:, 0:392], hT[:, kc, tok], ws_g5[:, kc, 0:392], kc == 0, kc == 7)

        def tm_post(ti):
            accv = pbf[2 + 2 * (ti % 2)]
            accq = pbf[3 + 2 * (ti % 2)]
            tok = slice(ti * 128, (ti + 1) * 128)
            E("act", "activation", out=vB[:, ti, :, 0:64], in_=accv.rearrange("p (h d) -> p h d", h=8), func=AF.Copy)
            E("act", "activation", out=vA[:, ti, :, 0:64],
              in_=accq[:, 0:128].rearrange("p (h d) -> p h d", h=2), func=AF.Copy)
            E("act", "activation", out=sgn[:, ti, :], in_=accq[:, 384:392], func=AF.Sign)
            E("dve", "scalar_tensor_tensor", out=wab[:, ti, :], in0=accq[:, 384:392], scalar=0.0625,
              in1=sgn[:, ti, :], op0=ALU.mult, op1=ALU.mult)
            qb2 = q2b[ti % 2]
            E("dve", "tensor_tensor", out=qb2[:].rearrange("p (h e) -> p h e", h=8),
              in0=accq[:, 128:384].rearrange("p (h e) -> p h e", h=8),
              in1=wab[:, ti, :].unsqueeze(2).broadcast_to([128, 8, 32]), op=ALU.mult)
            ptr = pbh[ti % 2][:, 0:256].rearrange("p (a b) -> p a b", a=2)
            for g in range(2):
                TR(ptr[:, g, :], qb2[:, g * 128:(g + 1) * 128], identb[:], inc=(g == 1))
            E("act", "activation", out=q2T[:, :, tok], in_=ptr, func=AF.Copy)

        tm_mm(0)
        for ti in range(NT):
            if ti + 1 < NT:
                tm_mm(ti + 1)
            tm_post(ti)

        dump("d_hT", hT[:], [128, 8, S_LEN], BF16)
        if stop == 0:
            return finish()
        for nm, t in (("d_qaT", qaT), ("d_kaT2", kaT2), ("d_vA", vA), ("d_qbT", qbT), ("d_kbT", kbT), ("d_vB", vB),
                      ("d_q2T", q2T), ("d_kiT", kiT), ("d_sgn", sgn)):
            dump(nm, t[:], [int(v) for v in t.shape], t.dtype)
        if stop == 1:
            return finish()
        import os
        _nt2 = int(os.environ.get("DBG_NT", NT))
        A2 = Arena(nc, R_W.base, R_W.size, "p2a")
        A2b = Arena(nc, R_SP.base, R_SP.size, "p2b")
        scoresb = [A2.alloc("scores%d" % k, [128, S_LEN], F32) for k in range(2)]
        masktb = [A2.alloc("maskt%d" % k, [128, S_LEN], BF16) for k in range(2)]
        maskTs = [A2b.alloc("maskT%d" % k, [128, NT, 128], BF16) for k in range(2)]
        exBp = [A2b.alloc("exBp%d" % i, [128, 2, 4, 128], BF16) for i in range(2)]
        exB = [exBp[i // 2][:, i % 2, :, :] for i in range(4)]
        PTBp = [A2b.alloc("PTBp%d" % i, [128, 2, 4, 128], BF16) for i in range(2)]
        PTB = [PTBp[i // 2][:, i % 2, :, :] for i in range(4)]
        ob = A2b.alloc("ob", [128, D], BF16)

        oaccv = [pbf[6][:, 0:260].rearrange("p (h d) -> p h d", h=4) for k in range(2)]
        sacc = pbf[7]
        scb2 = [[pbf[2 + 2 * s_ + k].rearrange("p (a b) -> p a b", a=4) for k in range(2)] for s_ in range(2)]
        scb = scb2[1]
        mtr = pbh[1][:, 0:1024].rearrange("p (a b) -> p a b", a=8)
        otr = pbh[2][:, 0:1024].rearrange("p (a b) -> p a b", a=8)
        ctr = {"g": 0}

        def gen_S1(i):
            qs = slice(i * 128, (i + 1) * 128)
            nk = (i + 1) * 128
            par = i % 2
            scores = scoresb[par]
            maskt = masktb[par]
            junk = maskt
            maskT = maskTs[par]
            if i >= 2:
                Rb = [maskt[:, 0:512], maskt[:, 512:1024]]
                Dh = maskt[:, 1024:2048].rearrange("p (h c) -> p h c", h=8)
                E("dve", "tensor_tensor", out=Dh, in0=identb[:].unsqueeze(1).broadcast_to([128, 8, 128]),
                  in1=sgn[:, i, :].unsqueeze(2).broadcast_to([128, 8, 128]), op=ALU.mult)
                work = [(ch, h) for ch in range((nk + 511) // 512) for h in range(8)]

                def idx_front(ch, h):
                    cw = min(512, nk - ch * 512)
                    csl = slice(ch * 512, ch * 512 + cw)
                    g, r = h // 4, h % 4
                    ip = pbf[h % 2]
                    MM(ip[:, 0:cw], q2T[32 * r:32 * r + 32, g, qs], kiT[32 * r:32 * r + 32, csl], True, True,
                       tile_position=(32 * r, 0))
                    E("act", "activation", out=Rb[h % 2][:, 0:cw], in_=ip[:, 0:cw], func=AF.Relu)

                def idx_back(ch, h):
                    cw = min(512, nk - ch * 512)
                    csl = slice(ch * 512, ch * 512 + cw)
                    MM(sacc[:, 0:cw], Dh[:, h, :], Rb[h % 2][:, 0:cw], h == 0, h == 7)
                    if h == 7:
                        E("dve", "tensor_copy", out=scores[:, csl], in_=sacc[:, 0:cw])

                idx_front(*work[0])
                for n_, wk in enumerate(work):
                    if n_ + 1 < len(work):
                        idx_front(*work[n_ + 1])
                    idx_back(*wk)
                    yield
                idx_done[i] = True
                Rr = Rrs[par]
                mid, cntv, dirv, csum, dirS, Rk = mids[par], cntvs[par], dirvs[par], csums[par], dirSs[par], Rks[par]
                E("dve", "tensor_reduce", out=Rr, in_=scores[:, 0:nk], axis=AX.X, op=ALU.max, apply_absolute_value=True)
                E("dve", "tensor_tensor", out=scores[:, i * 128:nk], in0=scores[:, i * 128:nk], in1=negm[:], op=ALU.add)
                E("dve", "tensor_scalar", out=Rk, in0=pow2[:], scalar1=Rr, scalar2=None, op0=ALU.mult)
                E("dve", "memset", ap=mid, constant=0.0)
                E("act", "activation", out=negRks[par], in_=npow2[:], func=AF.Copy, scale=Rr)
                yield
                for k in range(K0):
                    E("dve", "tensor_scalar", out=junk[:, 0:nk], in0=scores[:, 0:nk], scalar1=mid, scalar2=None,
                      op0=ALU.is_ge, op1=ALU.add, accum_out=cntv)
                    yield
                    E("dve", "tensor_scalar", out=dirv, in0=cntv, scalar1=255.5, scalar2=0.5,
                      op0=ALU.is_ge, op1=ALU.subtract)
                    yield
                    E("dve", "scalar_tensor_tensor", out=mid, in0=dirv, scalar=Rk[:, k:k + 1], in1=mid,
                      op0=ALU.mult, op1=ALU.add)
                    yield
                nmid = nmids[par]
                E("act", "activation", out=nmid, in_=mid, func=AF.Copy, scale=-1.0)
                for k in range(K0, NIT):
                    E("act", "activation", out=junk[:, 0:nk], in_=scores[:, 0:nk], func=AF.Sign, bias=nmid,
                      accum_out=csum)
                    yield
                    E("act", "activation", out=dirS, in_=csum, func=AF.Sign, bias=cthr[:, i:i + 1])
                    yield
                    E("act", "activation", out=nmid, in_=dirS, func=AF.Identity, scale=negRks[par][:, k:k + 1],
                      bias=nmid)
                    yield
                E("dve", "tensor_scalar", out=maskt[:, 0:nk], in0=scores[:, 0:nk], scalar1=nmid, scalar2=0.0,
                  op0=ALU.add, op1=ALU.is_ge)
            elif i == 0:
                E("dve", "tensor_copy", out=maskt[:, 0:128], in_=trilb[:])
            else:
                E("dve", "memset", ap=maskt[:, 0:128], constant=1.0)
                E("dve", "tensor_copy", out=maskt[:, 128:256], in_=trilb[:])
            yield
            for j0 in range(0, i + 1, 8):
                njs = min(i + 1, j0 + 8) - j0
                for jj in range(njs):
                    j = j0 + jj
                    TR(mtr[:, jj, :], maskt[:, j * 128:(j + 1) * 128], identb[:], inc=(jj == njs - 1))
                E("act", "activation", out=maskT[:, j0:j0 + njs, :], in_=mtr[:, 0:njs, :], func=AF.Copy)
                yield

        def n_S1(i):
            nk = (i + 1) * 128
            n = 1 + (i // 8 + 1)
            if i >= 2:
                n += 8 * ((nk + 511) // 512) + 1 + 3 * NIT
            return n

        def gen_S2(i):
            qs = slice(i * 128, (i + 1) * 128)
            maskT = maskTs[i % 2]
            oa = oaccv[0]
            jts = [(0, i)] + ([(1, i - 1)] if i >= 1 else [])
            njt = len(jts)
            nfar = max(0, i - 1)
            groups = [("far", list(range(j0, min(nfar, j0 + 4)))) for j0 in range(0, nfar, 4)]
            groups.append(("near", [j for j in (i - 1, i) if j >= 0]))
            items = [("A", hk, None) for hk in range(2)] + [(c, kind, js) for c in range(4) for (kind, js) in groups]

            def front(item, gi):
                c, kind, js = item
                if c == "A":
                    hk = kind
                    for jn, (jt, j) in enumerate(jts):
                        for hh in range(4):
                            hq = 4 * hk + hh
                            cq, hf = hq // 2, hq % 2
                            MM(scb2[gi][hf][:, jn * 2 + hh // 2, :],
                               kaT2[64 * hf:64 * hf + 64, hk, j * 128:(j + 1) * 128],
                               qaT[64 * hf:64 * hf + 64, cq, qs], True, True,
                               inc=(jn == njt - 1 and hh == 3))
                else:
                    for jj, j in enumerate(js):
                        for hf in range(2):
                            MM(scb2[gi][hf][:, jj, :], kbT[64 * hf:64 * hf + 64, c, j * 128:(j + 1) * 128],
                               qbT[64 * hf:64 * hf + 64, c, qs], True, True,
                               inc=(jj == len(js) - 1 and hf == 1))

            def mid(item, gi, hf):
                c, kind, js = item
                ex = exB[2 * gi + hf]
                pt = PTB[2 * gi + hf]
                if c == "A":
                    hk = kind
                    if hf == 0:
                        E("act", "activation", out=exBp[gi][:, :, 0:2 * njt, :], in_=sc2w[gi][:, :, 0:2 * njt, :],
                          func=AF.Exp)
                    if hf == 0:
                        for jn, (jt, j) in enumerate(jts):
                            E("dve", "tensor_tensor", out=PTBp[gi][:, :, 2 * jn:2 * jn + 2, :],
                              in0=exBp[gi][:, :, 2 * jn:2 * jn + 2, :],
                              in1=EbA[:, jt, 4 * hk:4 * hk + 4, :].rearrange("p (hh f) c -> p f hh c", f=2),
                              op=ALU.mult)
                else:
                    n = len(js)
                    h = 2 * c + hf
                    if hf == 0:
                        E("act", "activation", out=exBp[gi][:, :, 0:n, :], in_=sc2w[gi][:, :, 0:n, :], func=AF.Exp)
                        E("dve", "tensor_tensor", out=PTBp[gi][:, :, 0:n, :], in0=exBp[gi][:, :, 0:n, :],
                          in1=maskT[:, js[0]:js[0] + n, :].unsqueeze(1).broadcast_to([128, 2, n, 128]), op=ALU.mult)
                        if kind == "near":
                            E("dve", "tensor_tensor", out=PTBp[gi][:, :, 0:n, :], in0=PTBp[gi][:, :, 0:n, :],
                              in1=EbB[:, 2 - n:2, 2 * c:2 * c + 2, :].rearrange("p a b c -> p b a c"), op=ALU.mult)

            def back(item, gi):
                c, kind, js = item
                if c == "A":
                    hk = kind
                    for hh in range(4):
                        pt = PTB[2 * gi + hh % 2]
                        for jn, (jt, j) in enumerate(jts):
                            MM(oa[:, hh, :], pt[:, 2 * jn + hh // 2, :], vA[:, j, hk, :], jn == 0, jn == njt - 1,
                               inc=(hh == 3 and jn == njt - 1))
                    E("dve", "tensor_tensor", out=den[:, 0:4], in0=oa[:, :, 64], in1=esink[:, 4 * hk:4 * hk + 4],
                      op=ALU.add)
                    E("dve", "reciprocal", out=den[:, 0:4], in_=den[:, 0:4])
                    E("dve", "tensor_tensor", out=ob[:, hk * 256:(hk + 1) * 256].rearrange("p (h d) -> p h d", h=4),
                      in0=oa[:, :, 0:64], in1=den[:, 0:4].unsqueeze(2).broadcast_to([128, 4, 64]), op=ALU.mult)
                else:
                    n = len(js)
                    k4 = c // 2
                    for hf in range(2):
                        h = 2 * c + hf
                        pt = PTB[2 * gi + hf]
                        for jj, j in enumerate(js):
                            first = (c % 2 == 0 and hf == 0 and j == 0)
                            last = (c % 2 == 1 and hf == 1 and j == i)
                            MM(oa[:, h % 4, :], pt[:, jj, :], vB[:, j, h, :], first, last,
                               inc=(hf == 1 and jj == n - 1))
                    if c % 2 == 1 and kind == "near":
                        E("dve", "reciprocal", out=den[:, 4:8], in_=oa[:, :, 64])
                        E("dve", "tensor_tensor",
                          out=ob[:, 512 + k4 * 256:512 + (k4 + 1) * 256].rearrange("p (h d) -> p h d", h=4),
                          in0=oa[:, :, 0:64], in1=den[:, 4:8].unsqueeze(2).broadcast_to([128, 4, 64]), op=ALU.mult)

            gi0 = ctr["g"] % 2
            ctr["g"] += len(items)
            front(items[0], gi0)
            yield
            for k, item in enumerate(items):
                gi = (gi0 + k) % 2
                for hf in range(2):
                    mid(item, gi, hf)
                    yield
                if k + 1 < len(items):
                    front(items[k + 1], 1 - gi)
                    yield
                back(item, gi)
                yield
            for c in range(8):
                TR(otr[:, c, :], ob[:, c * 128:(c + 1) * 128], identb[:], inc=(c == 7))
            E("act", "activation", out=oT[:, :, qs], in_=otr, func=AF.Copy)
            yield

        def n_S2(i):
            nfar = max(0, i - 1)
            return 2 + 4 * (2 + 4 * ((nfar + 3) // 4 + 1))

        w_pa_r = w_pa.rearrange("(ec p) d -> p ec d", p=128)
        w_pb_r = w_pb.rearrange("(ec p) d -> p ec d", p=128)
        w_out_r = w_out.rearrange("(dc p) e -> p dc e", p=128)
        WgB0 = nc.alloc_sbuf_tensor_at("WgB0", [128, 8, 512], BF16, offset=int(q2T.manual_sbuf_range[0]))
        WpA0 = nc.alloc_sbuf_tensor_at("WpA0", [128, 4, 512], BF16, offset=int(kiT.manual_sbuf_range[0]))
        pf = {"z0": False, "r": False}

        def pf_z0():
            if not pf["z0"]:
                pf["z0"] = True
                S.dma("pool", "d_w0", [(wsl[0][:], w_in_r[:, :, 768:1280])])

        def pf_rest():
            if not pf["r"]:
                pf["r"] = True
                S.dma("pool", "d_w1", [(wsl[1][:], w_in_r[:, :, 2816:3328])])
                S.dma("pool", "d_g0", [
                    (wsl[2][:], w_in_r[:, :, 3624:3624 + 512]),
                    (WgB0[:], w_in_r[:, :, 4648:4648 + 512]),
                    (WpA0[:], w_pa_r[:, :, 0:512]),
                ])

        live = {}
        idx_done = {0: True, 1: True}

        def start(j):
            if j < _nt2 and j not in live:
                live[j] = [gen_S1(j), n_S1(j), 0]

        def adv(j, frac):
            st = live.get(j)
            if st is None:
                return
            pv = live.get(j - 1)
            while pv is not None and pv[0] is not None and not idx_done.get(j - 1, False):
                try:
                    next(pv[0])
                    pv[2] += 1
                except StopIteration:
                    pv[0] = None
            while st[0] is not None and st[2] < frac * st[1]:
                try:
                    next(st[0])
                    st[2] += 1
                except StopIteration:
                    st[0] = None
            if frac >= 1.0:
                while st[0] is not None:
                    try:
                        next(st[0])
                    except StopIteration:
                        st[0] = None

        start(0)
        adv(0, 1.0)
        for i in range(_nt2):
            start(i + 1)
            start(i + 2)
            if i == NT - 2:
                pf_z0()
            if i == NT - 1:
                pf_rest()
            g2 = gen_S2(i)
            n2 = n_S2(i)
            s = 0
            for _ in g2:
                s += 1
                p = min(1.0, s / n2)
                adv(i + 1, min(1.0, 0.5 + 0.5 * p / 0.95))
                adv(i + 2, 0.5 * p)
            adv(i + 1, 1.0)
            live.pop(i + 1, None)

        dump("d_oT", oT[:], [128, 8, S_LEN], BF16)
        if stop == 2:
            return finish()
        pf_z0()
        pf_rest()
        A3 = Arena(nc, R_QKV.base, R_QKV.size, "p3")
        A3b = Arena(nc, R_SP.base, R_SP.size, "p3b")
        mT = A3.alloc("mT", [128, 8, S_LEN], BF16)
        gws = [{"gA": wsl[2], "gB": WgB0, "pA": WpA0, "pB": None},
               {"gA": A3.alloc("WgA1", [128, 8, 512], BF16), "gB": A3.alloc("WgB1", [128, 8, 512], BF16),
                "pA": A3.alloc("WpA1", [128, 4, 512], BF16), "pB": A3.alloc("WpB1", [128, 4, 512], BF16)}]
        gws[0]["pB"] = A3.alloc("WpB0", [128, 4, 512], BF16)
        xts2 = [A3.alloc("xo%d" % i, [128, D], F32) for i in range(2)]
        tmpz = [A3b.alloc("tmpz%d" % i, [128, 512], BF16) for i in range(2)]
        sA = [A3b.alloc("sA%d" % i, [128, 512], F32) for i in range(2)]
        sB = [A3b.alloc("sB%d" % i, [128, 512], F32) for i in range(2)]
        tA = [A3b.alloc("tA%d" % i, [128, 512], F32) for i in range(2)]
        S.dma("pool", "d_g0b", [(gws[0]["pB"][:], w_pb_r[:, :, 0:512])])
        S.dma("pool", "d_g1", [
            (gws[1]["pA"][:], w_pa_r[:, :, 512:1024]), (gws[1]["pB"][:], w_pb_r[:, :, 512:1024]),
            (gws[1]["gA"][:], w_in_r[:, :, 3624 + 512:3624 + 1024]),
            (gws[1]["gB"][:], w_in_r[:, :, 4648 + 512:4648 + 1024]),
        ])

        zn = 0
        for zi, (col0, cbase) in enumerate(((768, 0), (2816, 4))):
            wz = wsl[zi]
            for cc in range(4):
                for tg in range(4):
                    pz = pbf[zn % 2]
                    tz = tmpz[zn % 2]
                    zn += 1
                    for kc in range(8):
                        MM(pz, wz[:, kc, cc * 128:(cc + 1) * 128], hT[:, kc, tsl(tg)], kc == 0, kc == 7)
                    E("act", "activation", out=tz[:], in_=pz, func=AF.Silu)
                    E("dve", "tensor_tensor", out=oT[:, cbase + cc, tsl(tg)], in0=oT[:, cbase + cc, tsl(tg)],
                      in1=tz[:], op=ALU.mult)
        wo = [wsl[0], wsl[1]]
        for hf in range(2):
            S.dma("pool", "d_w%d" % hf, [(wsl[hf][:], w_out_r[:, :, hf * 512:(hf + 1) * 512])])

        gn = 0
        for dg in range(2):
            gw = gws[dg]
            for dcl in range(4):
                dc = dg * 4 + dcl
                cs = slice(dcl * 128, (dcl + 1) * 128)
                for tg in range(4):
                    k = gn % 2
                    gn += 1
                    bPA, bPB, bgA, bgB = (2, 3, 4, 5) if k == 0 else (0, 1, 6, 7)
                    for kc in range(8):
                        MM(pbf[bgA], gw["gA"][:, kc, cs], hT[:, kc, tsl(tg)], kc == 0, kc == 7)
                    for kc in range(8):
                        MM(pbf[bgB], gw["gB"][:, kc, cs], hT[:, kc, tsl(tg)], kc == 0, kc == 7)
                    for ec in range(4):
                        MM(pbf[bPA], gw["pA"][:, ec, cs], oT[:, ec, tsl(tg)], ec == 0, ec == 3)
                    for ec in range(4):
                        MM(pbf[bPB], gw["pB"][:, ec, cs], oT[:, 4 + ec, tsl(tg)], ec == 0, ec == 3)
                    E("act", "activation", out=sA[k][:], in_=pbf[bgA], func=AF.Sigmoid)
                    E("act", "activation", out=sB[k][:], in_=pbf[bgB], func=AF.Sigmoid)
                    E("dve", "tensor_tensor", out=tA[k][:], in0=sA[k][:], in1=pbf[bPA], op=ALU.mult)
                    E("dve", "tensor_tensor", out=sB[k][:], in0=sB[k][:], in1=pbf[bPB], op=ALU.mult)
                    E("dve", "tensor_tensor", out=mT[:, dc, tsl(tg)], in0=tA[k][:], in1=sB[k][:], op=ALU.add)

        okeys = []
        for ti in range(NT):
            tok = slice(ti * 128, (ti + 1) * 128)
            xo = xts2[ti % 2]
            S.dma("sp", "d_xo%d" % (ti % 2), [(xo[:], x[tok, :])])
            for hf in range(2):
                po = pbf[2 * (ti % 2) + hf]
                for dc in range(8):
                    MM(po, mT[:, dc, tok], wo[hf][:, dc, :], dc == 0, dc == 7)
                E("dve", "tensor_tensor", out=xo[:, hf * 512:(hf + 1) * 512], in0=po,
                  in1=xo[:, hf * 512:(hf + 1) * 512], op=ALU.add)
            key = "d_o%d" % (ti % 2)
            if key not in okeys:
                okeys.append(key)
            S.dma("sp", key, [(out[tok, :], xo[:])])
        dkeys.extend(okeys)
        return finish()


def _t5_bucket_np(n):
    n = np.maximum(n, 0)
    nf = np.maximum(n, 1).astype(np.float32)
    large = 16 + (np.log(nf / np.float32(16)) / np.float32(math.log(128 / 16)) * np.float32(16)).astype(np.int32)
    large = np.minimum(large, 31)
    return np.where(n < 16, n, large)


def _host_consts(norm_g, qnorm_a, knorm_a, sinks_a, qnorm_b, knorm_b, rel_bias):
    f = np.float32
    s = np.arange(128)[:, None]
    t = np.arange(128)[None, :]
    d0 = t - s
    d1 = t + 128 - s
    b0 = _t5_bucket_np(d0)
    b1 = _t5_bucket_np(d1)
    ta = rel_bias[:, :8]
    tb = rel_bias[:, 8:]

    def gath(tab):
        a = np.stack([tab[b0], tab[b1]], axis=1)
        return np.ascontiguousarray(a.transpose(0, 1, 3, 2)).reshape(128, -1).astype(f)

    mA = np.stack([(s <= t), (s > t)], axis=1).astype(f).reshape(128, -1)
    tt = np.arange(128)[:, None]
    sx = np.arange(128)[None, :]
    tril = (sx <= tt).astype(f)
    bd = np.zeros((128, 128), f)
    bd[:64, :64] = 1
    bd[64:, 64:] = 1
    return {
        "c_gT": np.ascontiguousarray(norm_g.reshape(8, 128).T).astype(f),
        "c_gq": np.ascontiguousarray(np.stack([np.tile(qnorm_a, 2), np.tile(knorm_a, 2),
                                               np.tile(qnorm_b, 2), np.tile(knorm_b, 2)], axis=1)).astype(f),
        "c_sink": np.ascontiguousarray(np.broadcast_to(sinks_a[None, :], (128, 8))).astype(f),
        "c_cfar": np.ascontiguousarray(np.broadcast_to(tb[31][None, :], (128, 8))).astype(f),
        "c_biasA": gath(ta),
        "c_biasB": gath(tb),
        "c_mA": mA,
        "c_negm": np.where(sx <= tt, 0.0, -BIG).astype(f),
        "c_tril": tril,
        "c_ident": np.eye(128, dtype=f),
        "c_bd": bd,
        "c_pow2": np.ascontiguousarray(np.broadcast_to((0.5 ** np.arange(NIT))[None, :], (128, NIT))).astype(f),
        "c_npow2": np.ascontiguousarray(np.broadcast_to((-0.5 * 0.5 ** np.arange(NIT))[None, :], (128, NIT))).astype(f),
        "c_cthr": np.ascontiguousarray(np.broadcast_to(((np.arange(NT) + 1) * 128 - 511.5)[None, :], (128, NT))).astype(f),
    }


_CACHE = {}


def kernel(x, norm_g, w_in, qnorm_a, knorm_a, sinks_a, qnorm_b, knorm_b, rel_bias, w_proj_a, w_proj_b, w_out):
    a = lambda v: np.ascontiguousarray(np.asarray(v, dtype=np.float32))
    x = a(x)
    consts = _host_consts(a(norm_g)[0], a(qnorm_a)[0], a(knorm_a)[0], a(sinks_a)[0], a(qnorm_b)[0],
                          a(knorm_b)[0], a(rel_bias))
    shared = {"w_in": a(w_in)[0], "w_pa": a(w_proj_a)[0], "w_pb": a(w_proj_b)[0], "w_out": a(w_out)[0]}
    shared.update(consts)
    if "nc" not in _CACHE:
        _CACHE["nc"] = build_program()
    nc = _CACHE["nc"]
    in_maps = [dict(shared, x=x[b]) for b in range(8)]
    res = run_bass_kernel_spmd(nc, in_maps, core_ids=list(range(8)))
    return np.stack([r["out"] for r in res.results], axis=0).astype(np.float32)
```

```python
import math
from contextlib import ExitStack

import numpy as np
import concourse.bass as bass
import concourse.mybir as mybir
from concourse.bass_utils import run_bass_kernel_spmd

F32 = mybir.dt.float32
BF16 = mybir.dt.bfloat16
ALU = mybir.AluOpType
AF = mybir.ActivationFunctionType
AX = mybir.AxisListType

S_LEN = 2048
D = 1024
NT = 16
INW = 5672
NIT = 16
K0 = 10
BIG = 1.0e30
ENGS = ("pe", "dve", "act", "pool", "sp")
_ESZ = {F32: 4, BF16: 2}
PAGE = 2048


def _esize(dt):
    return _ESZ[dt]


def _box(ap):
    t = ap.tensor
    tn = type(t).__name__
    if tn.startswith("DRam"):
        return None
    dims = list(ap.ap)
    off = int(ap.offset)
    es = _esize(ap.dtype)
    pstride = int(dims[0][0])
    npart = int(dims[0][1])
    if pstride <= 0:
        pstride = 1
        for s in list(t.shape)[1:]:
            pstride *= int(s)
    p0 = off // pstride
    f0 = off % pstride
    f1 = f0
    for st, n in dims[1:]:
        f1 += abs(int(st)) * (int(n) - 1)
    if tn.startswith("PSum"):
        base = 1 << 24
        base += int(t.name[2:]) * 2048
    else:
        base = int(t.manual_sbuf_range[0])
    return (p0, p0 + npart, base + f0 * es, base + (f1 + 1) * es)


def _ovl(a, b):
    return a[0] < b[1] and b[0] < a[1] and a[2] < b[3] and b[2] < a[3]


def _contains(a, b):
    return a[0] <= b[0] and a[1] >= b[1] and a[2] <= b[2] and a[3] >= b[3]


class Sched:
    def __init__(self, nc, stack):
        self.nc = nc
        self.stack = stack
        self.q = {e: [] for e in ENGS}
        self.sems = {}
        self.cnt = {}
        self.unit = {}
        for e in ENGS:
            self._mksem(e, 1)
        self.seen = {e: {} for e in ENGS}
        self.pages = {}
        self.n_inst = 0

    def _mksem(self, key, unit):
        self.sems[key] = self.stack.enter_context(self.nc.semaphore("s_" + key))
        self.cnt[key] = 0
        self.unit[key] = unit

    def _pages(self, box):
        return range(box[2] // PAGE, (box[3] - 1) // PAGE + 1)

    def _scan(self, box, want_reads, deps, eng, raw):
        for pg in self._pages(box):
            for r in self.pages.get(pg, ()):
                if (want_reads or r[1] == "w") and _ovl(r[0], box):
                    k, i = r[2]
                    if k == eng and eng == "pe":
                        continue
                    if deps.get(k, 0) < i:
                        deps[k] = i

    def _record(self, prod, rboxes, wboxes):
        for box in wboxes:
            for pg in self._pages(box):
                lst = self.pages.setdefault(pg, [])
                lst[:] = [r for r in lst if not _contains(box, r[0])]
                lst.append((box, "w", prod))
        for box in rboxes:
            for pg in self._pages(box):
                lst = self.pages.setdefault(pg, [])
                lst[:] = [r for r in lst if not (r[1] == "r" and r[2][0] == prod[0] and _contains(box, r[0]))]
                lst.append((box, "r", prod))

    def _emit_waits(self, eng, deps):
        seen = self.seen[eng]
        for k, i in deps.items():
            if k == eng and eng == "pe":
                continue
            if seen.get(k, 0) >= i:
                continue
            seen[k] = i
            self.q[eng].append(("wait", k, i * self.unit[k]))

    def op(self, eng, fn, reads=(), writes=(), inc=True):
        rb = [b for b in (_box(a) for a in reads) if b is not None]
        wb = [b for b in (_box(a) for a in writes) if b is not None]
        deps = {}
        for b in rb:
            self._scan(b, False, deps, eng, True)
        for b in wb:
            self._scan(b, True, deps, eng, False)
        self._emit_waits(eng, deps)
        idx = self.cnt[eng] + 1
        if inc:
            self.cnt[eng] = idx
        self.q[eng].append(("inst", fn, inc, eng))
        self._record((eng, idx), rb, wb)
        self.n_inst += 1

    def dma(self, qeng, semkey, pairs, **kw):
        if semkey not in self.sems:
            self._mksem(semkey, 16)
        rb = [b for b in (_box(p[1]) for p in pairs) if b is not None]
        wb = [b for b in (_box(p[0]) for p in pairs) if b is not None]
        deps = {}
        for b in rb:
            self._scan(b, False, deps, "__dma__", False)
        for b in wb:
            self._scan(b, True, deps, "__dma__", False)
        self._emit_waits(qeng, deps)
        idx = self.cnt[semkey] + len(pairs)
        self.cnt[semkey] = idx
        for (o, i) in pairs:
            self.q[qeng].append(("dma", o, i, semkey, kw))
        self._record((semkey, idx), rb, wb)
        self.n_inst += len(pairs)

    def wait_all(self, eng, keys):
        for k in keys:
            if self.cnt[k] > self.seen[eng].get(k, 0):
                self.seen[eng][k] = self.cnt[k]
                self.q[eng].append(("wait", k, self.cnt[k] * self.unit[k]))

    def emit(self):
        nc = self.nc
        sems = self.sems
        q = self.q

        def run(engobj, items):
            for it in items:
                if it[0] == "wait":
                    engobj.wait_ge(sems[it[1]], it[2])
                elif it[0] == "inst":
                    ins = it[1](engobj)
                    if it[2]:
                        ins.then_inc(sems[it[3]], 1)
                else:
                    _, o, i, sk, kw = it
                    engobj.dma_start(out=o, in_=i, **kw).then_inc(sems[sk], 16)

        with nc.Block() as block:
            @block.tensor
            def _(e):
                run(e, q["pe"])

            @block.vector
            def _(e):
                run(e, q["dve"])

            @block.scalar
            def _(e):
                run(e, q["act"])

            @block.gpsimd
            def _(e):
                run(e, q["pool"])

            @block.sync
            def _(e):
                run(e, q["sp"])


class Arena:
    def __init__(self, nc, base, size, tag):
        self.nc, self.base, self.size, self.tag, self.off = nc, base, size, tag, 0

    def alloc(self, name, shape, dt):
        n = 1
        for s in shape[1:]:
            n *= s
        nb = n * _esize(dt)
        nb = (nb + 31) // 32 * 32
        assert self.off + nb <= self.size, (self.tag, name, self.off, nb, self.size)
        t = self.nc.alloc_sbuf_tensor_at(name, list(shape), dt, offset=self.base + self.off)
        self.off += nb
        return t


def build_program(stop=99, dbg=None):
    nc = bass.Bass("TRN2", target_bir_lowering=False)
    dr = lambda name, shape, kind="ExternalInput": nc.dram_tensor(name, shape, F32, kind=kind).ap()
    x = dr("x", [S_LEN, D])
    w_in = dr("w_in", [D, INW])
    w_pa = dr("w_pa", [512, D])
    w_pb = dr("w_pb", [512, D])
    w_out = dr("w_out", [D, D])
    c_gT = dr("c_gT", [128, 8])
    c_gq = dr("c_gq", [128, 4])
    c_sink = dr("c_sink", [128, 8])
    c_cfar = dr("c_cfar", [128, 8])
    c_biasA = dr("c_biasA", [128, 2 * 8 * 128])
    c_biasB = dr("c_biasB", [128, 2 * 8 * 128])
    c_mA = dr("c_mA", [128, 2 * 128])
    c_negm = dr("c_negm", [128, 128])
    c_tril = dr("c_tril", [128, 128])
    c_ident = dr("c_ident", [128, 128])
    c_bd = dr("c_bd", [128, 128])
    c_pow2 = dr("c_pow2", [128, NIT])
    c_npow2 = dr("c_npow2", [128, NIT])
    c_cthr = dr("c_cthr", [128, NT])
    out = dr("out", [S_LEN, D], kind="ExternalOutput")

    with ExitStack() as st:
        S = Sched(nc, st)

        def E(eng, meth, inc=True, **kw):
            reads, writes = [], []
            for k, v in kw.items():
                if hasattr(v, "tensor") and hasattr(v, "ap"):
                    (writes if k in ("out", "accum_out", "ap") else reads).append(v)
            S.op(eng, lambda e: getattr(e, meth)(**kw), reads=reads, writes=writes, inc=inc)

        def MM(outp, lhsT, rhs, start, stop, inc=None, **kw):
            if inc is None:
                inc = stop
            S.op("pe", lambda e: e.matmul(outp, lhsT=lhsT, rhs=rhs, start=start, stop=stop, **kw),
                 reads=[lhsT, rhs], writes=[outp], inc=inc)

        def TR(outp, in_, ident, inc=True):
            S.op("pe", lambda e: e.transpose(out=outp, in_=in_, identity=ident),
                 reads=[in_, ident], writes=[outp], inc=inc)

        dkeys = []

        def dump(name, sb_ap, shape, dt):
            if dbg is None:
                return
            d = nc.dram_tensor(name, list(shape), dt, kind="ExternalOutput").ap()
            S.dma("sp", "d_dbg", [(d, sb_ap)])
            dbg.append(name)
            if "d_dbg" not in dkeys:
                dkeys.append("d_dbg")

        def finish():
            S.wait_all("sp", dkeys)
            S.emit()
            return nc

        base0 = (int(nc.sbuf_base) + 63) // 64 * 64
        top = int(nc.sbuf_top)
        cur = [base0]

        def region(size, tag):
            a = Arena(nc, cur[0], size, tag)
            cur[0] += size
            assert cur[0] <= top, (tag, cur[0], top)
            return a

        R_CONST = region(12288, "const")
        R_HT = region(32768, "hT")
        R_W = region(24576, "w")
        R_OT = region(32768, "oT")
        R_QKV = region(90432, "qkv")
        R_SP = region((top - cur[0]) // 64 * 64, "spare")

        pb0 = nc.alloc_psum_tensor("pb0", [128, 512], F32)
        pb1 = nc.alloc_psum_tensor("pb1", [128, 512], F32)
        pb2 = nc.alloc_psum_tensor("pb2", [128, 1024], F32)
        pb4 = nc.alloc_psum_tensor("pb4", [128, 1024], F32)
        pb6 = nc.alloc_psum_tensor("pb6", [128, 512], F32)
        pb7 = nc.alloc_psum_tensor("pb7", [128, 512], F32)
        pbf = [pb0.ap(), pb1.ap(), pb2.ap()[:, 0:512], pb2.ap()[:, 512:1024], pb4.ap()[:, 0:512],
               pb4.ap()[:, 512:1024], pb6.ap(), pb7.ap()]
        _h2, _h4 = pb2.bitcast(BF16).ap(), pb4.bitcast(BF16).ap()
        pbh = [pb0.bitcast(BF16).ap(), pb1.bitcast(BF16).ap(), _h2[:, 0:1024], _h2[:, 1024:2048], _h4[:, 0:1024],
               _h4[:, 1024:2048], pb6.bitcast(BF16).ap(), pb7.bitcast(BF16).ap()]
        sc2w = [pb2.ap().rearrange("p (f a b) -> p f a b", f=2, a=4), pb4.ap().rearrange("p (f a b) -> p f a b", f=2, a=4)]

        identb = R_CONST.alloc("identb", [128, 128], BF16)
        bdb = R_CONST.alloc("bdb", [128, 128], BF16)
        EbA = R_CONST.alloc("EbA", [128, 2, 8, 128], BF16)
        EbB = R_CONST.alloc("EbB", [128, 2, 8, 128], BF16)
        negm = R_CONST.alloc("negm", [128, 128], F32)
        trilb = R_CONST.alloc("trilb", [128, 128], BF16)
        gT = R_CONST.alloc("gT", [128, 8], F32)
        gq = R_CONST.alloc("gq", [128, 4], F32)
        esink = R_CONST.alloc("esink", [128, 8], F32)
        cfar = R_CONST.alloc("cfar", [128, 8], F32)
        epsT = R_CONST.alloc("epsT", [128, 1], F32)
        pow2 = R_CONST.alloc("pow2", [128, NIT], F32)
        npow2 = R_CONST.alloc("npow2", [128, NIT], F32)
        cthr = R_CONST.alloc("cthr", [128, NT], F32)
        sgn = R_CONST.alloc("sgn", [128, NT, 8], F32)
        wab = R_CONST.alloc("wab", [128, NT, 8], F32)
        ss = R_CONST.alloc("ss", [128, NT], F32)
        sd = R_CONST.alloc("sd", [128, NT], F32)
        rstd = R_CONST.alloc("rstd", [128, NT], F32)
        smalls = R_CONST.alloc("smalls", [128, 96], F32)
        Rrs = [smalls[:, 0:1], smalls[:, 7:8]]
        mid = smalls[:, 1:2]
        cntv = smalls[:, 2:3]
        dirv = smalls[:, 3:4]
        den = smalls[:, 8:16]
        Rk = smalls[:, 16:16 + NIT]
        negRks = [smalls[:, 32:32 + NIT], smalls[:, 48:48 + NIT]]
        csum = smalls[:, 4:5]
        dirS = smalls[:, 5:6]
        nmids = [smalls[:, 6:7], smalls[:, 64:65]]
        mids = [smalls[:, 1:2], smalls[:, 65:66]]
        cntvs = [smalls[:, 2:3], smalls[:, 66:67]]
        dirvs = [smalls[:, 3:4], smalls[:, 67:68]]
        csums = [smalls[:, 4:5], smalls[:, 68:69]]
        dirSs = [smalls[:, 5:6], smalls[:, 69:70]]
        Rks = [smalls[:, 16:16 + NIT], smalls[:, 70:70 + NIT]]

        hT = R_HT.alloc("hT", [128, 8, S_LEN], BF16)
        wsl = [R_W.alloc("wsl%d" % i, [128, 8, 512], BF16) for i in range(3)]
        oT = R_OT.alloc("oT", [128, 8, S_LEN], BF16)

        qaT = R_QKV.alloc("qaT", [128, 4, S_LEN], BF16)
        kaT2 = R_QKV.alloc("kaT2", [128, 2, S_LEN], BF16)
        vA = R_QKV.alloc("vA", [128, NT, 2, 65], BF16)
        qbT = R_QKV.alloc("qbT", [128, 4, S_LEN], BF16)
        kbT = R_QKV.alloc("kbT", [128, 4, S_LEN], BF16)
        vB = R_QKV.alloc("vB", [128, NT, 8, 65], BF16)
        q2T = R_QKV.alloc("q2T", [128, 2, S_LEN], BF16)
        kiT = R_QKV.alloc("kiT", [128, S_LEN], BF16)

        A1 = Arena(nc, R_OT.base, R_OT.size, "p01")
        xts = [A1.alloc("xt%d" % i, [128, D], F32) for i in range(2)]
        hns = [A1.alloc("hn%d" % i, [128, D], BF16) for i in range(2)]
        sqb = [A1.alloc("sqb%d" % i, [128, 512], BF16) for i in range(2)]
        sdb = [A1.alloc("sdb%d" % i, [128, 512], F32) for i in range(2)]
        q2b = [A1.alloc("q2b%d" % i, [128, 256], BF16) for i in range(2)]
        stgA = A1.alloc("stgA", [128, 2, 8, 128], F32)
        _sb = int(stgA.manual_sbuf_range[0])
        xts += [nc.alloc_sbuf_tensor_at("xt%d" % (2 + k), [128, D], F32, offset=_sb + 4096 * k) for k in range(2)]
        stgm = A1.alloc("stgm", [128, 2, 128], F32)
        stgi = A1.alloc("stgi", [128, 128], F32)
        stgd = A1.alloc("stgd", [128, 128], F32)
        stgt = A1.alloc("stgt", [128, 128], F32)

        xpre = set()
        for _i in range(4):
            S.dma("sp", "d_x%d" % _i, [(xts[_i][:], x[_i * 128:(_i + 1) * 128, :])])
            xpre.add(_i)
        S.dma("sp", "d_c", [
            (gT[:], c_gT[:, :]), (gq[:], c_gq[:, :]), (esink[:], c_sink[:, :]), (cfar[:], c_cfar[:, :]),
            (negm[:], c_negm[:, :]), (pow2[:], c_pow2[:, :]), (npow2[:], c_npow2[:, :]), (cthr[:], c_cthr[:, :]),
            (stgm[:].rearrange("p a b -> p (a b)"), c_mA[:, :]),
            (stgi[:], c_ident[:, :]), (stgd[:], c_bd[:, :]), (stgt[:], c_tril[:, :]),
        ])
        E("dve", "memset", ap=epsT[:], constant=1e-6)
        E("dve", "tensor_copy", out=identb[:], in_=stgi[:])
        E("dve", "tensor_copy", out=bdb[:], in_=stgd[:])
        E("dve", "tensor_copy", out=trilb[:], in_=stgt[:])
        E("pool", "memset", ap=vA[:, :, :, 64:65], constant=1.0)
        E("pool", "memset", ap=vB[:, :, :, 64:65], constant=1.0)
        E("dve", "tensor_scalar", out=gq[:, 0:1], in0=gq[:, 0:1], scalar1=0.125, scalar2=None, op0=ALU.mult)
        E("dve", "tensor_scalar", out=gq[:, 2:3], in0=gq[:, 2:3], scalar1=0.125, scalar2=None, op0=ALU.mult)
        E("act", "activation", out=esink[:], in_=esink[:], func=AF.Exp)
        w_in_r = w_in.rearrange("(kc p) c -> p kc c", p=128)
        wstate = {"n": 0}

        def load_w(pieces):
            s = wstate["n"] % 3
            wstate["n"] += 1
            t = wsl[s]
            S.dma("pool", "d_w%d" % s,
                  [(t[:, :, d0:d0 + n], w_in_r[:, :, s0:s0 + n]) for (d0, n, s0) in pieces])
            return t

        def p0A(i):
            xt = xts[i % 4]
            hn = hns[i % 2]
            if i not in xpre:
                S.dma("sp", "d_x%d" % (i % 4), [(xt[:], x[i * 128:(i + 1) * 128, :])])
            E("act", "activation", out=hn[:], in_=xt[:], func=AF.Square, accum_out=ss[:, i:i + 1])
            E("act", "activation", out=sd[:, i:i + 1], in_=ss[:, i:i + 1], func=AF.Ln,
              scale=1.0 / D, bias=epsT[:])
            E("act", "activation", out=rstd[:, i:i + 1], in_=sd[:, i:i + 1], func=AF.Exp, scale=-0.5)
            E("pool", "tensor_scalar", out=hn[:], in0=xt[:], scalar1=rstd[:, i:i + 1], scalar2=1.0,
              op0=ALU.mult, op1=ALU.mult)
            ptr = pbh[i % 2][:, 0:1024].rearrange("p (a b) -> p a b", a=8)
            for kc in range(8):
                TR(ptr[:, kc, :], hn[:, kc * 128:(kc + 1) * 128], identb[:], inc=(kc == 7))

        def p0B(i):
            ptr = pbh[i % 2][:, 0:1024].rearrange("p (a b) -> p a b", a=8)
            E("dve", "tensor_tensor", out=hT[:, :, i * 128:(i + 1) * 128], in0=ptr,
              in1=gT[:, :].unsqueeze(2).broadcast_to([128, 8, 128]), op=ALU.mult)

        p0s = {"a": 0, "b": 0}

        def p0_adv():
            if p0s["a"] < NT:
                p0A(p0s["a"])
                p0s["a"] += 1
            if p0s["b"] < p0s["a"] - 1 or (p0s["a"] == NT and p0s["b"] < NT):
                p0B(p0s["b"])
                p0s["b"] += 1

        def p0_need(ntiles):
            while p0s["b"] < ntiles:
                if p0s["a"] < NT and p0s["a"] <= p0s["b"] + 1:
                    p0A(p0s["a"])
                    p0s["a"] += 1
                else:
                    p0B(p0s["b"])
                    p0s["b"] += 1

        def tsl(tg):
            return slice(tg * 512, (tg + 1) * 512)

        fmn = [0]
        pend = [None]

        def fm_mm(ws, ccol, tg):
            n = fmn[0]
            fmn[0] += 1
            acc = pbf[(2, 3, 6, 7)[n % 4]]
            for kc in range(8):
                MM(acc, ws[:, kc, ccol:ccol + 128], hT[:, kc, tsl(tg)], kc == 0, kc == 7)
            return n

        def fm_post(n, dst, gain, norm):
            acc = pbf[(2, 3, 6, 7)[n % 4]]
            if norm:
                sq = sqb[n % 2]
                E("act", "activation", out=sq[:], in_=acc, func=AF.Square)
                ssb = pbf[4 + n % 2]
                MM(ssb, bdb[:], sq[:], True, True)
                sdt = sdb[n % 2]
                E("act", "activation", out=sdt[:], in_=ssb, func=AF.Ln, scale=1.0 / 64, bias=epsT[:])
                E("act", "activation", out=sdt[:], in_=sdt[:], func=AF.Exp, scale=-0.5)
                E("dve", "scalar_tensor_tensor", out=dst, in0=acc, scalar=gain, in1=sdt[:],
                  op0=ALU.mult, op1=ALU.mult)
            else:
                E("act", "activation", out=dst, in_=acc, func=AF.Copy)

        pendq = []

        def fm(ws, ccol, dst, gain, norm, tg):
            n = fm_mm(ws, ccol, tg)
            if len(pendq) == 2:
                fm_post(*pendq.pop(0))
            pendq.append((n, dst, gain, norm))

        def fm_flush():
            while pendq:
                fm_post(*pendq.pop(0))

        ws = load_w([(0, 512, 0)])
        p0_need(NT)
        S.dma("sp", "d_c2", [(stgA[:].rearrange("p a h t -> p (a h t)"), c_biasA[:, :])])
        E("act", "activation", out=stgA[:], in_=stgA[:], func=AF.Exp)
        for jt in range(2):
            E("dve", "tensor_tensor", out=EbA[:, jt, :, :], in0=stgA[:, jt, :, :],
              in1=stgm[:, jt, :].unsqueeze(1).broadcast_to([128, 8, 128]), op=ALU.mult)
        stgB = nc.alloc_sbuf_tensor_at("stgB", [128, 2, 8, 128], F32, offset=R_SP.base)
        S.dma("sp", "d_c3", [(stgB[:].rearrange("p a h t -> p (a h t)"), c_biasB[:, :])])
        for jt in range(2):
            E("dve", "tensor_tensor", out=stgB[:, jt, :, :], in0=stgB[:, jt, :, :],
              in1=cfar[:, :].unsqueeze(2).broadcast_to([128, 8, 128]), op=ALU.subtract)
        for jt in range(2):
            E("act", "activation", out=EbB[:, 1 - jt, :, :], in_=stgB[:, jt, :, :], func=AF.Exp)

        ws_qb = load_w([(0, 512, 1280)])
        ws_kb = load_w([(0, 512, 1792)])
        for tg in range(4):
            for c in range(4):
                fm(ws, c * 128, qaT[:, c, tsl(tg)], gq[:, 0:1], True, tg)
        ws_k = load_w([(0, 64, 512), (64, 64, 512), (128, 64, 576), (192, 64, 576),
                       (256, 32, 3584), (288, 32, 3584), (320, 32, 3584), (352, 32, 3584)])
        for c in range(4):
            for tg in range(4):
                fm(ws_qb, c * 128, qbT[:, c, tsl(tg)], gq[:, 2:3], True, tg)
        ws_vb = load_w([(0, 512, 2304)])
        for c in range(4):
            for tg in range(4):
                fm(ws_kb, c * 128, kbT[:, c, tsl(tg)], gq[:, 3:4], True, tg)
        ws_g5 = load_w([(0, 128, 640), (128, 256, 3328), (384, 8, 3616)])
        for c in range(2):
            for tg in range(4):
                fm(ws_k, c * 128, kaT2[:, c, tsl(tg)], gq[:, 1:2], True, tg)
        for tg in range(4):
            fm(ws_k, 256, kiT[:, tsl(tg)], None, False, tg)
        fm_flush()

        def tm_mm(ti):
            accv = pbf[2 + 2 * (ti % 2)]
            accq = pbf[3 + 2 * (ti % 2)]
            tok = slice(ti * 128, (ti + 1) * 128)
            for kc in range(8):
                MM(accv, hT[:, kc, tok], ws_vb[:, kc, 0:512], kc == 0, kc == 7)
            for kc in range(8):
                MM(accq[:, 0:392], hT[:, kc, tok], ws_g5[:, kc, 0:392], kc == 0, kc == 7)

        def tm_post(ti):
            accv = pbf[2 + 2 * (ti % 2)]
            accq = pbf[3 + 2 * (ti % 2)]
            tok = slice(ti * 128, (ti + 1) * 128)
            E("act", "activation", out=vB[:, ti, :, 0:64], in_=accv.rearrange("p (h d) -> p h d", h=8), func=AF.Copy)
            E("act", "activation", out=vA[:, ti, :, 0:64],
              in_=accq[:, 0:128].rearrange("p (h d) -> p h d", h=2), func=AF.Copy)
            E("act", "activation", out=sgn[:, ti, :], in_=accq[:, 384:392], func=AF.Sign)
            E("dve", "scalar_tensor_tensor", out=wab[:, ti, :], in0=accq[:, 384:392], scalar=0.0625,
              in1=sgn[:, ti, :], op0=ALU.mult, op1=ALU.mult)
            qb2 = q2b[ti % 2]
            E("dve", "tensor_tensor", out=qb2[:].rearrange("p (h e) -> p h e", h=8),
              in0=accq[:, 128:384].rearrange("p (h e) -> p h e", h=8),
              in1=wab[:, ti, :].unsqueeze(2).broadcast_to([128, 8, 32]), op=ALU.mult)
            ptr = pbh[ti % 2][:, 0:256].rearrange("p (a b) -> p a b", a=2)
            for g in range(2):
                TR(ptr[:, g, :], qb2[:, g * 128:(g + 1) * 128], identb[:], inc=(g == 1))
            E("act", "activation", out=q2T[:, :, tok], in_=ptr, func=AF.Copy)

        tm_mm(0)
        for ti in range(NT):
            if ti + 1 < NT:
                tm_mm(ti + 1)
            tm_post(ti)

        dump("d_hT", hT[:], [128, 8, S_LEN], BF16)
        if stop == 0:
            return finish()
        for nm, t in (("d_qaT", qaT), ("d_kaT2", kaT2), ("d_vA", vA), ("d_qbT", qbT), ("d_kbT", kbT), ("d_vB", vB),
                      ("d_q2T", q2T), ("d_kiT", kiT), ("d_sgn", sgn)):
            dump(nm, t[:], [int(v) for v in t.shape], t.dtype)
        if stop == 1:
            return finish()
        import os
        _nt2 = int(os.environ.get("DBG_NT", NT))
        A2 = Arena(nc, R_W.base, R_W.size, "p2a")
        A2b = Arena(nc, R_SP.base, R_SP.size, "p2b")
        scoresb = [A2.alloc("scores%d" % k, [128, S_LEN], F32) for k in range(2)]
        masktb = [A2.alloc("maskt%d" % k, [128, S_LEN], BF16) for k in range(2)]
        maskTs = [A2b.alloc("maskT%d" % k, [128, NT, 128], BF16) for k in range(2)]
        exBp = [A2b.alloc("exBp%d" % i, [128, 2, 4, 128], BF16) for i in range(2)]
        exB = [exBp[i // 2][:, i % 2, :, :] for i in range(4)]
        PTBp = [A2b.alloc("PTBp%d" % i, [128, 2, 4, 128], BF16) for i in range(2)]
        PTB = [PTBp[i // 2][:, i % 2, :, :] for i in range(4)]
        ob = A2b.alloc("ob", [128, D], BF16)

        oaccv = [pbf[6][:, 0:260].rearrange("p (h d) -> p h d", h=4) for k in range(2)]
        sacc = pbf[7]
        scb2 = [[pbf[2 + 2 * s_ + k].rearrange("p (a b) -> p a b", a=4) for k in range(2)] for s_ in range(2)]
        scb = scb2[1]
        mtr = pbh[1][:, 0:1024].rearrange("p (a b) -> p a b", a=8)
        otr = pbh[2][:, 0:1024].rearrange("p (a b) -> p a b", a=8)
        ctr = {"g": 0}

        def gen_S1(i):
            qs = slice(i * 128, (i + 1) * 128)
            nk = (i + 1) * 128
            par = i % 2
            scores = scoresb[par]
            maskt = masktb[par]
            junk = maskt
            maskT = maskTs[par]
            if i >= 2:
                Rb = [maskt[:, 0:512], maskt[:, 512:1024]]
                Dh = maskt[:, 1024:2048].rearrange("p (h c) -> p h c", h=8)
                E("dve", "tensor_tensor", out=Dh, in0=identb[:].unsqueeze(1).broadcast_to([128, 8, 128]),
                  in1=sgn[:, i, :].unsqueeze(2).broadcast_to([128, 8, 128]), op=ALU.mult)
                work = [(ch, h) for ch in range((nk + 511) // 512) for h in range(8)]

                def idx_front(ch, h):
                    cw = min(512, nk - ch * 512)
                    csl = slice(ch * 512, ch * 512 + cw)
                    g, r = h // 4, h % 4
                    ip = pbf[h % 2]
                    MM(ip[:, 0:cw], q2T[32 * r:32 * r + 32, g, qs], kiT[32 * r:32 * r + 32, csl], True, True,
                       tile_position=(32 * r, 0))
                    E("act", "activation", out=Rb[h % 2][:, 0:cw], in_=ip[:, 0:cw], func=AF.Relu)

                def idx_back(ch, h):
                    cw = min(512, nk - ch * 512)
                    csl = slice(ch * 512, ch * 512 + cw)
                    MM(sacc[:, 0:cw], Dh[:, h, :], Rb[h % 2][:, 0:cw], h == 0, h == 7)
                    if h == 7:
                        E("dve", "tensor_copy", out=scores[:, csl], in_=sacc[:, 0:cw])

                idx_front(*work[0])
                for n_, wk in enumerate(work):
                    if n_ + 1 < len(work):
                        idx_front(*work[n_ + 1])
                    idx_back(*wk)
                    yield
                idx_done[i] = True
                Rr = Rrs[par]
                mid, cntv, dirv, csum, dirS, Rk = mids[par], cntvs[par], dirvs[par], csums[par], dirSs[par], Rks[par]
                E("dve", "tensor_reduce", out=Rr, in_=scores[:, 0:nk], axis=AX.X, op=ALU.max, apply_absolute_value=True)
                E("dve", "tensor_tensor", out=scores[:, i * 128:nk], in0=scores[:, i * 128:nk], in1=negm[:], op=ALU.add)
                E("dve", "tensor_scalar", out=Rk, in0=pow2[:], scalar1=Rr, scalar2=None, op0=ALU.mult)
                E("dve", "memset", ap=mid, constant=0.0)
                E("act", "activation", out=negRks[par], in_=npow2[:], func=AF.Copy, scale=Rr)
                yield
                for k in range(K0):
                    E("dve", "tensor_scalar", out=junk[:, 0:nk], in0=scores[:, 0:nk], scalar1=mid, scalar2=None,
                      op0=ALU.is_ge, op1=ALU.add, accum_out=cntv)
                    yield
                    E("dve", "tensor_scalar", out=dirv, in0=cntv, scalar1=255.5, scalar2=0.5,
                      op0=ALU.is_ge, op1=ALU.subtract)
                    yield
                    E("dve", "scalar_tensor_tensor", out=mid, in0=dirv, scalar=Rk[:, k:k + 1], in1=mid,
                      op0=ALU.mult, op1=ALU.add)
                    yield
                nmid = nmids[par]
                E("act", "activation", out=nmid, in_=mid, func=AF.Copy, scale=-1.0)
                for k in range(K0, NIT):
                    E("act", "activation", out=junk[:, 0:nk], in_=scores[:, 0:nk], func=AF.Sign, bias=nmid,
                      accum_out=csum)
                    yield
                    E("act", "activation", out=dirS, in_=csum, func=AF.Sign, bias=cthr[:, i:i + 1])
                    yield
                    E("act", "activation", out=nmid, in_=dirS, func=AF.Identity, scale=negRks[par][:, k:k + 1],
                      bias=nmid)
                    yield
                E("dve", "tensor_scalar", out=maskt[:, 0:nk], in0=scores[:, 0:nk], scalar1=nmid, scalar2=0.0,
                  op0=ALU.add, op1=ALU.is_ge)
            elif i == 0:
                E("dve", "tensor_copy", out=maskt[:, 0:128], in_=trilb[:])
            else:
                E("dve", "memset", ap=maskt[:, 0:128], constant=1.0)
                E("dve", "tensor_copy", out=maskt[:, 128:256], in_=trilb[:])
            yield
            for j0 in range(0, i + 1, 8):
                njs = min(i + 1, j0 + 8) - j0
                for jj in range(njs):
                    j = j0 + jj
                    TR(mtr[:, jj, :], maskt[:, j * 128:(j + 1) * 128], identb[:], inc=(jj == njs - 1))
                E("act", "activation", out=maskT[:, j0:j0 + njs, :], in_=mtr[:, 0:njs, :], func=AF.Copy)
                yield

        def n_S1(i):
            nk = (i + 1) * 128
            n = 1 + (i // 8 + 1)
            if i >= 2:
                n += 8 * ((nk + 511) // 512) + 1 + 3 * NIT
            return n

        def gen_S2(i):
            qs = slice(i * 128, (i + 1) * 128)
            maskT = maskTs[i % 2]
            oa = oaccv[0]
            jts = [(0, i)] + ([(1, i - 1)] if i >= 1 else [])
            njt = len(jts)
            nfar = max(0, i - 1)
            groups = [("far", list(range(j0, min(nfar, j0 + 4)))) for j0 in range(0, nfar, 4)]
            groups.append(("near", [j for j in (i - 1, i) if j >= 0]))
            items = [("A", hk, None) for hk in range(2)] + [(c, kind, js) for c in range(4) for (kind, js) in groups]

            def front(item, gi):
                c, kind, js = item
                if c == "A":
                    hk = kind
                    for jn, (jt, j) in enumerate(jts):
                        for hh in range(4):
                            hq = 4 * hk + hh
                            cq, hf = hq // 2, hq % 2
                            MM(scb2[gi][hf][:, jn * 2 + hh // 2, :],
                               kaT2[64 * hf:64 * hf + 64, hk, j * 128:(j + 1) * 128],
                               qaT[64 * hf:64 * hf + 64, cq, qs], True, True,
                               inc=(jn == njt - 1 and hh == 3))
                else:
                    for jj, j in enumerate(js):
                        for hf in range(2):
                            MM(scb2[gi][hf][:, jj, :], kbT[64 * hf:64 * hf + 64, c, j * 128:(j + 1) * 128],
                               qbT[64 * hf:64 * hf + 64, c, qs], True, True,
                               inc=(jj == len(js) - 1 and hf == 1))

            def mid(item, gi, hf):
                c, kind, js = item
                ex = exB[2 * gi + hf]
                pt = PTB[2 * gi + hf]
                if c == "A":
                    hk = kind
                    if hf == 0:
                        E("act", "activation", out=exBp[gi][:, :, 0:2 * njt, :], in_=sc2w[gi][:, :, 0:2 * njt, :],
                          func=AF.Exp)
                    if hf == 0:
                        for jn, (jt, j) in enumerate(jts):
                            E("dve", "tensor_tensor", out=PTBp[gi][:, :, 2 * jn:2 * jn + 2, :],
                              in0=exBp[gi][:, :, 2 * jn:2 * jn + 2, :],
                              in1=EbA[:, jt, 4 * hk:4 * hk + 4, :].rearrange("p (hh f) c -> p f hh c", f=2),
                              op=ALU.mult)
                else:
                    n = len(js)
                    h = 2 * c + hf
                    if hf == 0:
                        E("act", "activation", out=exBp[gi][:, :, 0:n, :], in_=sc2w[gi][:, :, 0:n, :], func=AF.Exp)
                        E("dve", "tensor_tensor", out=PTBp[gi][:, :, 0:n, :], in0=exBp[gi][:, :, 0:n, :],
                          in1=maskT[:, js[0]:js[0] + n, :].unsqueeze(1).broadcast_to([128, 2, n, 128]), op=ALU.mult)
                        if kind == "near":
                            E("dve", "tensor_tensor", out=PTBp[gi][:, :, 0:n, :], in0=PTBp[gi][:, :, 0:n, :],
                              in1=EbB[:, 2 - n:2, 2 * c:2 * c + 2, :].rearrange("p a b c -> p b a c"), op=ALU.mult)

            def back(item, gi):
                c, kind, js = item
                if c == "A":
                    hk = kind
                    for hh in range(4):
                        pt = PTB[2 * gi + hh % 2]
                        for jn, (jt, j) in enumerate(jts):
                            MM(oa[:, hh, :], pt[:, 2 * jn + hh // 2, :], vA[:, j, hk, :], jn == 0, jn == njt - 1,
                               inc=(hh == 3 and jn == njt - 1))
                    E("dve", "tensor_tensor", out=den[:, 0:4], in0=oa[:, :, 64], in1=esink[:, 4 * hk:4 * hk + 4],
                      op=ALU.add)
                    E("dve", "reciprocal", out=den[:, 0:4], in_=den[:, 0:4])
                    E("dve", "tensor_tensor", out=ob[:, hk * 256:(hk + 1) * 256].rearrange("p (h d) -> p h d", h=4),
                      in0=oa[:, :, 0:64], in1=den[:, 0:4].unsqueeze(2).broadcast_to([128, 4, 64]), op=ALU.mult)
                else:
                    n = len(js)
                    k4 = c // 2
                    for hf in range(2):
                        h = 2 * c + hf
                        pt = PTB[2 * gi + hf]
                        for jj, j in enumerate(js):
                            first = (c % 2 == 0 and hf == 0 and j == 0)
                            last = (c % 2 == 1 and hf == 1 and j == i)
                            MM(oa[:, h % 4, :], pt[:, jj, :], vB[:, j, h, :], first, last,
                               inc=(hf == 1 and jj == n - 1))
                    if c % 2 == 1 and kind == "near":
                        E("dve", "reciprocal", out=den[:, 4:8], in_=oa[:, :, 64])
                        E("dve", "tensor_tensor",
                          out=ob[:, 512 + k4 * 256:512 + (k4 + 1) * 256].rearrange("p (h d) -> p h d", h=4),
                          in0=oa[:, :, 0:64], in1=den[:, 4:8].unsqueeze(2).broadcast_to([128, 4, 64]), op=ALU.mult)

            gi0 = ctr["g"] % 2
            ctr["g"] += len(items)
            front(items[0], gi0)
            yield
            for k, item in enumerate(items):
                gi = (gi0 + k) % 2
                for hf in range(2):
                    mid(item, gi, hf)
                    yield
                if k + 1 < len(items):
                    front(items[k + 1], 1 - gi)
                    yield
                back(item, gi)
                yield
            for c in range(8):
                TR(otr[:, c, :], ob[:, c * 128:(c + 1) * 128], identb[:], inc=(c == 7))
            E("act", "activation", out=oT[:, :, qs], in_=otr, func=AF.Copy)
            yield

        def n_S2(i):
            nfar = max(0, i - 1)
            return 2 + 4 * (2 + 4 * ((nfar + 3) // 4 + 1))

        w_pa_r = w_pa.rearrange("(ec p) d -> p ec d", p=128)
        w_pb_r = w_pb.rearrange("(ec p) d -> p ec d", p=128)
        w_out_r = w_out.rearrange("(dc p) e -> p dc e", p=128)
        WgB0 = nc.alloc_sbuf_tensor_at("WgB0", [128, 8, 512], BF16, offset=int(q2T.manual_sbuf_range[0]))
        WpA0 = nc.alloc_sbuf_tensor_at("WpA0", [128, 4, 512], BF16, offset=int(kiT.manual_sbuf_range[0]))
        pf = {"z0": False, "r": False}

        def pf_z0():
            if not pf["z0"]:
                pf["z0"] = True
                S.dma("pool", "d_w0", [(wsl[0][:], w_in_r[:, :, 768:1280])])

        def pf_rest():
            if not pf["r"]:
                pf["r"] = True
                S.dma("pool", "d_w1", [(wsl[1][:], w_in_r[:, :, 2816:3328])])
                S.dma("pool", "d_g0", [
                    (wsl[2][:], w_in_r[:, :, 3624:3624 + 512]),
                    (WgB0[:], w_in_r[:, :, 4648:4648 + 512]),
                    (WpA0[:], w_pa_r[:, :, 0:512]),
                ])

        live = {}
        idx_done = {0: True, 1: True}

        def start(j):
            if j < _nt2 and j not in live:
                live[j] = [gen_S1(j), n_S1(j), 0]

        def adv(j, frac):
            st = live.get(j)
            if st is None:
                return
            pv = live.get(j - 1)
            while pv is not None and pv[0] is not None and not idx_done.get(j - 1, False):
                try:
                    next(pv[0])
                    pv[2] += 1
                except StopIteration:
                    pv[0] = None
            while st[0] is not None and st[2] < frac * st[1]:
                try:
                    next(st[0])
                    st[2] += 1
                except StopIteration:
                    st[0] = None
            if frac >= 1.0:
                while st[0] is not None:
                    try:
                        next(st[0])
                    except StopIteration:
                        st[0] = None

        start(0)
        adv(0, 1.0)
        for i in range(_nt2):
            start(i + 1)
            start(i + 2)
            if i == NT - 2:
                pf_z0()
            if i == NT - 1:
                pf_rest()
            g2 = gen_S2(i)
            n2 = n_S2(i)
            s = 0
            for _ in g2:
                s += 1
                p = min(1.0, s / n2)
                adv(i + 1, min(1.0, 0.5 + 0.5 * p / 0.95))
                adv(i + 2, 0.5 * p)
            adv(i + 1, 1.0)
            live.pop(i + 1, None)

        dump("d_oT", oT[:], [128, 8, S_LEN], BF16)
        if stop == 2:
            return finish()
        pf_z0()
        pf_rest()
        A3 = Arena(nc, R_QKV.base, R_QKV.size, "p3")
        A3b = Arena(nc, R_SP.base, R_SP.size, "p3b")
        mT = A3.alloc("mT", [128, 8, S_LEN], BF16)
        gws = [{"gA": wsl[2], "gB": WgB0, "pA": WpA0, "pB": None},
               {"gA": A3.alloc("WgA1", [128, 8, 512], BF16), "gB": A3.alloc("WgB1", [128, 8, 512], BF16),
                "pA": A3.alloc("WpA1", [128, 4, 512], BF16), "pB": A3.alloc("WpB1", [128, 4, 512], BF16)}]
        gws[0]["pB"] = A3.alloc("WpB0", [128, 4, 512], BF16)
        xts2 = [A3.alloc("xo%d" % i, [128, D], F32) for i in range(2)]
        tmpz = [A3b.alloc("tmpz%d" % i, [128, 512], BF16) for i in range(2)]
        sA = [A3b.alloc("sA%d" % i, [128, 512], F32) for i in range(2)]
        sB = [A3b.alloc("sB%d" % i, [128, 512], F32) for i in range(2)]
        tA = [A3b.alloc("tA%d" % i, [128, 512], F32) for i in range(2)]
        S.dma("pool", "d_g0b", [(gws[0]["pB"][:], w_pb_r[:, :, 0:512])])
        S.dma("pool", "d_g1", [
            (gws[1]["pA"][:], w_pa_r[:, :, 512:1024]), (gws[1]["pB"][:], w_pb_r[:, :, 512:1024]),
            (gws[1]["gA"][:], w_in_r[:, :, 3624 + 512:3624 + 1024]),
            (gws[1]["gB"][:], w_in_r[:, :, 4648 + 512:4648 + 1024]),
        ])

        zn = 0
        for zi, (col0, cbase) in enumerate(((768, 0), (2816, 4))):
            wz = wsl[zi]
            for cc in range(4):
                for tg in range(4):
                    pz = pbf[zn % 2]
                    tz = tmpz[zn % 2]
                    zn += 1
                    for kc in range(8):
                        MM(pz, wz[:, kc, cc * 128:(cc + 1) * 128], hT[:, kc, tsl(tg)], kc == 0, kc == 7)
                    E("act", "activation", out=tz[:], in_=pz, func=AF.Silu)
                    E("dve", "tensor_tensor", out=oT[:, cbase + cc, tsl(tg)], in0=oT[:, cbase + cc, tsl(tg)],
                      in1=tz[:], op=ALU.mult)
        wo = [wsl[0], wsl[1]]
        for hf in range(2):
            S.dma("pool", "d_w%d" % hf, [(wsl[hf][:], w_out_r[:, :, hf * 512:(hf + 1) * 512])])

        gn = 0
        for dg in range(2):
            gw = gws[dg]
            for dcl in range(4):
                dc = dg * 4 + dcl
                cs = slice(dcl * 128, (dcl + 1) * 128)
                for tg in range(4):
                    k = gn % 2
                    gn += 1
                    bPA, bPB, bgA, bgB = (2, 3, 4, 5) if k == 0 else (0, 1, 6, 7)
                    for kc in range(8):
                        MM(pbf[bgA], gw["gA"][:, kc, cs], hT[:, kc, tsl(tg)], kc == 0, kc == 7)
                    for kc in range(8):
                        MM(pbf[bgB], gw["gB"][:, kc, cs], hT[:, kc, tsl(tg)], kc == 0, kc == 7)
                    for ec in range(4):
                        MM(pbf[bPA], gw["pA"][:, ec, cs], oT[:, ec, tsl(tg)], ec == 0, ec == 3)
                    for ec in range(4):
                        MM(pbf[bPB], gw["pB"][:, ec, cs], oT[:, 4 + ec, tsl(tg)], ec == 0, ec == 3)
                    E("act", "activation", out=sA[k][:], in_=pbf[bgA], func=AF.Sigmoid)
                    E("act", "activation", out=sB[k][:], in_=pbf[bgB], func=AF.Sigmoid)
                    E("dve", "tensor_tensor", out=tA[k][:], in0=sA[k][:], in1=pbf[bPA], op=ALU.mult)
                    E("dve", "tensor_tensor", out=sB[k][:], in0=sB[k][:], in1=pbf[bPB], op=ALU.mult)
                    E("dve", "tensor_tensor", out=mT[:, dc, tsl(tg)], in0=tA[k][:], in1=sB[k][:], op=ALU.add)

        okeys = []
        for ti in range(NT):
            tok = slice(ti * 128, (ti + 1) * 128)
            xo = xts2[ti % 2]
            S.dma("sp", "d_xo%d" % (ti % 2), [(xo[:], x[tok, :])])
            for hf in range(2):
                po = pbf[2 * (ti % 2) + hf]
                for dc in range(8):
                    MM(po, mT[:, dc, tok], wo[hf][:, dc, :], dc == 0, dc == 7)
                E("dve", "tensor_tensor", out=xo[:, hf * 512:(hf + 1) * 512], in0=po,
                  in1=xo[:, hf * 512:(hf + 1) * 512], op=ALU.add)
                key = "d_o%d" % (ti % 2)
                if key not in okeys:
                    okeys.append(key)
                S.dma("sp", key, [(out[tok, hf * 512:(hf + 1) * 512], xo[:, hf * 512:(hf + 1) * 512])])
        dkeys.extend(okeys)
        return finish()


def _t5_bucket_np(n):
    n = np.maximum(n, 0)
    nf = np.maximum(n, 1).astype(np.float32)
    large = 16 + (np.log(nf / np.float32(16)) / np.float32(math.log(128 / 16)) * np.float32(16)).astype(np.int32)
    large = np.minimum(large, 31)
    return np.where(n < 16, n, large)


def _host_consts(norm_g, qnorm_a, knorm_a, sinks_a, qnorm_b, knorm_b, rel_bias):
    f = np.float32
    s = np.arange(128)[:, None]
    t = np.arange(128)[None, :]
    d0 = t - s
    d1 = t + 128 - s
    b0 = _t5_bucket_np(d0)
    b1 = _t5_bucket_np(d1)
    ta = rel_bias[:, :8]
    tb = rel_bias[:, 8:]

    def gath(tab):
        a = np.stack([tab[b0], tab[b1]], axis=1)
        return np.ascontiguousarray(a.transpose(0, 1, 3, 2)).reshape(128, -1).astype(f)

    mA = np.stack([(s <= t), (s > t)], axis=1).astype(f).reshape(128, -1)
    tt = np.arange(128)[:, None]
    sx = np.arange(128)[None, :]
    tril = (sx <= tt).astype(f)
    bd = np.zeros((128, 128), f)
    bd[:64, :64] = 1
    bd[64:, 64:] = 1
    return {
        "c_gT": np.ascontiguousarray(norm_g.reshape(8, 128).T).astype(f),
        "c_gq": np.ascontiguousarray(np.stack([np.tile(qnorm_a, 2), np.tile(knorm_a, 2),
                                               np.tile(qnorm_b, 2), np.tile(knorm_b, 2)], axis=1)).astype(f),
        "c_sink": np.ascontiguousarray(np.broadcast_to(sinks_a[None, :], (128, 8))).astype(f),
        "c_cfar": np.ascontiguousarray(np.broadcast_to(tb[31][None, :], (128, 8))).astype(f),
        "c_biasA": gath(ta),
        "c_biasB": gath(tb),
        "c_mA": mA,
        "c_negm": np.where(sx <= tt, 0.0, -BIG).astype(f),
        "c_tril": tril,
        "c_ident": np.eye(128, dtype=f),
        "c_bd": bd,
        "c_pow2": np.ascontiguousarray(np.broadcast_to((0.5 ** np.arange(NIT))[None, :], (128, NIT))).astype(f),
        "c_npow2": np.ascontiguousarray(np.broadcast_to((-0.5 * 0.5 ** np.arange(NIT))[None, :], (128, NIT))).astype(f),
        "c_cthr": np.ascontiguousarray(np.broadcast_to(((np.arange(NT) + 1) * 128 - 511.5)[None, :], (128, NT))).astype(f),
    }


_CACHE = {}


def kernel(x, norm_g, w_in, qnorm_a, knorm_a, sinks_a, qnorm_b, knorm_b, rel_bias, w_proj_a, w_proj_b, w_out):
    a = lambda v: np.ascontiguousarray(np.asarray(v, dtype=np.float32))
    x = a(x)
    consts = _host_consts(a(norm_g)[0], a(qnorm_a)[0], a(knorm_a)[0], a(sinks_a)[0], a(qnorm_b)[0],
                          a(knorm_b)[0], a(rel_bias))
    shared = {"w_in": a(w_in)[0], "w_pa": a(w_proj_a)[0], "w_pb": a(w_proj_b)[0], "w_out": a(w_out)[0]}
    shared.update(consts)
    if "nc" not in _CACHE:
        _CACHE["nc"] = build_program()
    nc = _CACHE["nc"]
    in_maps = [dict(shared, x=x[b]) for b in range(8)]
    res = run_bass_kernel_spmd(nc, in_maps, core_ids=list(range(8)))
    return np.stack([r["out"] for r in res.results], axis=0).astype(np.float32)
```
